# Optimizing a Trainium2 kernel written in Bass

```python
import math
import jax, jax.numpy as jnp
from jax import lax
import numpy as np

D_MODEL = 1024
BATCH = 8
SEQ = 2048
DEPTH = 1
DEC_BATCH = 128
DEC_SEQ = 1
PAST_LEN = 16384
PAGE_SIZE = 128

N_META = 16
H_A = 8
N_A = 64
W_A = H_A * N_A
D_DECAY = 32
D_AAA = 32
D_GATE = 96
N_A_IN = 3 * W_A + D_DECAY + D_AAA + D_GATE
SPLIT_A = (W_A, 2 * W_A, 3 * W_A, 3 * W_A + D_DECAY, 3 * W_A + D_DECAY + D_AAA)
LNX_EPS = 64e-5
H_B = 4
DK_B = 128
DV_B = 128
W_B = H_B * DV_B
CONV_W = 4
GDN_CHUNK = 64
N_B_IN = 4 * W_B + 2 * H_B
N_IN = N_A_IN + N_B_IN
W_MIX = W_A + W_B
D_FF = 2816
NORM_EPS = 1e-6

kernel_name = "hymba_rwkv7_gdn_macaron_step"


def rms_norm(x, gain):
    xf = x.astype(jnp.float32)
    return xf * lax.rsqrt(jnp.mean(xf * xf, axis=-1, keepdims=True) + NORM_EPS) * gain


def l2norm(x):
    return x * lax.rsqrt(jnp.sum(x * x, axis=-1, keepdims=True) + 1e-12)


def swiglu(x, w_gate, w_up, w_down):
    return (jax.nn.silu(x @ w_gate) * (x @ w_up)) @ w_down


def rwkv7_mixer(pa, shift_prev, s0, mu, w0, w_decay_up, a0, w_a_up, w_g_up, k_k, k_a, r_k, lnx_w, lnx_b):
    B, L, _ = pa.shape
    prev = jnp.concatenate([shift_prev[:, None, :].astype(jnp.float32), pa[:, :-1]], axis=1)
    xm = pa + (prev - pa) * mu
    r, k, v, wd, ad, gd = jnp.split(xm, SPLIT_A, axis=-1)
    w = -jax.nn.softplus(-(w0 + jnp.tanh(wd) @ w_decay_up)) - 0.5
    decay = jnp.exp(-jnp.exp(w))
    a = jax.nn.sigmoid(a0 + ad @ w_a_up)
    g = jax.nn.sigmoid(gd) @ w_g_up
    hd = lambda t: t.reshape(B, L, H_A, N_A)
    kk = l2norm(hd(k * k_k))
    k = k * (1.0 + (a - 1.0) * k_a)
    r_h, k_h, v_h, a_h, w_h = hd(r), hd(k), hd(v), hd(a), hd(decay)
    tm = lambda t: jnp.moveaxis(t, 1, 0)

    def step(S, inp):
        r_t, w_t, k_t, v_t, kk_t, kka_t = inp
        S = (S * w_t[:, :, None, :]
             - jnp.einsum('bhvk,bhk->bhv', S, kk_t)[..., None] * kka_t[:, :, None, :]
             + v_t[..., None] * k_t[:, :, None, :])
        return S, jnp.einsum('bhvk,bhk->bhv', S, r_t)

    s_final, o = lax.scan(step, s0.astype(jnp.float32),
                          (tm(r_h), tm(w_h), tm(k_h), tm(v_h), tm(kk), tm(kk * a_h)))
    o = jnp.moveaxis(o, 0, 1)
    mean = jnp.mean(o, axis=-1, keepdims=True)
    var = jnp.mean(jnp.square(o - mean), axis=-1, keepdims=True)
    o = ((o - mean) * lax.rsqrt(var + LNX_EPS)).reshape(B, L, W_A) * lnx_w + lnx_b
    bonus = jnp.sum(r_h * k_h * r_k, axis=-1, keepdims=True) * v_h
    out = (o + bonus.reshape(B, L, W_A)) * g
    return out, pa[:, -1], s_final


def gdn_chunked(q, k, v, logdecay, beta, s0, chunk):
    B, L, H, _ = q.shape
    n = L // chunk

    def blk(t):
        t = t.reshape((B, n, chunk) + t.shape[2:])
        return jnp.moveaxis(jnp.moveaxis(t, 1, 0), 2, 3)

    qc, kc, vc, gc, bc = blk(q), blk(k), blk(v), blk(logdecay), blk(beta)
    G = jnp.cumsum(gc, axis=-1)
    idx = jnp.arange(chunk)
    causal = idx[:, None] >= idx[None, :]
    strict = idx[:, None] > idx[None, :]
    dmat = jnp.exp(jnp.where(causal, G[..., :, None] - G[..., None, :], -jnp.inf))
    kb = kc * bc[..., None]
    lower = jnp.where(strict, jnp.einsum('nbhid,nbhjd->nbhij', kb, kc) * dmat, 0.0)
    amat = lower + jnp.eye(chunk, dtype=lower.dtype)
    solve = lambda rhs: lax.linalg.triangular_solve(amat, rhs, left_side=True, lower=True,
                                                    unit_diagonal=True)
    u = solve(vc * bc[..., None])
    wk = solve(kb * jnp.exp(G)[..., None])
    qk = jnp.einsum('nbhid,nbhjd->nbhij', qc, kc) * dmat

    def step(S, inp):
        q_i, k_i, u_i, w_i, G_i, qk_i = inp
        v_new = u_i - jnp.einsum('bhcd,bhde->bhce', w_i, S)
        o = (jnp.einsum('bhcd,bhde->bhce', q_i * jnp.exp(G_i)[..., None], S)
             + jnp.einsum('bhij,bhje->bhie', qk_i, v_new))
        g_last = G_i[..., -1:]
        S = (S * jnp.exp(g_last)[..., None]
             + jnp.einsum('bhcd,bhce->bhde', k_i * jnp.exp(g_last - G_i)[..., None], v_new))
        return S, o

    s_final, o = lax.scan(step, s0, (qc, kc, u, wk, G, qk))
    o = jnp.moveaxis(jnp.moveaxis(o, 2, 3), 0, 1).reshape(B, L, H, -1)
    return o, s_final


def gdn_mixer(pb, conv_prev, s0, segments, conv_w, a_log, dt_bias, norm_w):
    B, L, _ = pb.shape
    qkv_raw, z, a_raw, b_raw = jnp.split(pb, (3 * W_B, 4 * W_B, 4 * W_B + H_B), axis=-1)
    full = jnp.concatenate([conv_prev.astype(jnp.float32), qkv_raw], axis=1)
    conv = full[:, 0:L] * conv_w[0]
    for j in range(1, CONV_W):
        conv = conv + full[:, j:j + L] * conv_w[j]
    q, k, v = jnp.split(jax.nn.silu(conv), 3, axis=-1)
    q = l2norm(q.reshape(B, L, H_B, DK_B)) * (DK_B ** -0.5)
    k = l2norm(k.reshape(B, L, H_B, DK_B))
    v = v.reshape(B, L, H_B, DV_B)
    beta = jax.nn.sigmoid(b_raw)
    logdecay = -jnp.exp(a_log) * jax.nn.softplus(a_raw + dt_bias)
    S = s0.astype(jnp.float32)
    outs = []
    start = 0
    for length, chunk in segments:
        seg = slice(start, start + length)
        o_seg, S = gdn_chunked(q[:, seg], k[:, seg], v[:, seg], logdecay[:, seg], beta[:, seg], S, chunk)
        outs.append(o_seg)
        start += length
    o = jnp.concatenate(outs, axis=1)
    o = o * lax.rsqrt(jnp.mean(o * o, axis=-1, keepdims=True) + NORM_EPS) * norm_w
    o = o * jax.nn.silu(z.reshape(B, L, H_B, DV_B))
    return o.reshape(B, L, W_B), full[:, -(CONV_W - 1):], S


def trunk_layer(h, shift_prev, conv_prev, s_rwkv, s_gdn, segments, p):
    h = h + 0.5 * swiglu(rms_norm(h, p['g_ffn1']), p['w_gate1'], p['w_up1'], p['w_down1'])
    proj = rms_norm(h, p['g_mix']) @ p['w_in']
    pa, pb = proj[..., :N_A_IN], proj[..., N_A_IN:]
    o_a, shift_new, rwkv_new = rwkv7_mixer(pa, shift_prev, s_rwkv, p['mu_shift'], p['w0'], p['w_decay_up'],
                                           p['a0'], p['w_a_up'], p['w_g_up'], p['k_k'], p['k_a'],
                                           p['r_k'], p['lnx_w'], p['lnx_b'])
    o_b, conv_new, gdn_new = gdn_mixer(pb, conv_prev, s_gdn, segments, p['conv_w'], p['a_log'],
                                       p['dt_bias'], p['gdn_norm_w'])
    h = h + jnp.concatenate([o_a, o_b], axis=-1) @ p['w_out']
    h = h + 0.5 * swiglu(rms_norm(h, p['g_ffn2']), p['w_gate2'], p['w_up2'], p['w_down2'])
    return h, rwkv_new, shift_new, gdn_new, conv_new


def setup_inputs(seed: int = 0) -> dict:
    key = jax.random.key(seed)
    ks = iter(jax.random.split(key, 48))
    f32 = jnp.float32
    nrm = lambda shape, s: jax.random.normal(next(ks), shape, f32) * s
    uni = lambda shape, lo, hi: jax.random.uniform(next(ks), shape, f32, lo, hi)
    D = D_MODEL
    dt = jnp.exp(uni((DEPTH, H_B), math.log(1e-3), math.log(1e-1)))
    return {
        'x_prompt': nrm((BATCH, SEQ, D), 1.0),
        'x_sample': nrm((DEC_BATCH, DEC_SEQ, D), 1.0),
        'state_rwkv': nrm((DEPTH, DEC_BATCH, H_A, N_A, N_A), 0.3),
        'state_shift': nrm((DEPTH, DEC_BATCH, N_A_IN), 1.0),
        'state_gdn': nrm((DEPTH, DEC_BATCH, H_B, DK_B, DV_B), 0.3),
        'state_conv': nrm((DEPTH, DEC_BATCH, CONV_W - 1, 3 * W_B), 1.0),
        'meta_tokens': nrm((N_META, D), 1.0),
        'g_ffn1': 1.0 + nrm((DEPTH, D), 0.05),
        'w_gate1': nrm((DEPTH, D, D_FF), D ** -0.5),
        'w_up1': nrm((DEPTH, D, D_FF), D ** -0.5),
        'w_down1': nrm((DEPTH, D_FF, D), D_FF ** -0.5),
        'g_mix': 1.0 + nrm((DEPTH, D), 0.05),
        'w_in': nrm((DEPTH, D, N_IN), D ** -0.5),
        'mu_shift': uni((DEPTH, N_A_IN), 0.0, 1.0),
        'w0': uni((DEPTH, W_A), -6.0, -1.0),
        'w_decay_up': nrm((DEPTH, D_DECAY, W_A), 0.1 * D_DECAY ** -0.5),
        'a0': nrm((DEPTH, W_A), 0.1),
        'w_a_up': nrm((DEPTH, D_AAA, W_A), 0.1 * D_AAA ** -0.5),
        'w_g_up': nrm((DEPTH, D_GATE, W_A), D_GATE ** -0.5),
        'k_k': 0.85 + nrm((DEPTH, W_A), 0.05),
        'k_a': 1.0 + nrm((DEPTH, W_A), 0.05),
        'r_k': nrm((DEPTH, H_A, N_A), 0.1),
        'lnx_w': 1.0 + nrm((DEPTH, W_A), 0.05),
        'lnx_b': nrm((DEPTH, W_A), 0.01),
        'conv_w': nrm((DEPTH, CONV_W, 3 * W_B), CONV_W ** -0.5),
        'a_log': jnp.log(uni((DEPTH, H_B), 1.0, 16.0)),
        'dt_bias': dt + jnp.log(-jnp.expm1(-dt)),
        'gdn_norm_w': 1.0 + nrm((DEPTH, DV_B), 0.05),
        'w_out': nrm((DEPTH, W_MIX, D), W_MIX ** -0.5),
        'g_ffn2': 1.0 + nrm((DEPTH, D), 0.05),
        'w_gate2': nrm((DEPTH, D, D_FF), D ** -0.5),
        'w_up2': nrm((DEPTH, D, D_FF), D ** -0.5),
        'w_down2': nrm((DEPTH, D_FF, D), D_FF ** -0.5),
        'g_final': 1.0 + nrm((D,), 0.05),
    }


def reference(x_prompt, x_sample, state_rwkv, state_shift, state_gdn, state_conv, meta_tokens,
              g_ffn1, w_gate1, w_up1, w_down1, g_mix, w_in, mu_shift, w0, w_decay_up, a0, w_a_up,
              w_g_up, k_k, k_a, r_k, lnx_w, lnx_b, conv_w, a_log, dt_bias, gdn_norm_w, w_out,
              g_ffn2, w_gate2, w_up2, w_down2, g_final):
    f32 = jnp.float32
    bp, sp = x_prompt.shape[0], x_prompt.shape[1]
    ss = x_sample.shape[1]
    hp = jnp.concatenate([jnp.broadcast_to(meta_tokens.astype(f32)[None], (bp, N_META, D_MODEL)),
                          x_prompt.astype(f32)], axis=1)
    hs = x_sample.astype(f32)
    prompt_segments = ((N_META, N_META), (sp, GDN_CHUNK))
    sample_segments = ((ss, math.gcd(ss, GDN_CHUNK)),)
    zero_shift = jnp.zeros((bp, N_A_IN), f32)
    zero_conv = jnp.zeros((bp, CONV_W - 1, 3 * W_B), f32)
    zero_rwkv = jnp.zeros((bp, H_A, N_A, N_A), f32)
    zero_gdn = jnp.zeros((bp, H_B, DK_B, DV_B), f32)
    p_rwkv, p_shift, p_gdn, p_conv = [], [], [], []
    s_rwkv, s_shift, s_gdn, s_conv = [], [], [], []
    for l in range(DEPTH):
        lp = dict(g_ffn1=g_ffn1[l], w_gate1=w_gate1[l], w_up1=w_up1[l], w_down1=w_down1[l],
                  g_mix=g_mix[l], w_in=w_in[l], mu_shift=mu_shift[l], w0=w0[l],
                  w_decay_up=w_decay_up[l], a0=a0[l], w_a_up=w_a_up[l], w_g_up=w_g_up[l],
                  k_k=k_k[l], k_a=k_a[l], r_k=r_k[l], lnx_w=lnx_w[l], lnx_b=lnx_b[l],
                  conv_w=conv_w[l], a_log=a_log[l], dt_bias=dt_bias[l], gdn_norm_w=gdn_norm_w[l],
                  w_out=w_out[l], g_ffn2=g_ffn2[l], w_gate2=w_gate2[l], w_up2=w_up2[l],
                  w_down2=w_down2[l])
        hp, r1, sh1, g1, c1 = trunk_layer(hp, zero_shift, zero_conv, zero_rwkv, zero_gdn,
                                          prompt_segments, lp)
        hs, r2, sh2, g2, c2 = trunk_layer(hs, state_shift[l], state_conv[l], state_rwkv[l],
                                          state_gdn[l], sample_segments, lp)
        p_rwkv.append(r1); p_shift.append(sh1); p_gdn.append(g1); p_conv.append(c1)
        s_rwkv.append(r2); s_shift.append(sh2); s_gdn.append(g2); s_conv.append(c2)
    y_prompt = rms_norm(hp, g_final)[:, N_META:].astype(x_prompt.dtype)
    y_sample = rms_norm(hs, g_final).astype(x_sample.dtype)
    return (y_prompt, y_sample,
            jnp.stack(p_rwkv), jnp.stack(p_shift), jnp.stack(p_gdn), jnp.stack(p_conv),
            jnp.stack(s_rwkv), jnp.stack(s_shift), jnp.stack(s_gdn), jnp.stack(s_conv))
```

```python
import contextlib
import numpy as np
import concourse.bass as bass
import concourse.mybir as mybir
from concourse.bass_utils import run_bass_kernel_spmd

F32 = mybir.dt.float32
BF16 = mybir.dt.bfloat16
R32 = mybir.dt.float32r


def R(ap):
    return ap.bitcast(R32)
AF = mybir.ActivationFunctionType
ALU = mybir.AluOpType
AX = mybir.AxisListType

ENGS = ("pe", "act", "dve", "pool", "sp")
NCORE = 8
TTOT = 2080
TM = 544
NSAMP = 16
EXPM05 = 0.6065306597126334
NEG = -1.0e30
STRICT = False
USE_R32 = True


class Buf:
    __slots__ = ("name", "w", "r", "dsem", "dcount", "excl")

    def __init__(self, name):
        self.name = name
        self.excl = False
        self.w = None
        self.r = []
        self.dsem = None
        self.dcount = 0


class Prog:
    def __init__(self, nc, stack):
        self.nc = nc
        self.stack = stack
        self.ops = {e: [] for e in ENGS}
        self.count = {e: 0 for e in ENGS}
        self.seen = {e: {} for e in ENGS}
        self.sems = {}
        for e in ENGS:
            self.sems[e] = stack.enter_context(nc.semaphore("s_" + e))
        self.nbuf = 0
        self.final_tokens = []
        self.final_eng = "sp"

    def sbuf(self, name, shape, dt=F32):
        return self.stack.enter_context(self.nc.sbuf_tensor("sb_" + name, list(shape), dt))

    def psum(self, name, shape, dt=F32):
        return self.stack.enter_context(self.nc.psum_tensor("pp_" + name, list(shape), dt))

    def buf(self, name=None):
        self.nbuf += 1
        return Buf(name or "b%d" % self.nbuf)

    def _dsem(self, b):
        if b.dsem is None:
            self.nbuf += 1
            key = "d%d" % self.nbuf
            self.sems[key] = self.stack.enter_context(self.nc.semaphore(key))
            b.dsem = key
        return b.dsem

    def _waits(self, eng, reads, writes):
        need = {}

        def add(tok, raw):
            if tok is None:
                return
            k, v = tok
            if k == eng:
                if eng == "pe":
                    return
                if not STRICT and (not raw or v < self.count[eng] - 1):
                    return
            if v > need.get(k, 0):
                need[k] = v

        for b in reads:
            add(b.w, True)
            if b.excl:
                for t in b.r:
                    if t[0] != eng:
                        add(t, False)
        for b in writes:
            add(b.w, False)
            for t in b.r:
                add(t, False)
        out = []
        seen = self.seen[eng]
        for k, v in need.items():
            if seen.get(k, 0) >= v:
                continue
            seen[k] = v
            out.append((k, v))
        return out

    def _record(self, tok, reads, writes):
        for b in reads:
            if len(b.r) > 24:
                best = {}
                for k, v in b.r:
                    if v > best.get(k, 0):
                        best[k] = v
                b.r = list(best.items())
            b.r.append(tok)
        for b in writes:
            b.w = tok
            b.r = []

    def op(self, eng, fn, reads=(), writes=()):
        waits = self._waits(eng, reads, writes)
        self.count[eng] += 1
        tok = (eng, self.count[eng])
        self.ops[eng].append((waits, fn, eng, 1))
        self._record(tok, reads, writes)
        return tok

    def dma(self, eng, fn, reads=(), writes=(), sem_buf=None):
        waits = self._waits(eng, reads, writes)
        sb = sem_buf or (writes[0] if writes else reads[0])
        key = self._dsem(sb)
        sb.dcount += 16
        tok = (key, sb.dcount)
        self.ops[eng].append((waits, fn, key, 16))
        self._record(tok, reads, writes)
        return tok

    def emit(self):
        nc = self.nc
        engobj = {"pe": "tensor", "act": "scalar", "dve": "vector", "pool": "gpsimd", "sp": "sync"}
        with nc.Block() as block:
            for e in ENGS:
                ops = self.ops[e]
                fin = self.final_tokens if self.final_eng == e else []
                if not ops and not fin:
                    continue

                def body(eobj, ops=ops, fin=fin):
                    for waits, fn, key, inc in ops:
                        for k, v in waits:
                            eobj.wait_ge(self.sems[k], v)
                        ins = fn(eobj)
                        ins.then_inc(self.sems[key], inc)
                    for k, v in fin:
                        eobj.wait_ge(self.sems[k], v)

                getattr(block, engobj[e])(body)


OC = []
for i in range(12):
    OC.append((i * 128, 128))
OC.append((1536, 32))
OC.append((1568, 32))
OC.append((1600, 96))
OC.append(None)
for i in range(16):
    OC.append((1696 + i * 128, 128))
OC.append((1696 + 2048, 4))
OC.append((1696 + 2052, 4))
NOC = 34

PASSES = []
_ch0 = [(s, 1, s) for s in range(16)] + [(16, 16, 16)] + [(32 + 64 * i, 64, 16) for i in range(8)]
PASSES.append(dict(col0=0, T=544, nsamp=16, chunks=_ch0, subs=[(0, 272), (272, 272)]))
for _p in range(3):
    PASSES.append(dict(col0=544 + 512 * _p, T=512, nsamp=0,
                       chunks=[(64 * i, 64, 16) for i in range(8)], subs=[(0, 512)]))

COLS = {}


def _build_cols_index():
    n = 0
    for nm in ("gf1", "gmx", "gf2", "gfn"):
        for k in range(8):
            COLS["%s%d" % (nm, k)] = n
            n += 1
    for oc in range(15):
        COLS["mu%d" % oc] = n
        n += 1
    for nm in ("w0", "a0", "kk", "ka", "rk"):
        for j in range(4):
            COLS["%s%d" % (nm, j)] = n
            n += 1
    for i in range(12):
        for t in range(4):
            COLS["cw%d_%d" % (i, t)] = n
            n += 1
    COLS["dtb"] = n
    n += 1
    COLS["alog"] = n
    n += 1
    return n


NCOLS = _build_cols_index()


class Builder:
    def __init__(self, npass=4, do_mix=True, do_tail=True, stop=99, cf="all", sub=99):
        self.sub = sub
        self.cf = cf
        self.stop = stop
        self.npass = npass
        self.do_mix = do_mix
        self.do_tail = do_tail

    def din(self, name, shape):
        return self.nc.dram_tensor(name, list(shape), F32, kind="ExternalInput").ap()

    def dout(self, name, shape):
        return self.nc.dram_tensor(name, list(shape), F32, kind="ExternalOutput").ap()

    def mm(self, out, lhsT, rhs, rd, wr, start=True, stop=True, r=False, g=False):
        if r and USE_R32:
            lhsT = R(lhsT)
            rhs = R(rhs)
        if g:
            self.P.op("pe", lambda e: e.matmul(out, lhsT=lhsT, rhs=rhs, start=start, stop=stop, skip_group_check=True), rd, wr)
        else:
            self.P.op("pe", lambda e: e.matmul(out, lhsT=lhsT, rhs=rhs, start=start, stop=stop), rd, wr)

    def tr(self, out, in_, ident, rd, wr):
        self.P.op("pe", lambda e: e.transpose(out=out, in_=in_, identity=ident), rd, wr)

    def act(self, out, in_, func, rd, wr, bias=None, scale=None):
        kw = {}
        if bias is not None:
            kw["bias"] = bias
        if scale is not None:
            kw["scale"] = scale
        self.P.op("act", lambda e: e.activation(out=out, in_=in_, func=func, **kw), rd, wr)

    def tt(self, out, in0, in1, op, rd, wr, eng="dve"):
        self.P.op(eng, lambda e: e.tensor_tensor(out=out, in0=in0, in1=in1, op=op), rd, wr)

    def ts(self, out, in0, s1, op0, rd, wr, s2=None, op1=None, eng="dve"):
        if op1 is None:
            self.P.op(eng, lambda e: e.tensor_scalar(out=out, in0=in0, scalar1=s1, scalar2=None, op0=op0), rd, wr)
        else:
            self.P.op(eng, lambda e: e.tensor_scalar(out=out, in0=in0, scalar1=s1, scalar2=s2, op0=op0, op1=op1), rd, wr)

    def stt(self, out, in0, scalar, in1, op0, op1, rd, wr):
        self.P.op("dve", lambda e: e.scalar_tensor_tensor(out=out, in0=in0, scalar=scalar, in1=in1, op0=op0, op1=op1), rd, wr)

    def cp(self, out, in_, rd, wr, eng="dve"):
        if eng == "act":
            self.P.op("act", lambda e: e.activation(out=out, in_=in_, func=AF.Copy), rd, wr)
        elif eng == "dve":
            self.P.op("dve", lambda e: e.tensor_scalar(out=out, in0=in_, scalar1=1.0, scalar2=None, op0=ALU.mult), rd, wr)
        else:
            self.P.op(eng, lambda e: e.tensor_copy(out=out, in_=in_), rd, wr)

    def red(self, out, in_, rd, wr):
        self.P.op("dve", lambda e: e.tensor_reduce(out=out, in_=in_, axis=AX.X, op=ALU.add), rd, wr)

    def recip(self, out, in_, rd, wr):
        self.P.op("dve", lambda e: e.reciprocal(out=out, in_=in_), rd, wr)

    def scan(self, out, d0, d1, rd, wr):
        self.P.op("dve", lambda e: e.tensor_tensor_scan(out=out, data0=d0, data1=d1, initial=0.0, op0=ALU.mult, op1=ALU.add), rd, wr)

    def memset(self, ap, val, wr, eng="dve"):
        self.P.op(eng, lambda e: e.memset(ap, val), (), wr)

    def dma(self, out, in_, rd, wr, eng="sp", sem_buf=None):
        return self.P.dma(eng, lambda e: e.dma_start(out=out, in_=in_), rd, wr, sem_buf=sem_buf)

    def col(self, name, m=128):
        i = COLS[name]
        return self.cols[0:m, i:i + 1]

    def get_w(self, expect):
        i = self.wpos
        assert self.wlist[i][0] == expect, (self.wlist[i][0], expect)
        self.wpos += 1
        NB = len(self.WT)
        while self.wnext < len(self.wlist) and self.wnext <= i + NB - 1:
            k = self.wnext
            t = self.WT[k % NB]
            src = self.wlist[k][1]
            self.dma(t[:, 0:src.shape[1]], src, [], [self.bWT[k % NB]], eng="pool")
            self.wnext += 1
        return self.WT[i % NB], self.bWT[i % NB]

    def build(self):
        nc = bass.Bass("TRN2", target_bir_lowering=False)
        self.nc = nc
        d = {}
        d["xT"] = self.din("xT", [8, 128, TTOT])
        for nm in ("wgu1", "wgu2"):
            d[nm] = self.din(nm, [22, 128, 2048])
        for nm in ("wd1", "wd2"):
            d[nm] = self.din(nm, [22, 128, 1024])
        d["win"] = self.din("win", [17, 128, 2048])
        d["wout"] = self.din("wout", [8, 128, 1024])
        d["cols"] = self.din("cols", [128, NCOLS])
        d["bc"] = self.din("bc", [64, 1536])
        d["wdu"] = self.din("wdu", [32, 512])
        d["wau"] = self.din("wau", [32, 512])
        d["wgu"] = self.din("wgu", [96, 512])
        d["srw"] = self.din("srw", [16, 128, 256])
        d["sgd"] = self.din("sgd", [16, 128, 512])
        d["ssh"] = self.din("ssh", [128, 15 * 16])
        d["scv"] = self.din("scv", [128, 12 * 48])
        d["yT"] = self.dout("yT", [8, 128, TTOT])
        d["orw"] = self.dout("orw", [17, 128, 256])
        d["ogd"] = self.dout("ogd", [17, 128, 512])
        d["osh"] = self.dout("osh", [128, 15 * 17])
        d["ocv"] = self.dout("ocv", [128, 12 * 51])
        self.d = d

        with contextlib.ExitStack() as st:
            P = Prog(nc, st)
            self.P = P
            self.alloc()
            self.make_wlist()
            self.setup_consts()
            self.out_tokens = []
            for p in range(self.npass):
                self.run_pass(p)
            self.finish_outputs()
            P.final_tokens = [t for t in self.out_tokens if t is not None]
            P.emit()
        return nc

    def alloc(self):
        P = self.P
        self.h = P.sbuf("h", [128, 8, TM])
        self.bh = [P.buf("h%d" % k) for k in range(8)]
        self.xn = P.sbuf("xn", [128, 8, TM], BF16)
        self.bxn = [P.buf("xn%d" % k) for k in range(8)]
        self.mixT = P.sbuf("mixT", [128, 8, TM], BF16)
        self.bmx = [P.buf("mx%d" % k) for k in range(8)]
        self.PJ = P.sbuf("PJ", [128, 20, TM])
        self.bpj = [P.buf("pj%d" % k) for k in range(20)]
        self.PJb = self.PJ[:, :, :].rearrange("p a t -> p (a t)").bitcast(BF16)
        NSC = 8
        self.SC = P.sbuf("SC", [128, NSC, TM])
        self.bsc = [P.buf("sc%d" % k) for k in range(NSC)]
        NSR = 25
        self.SR = P.sbuf("SR", [128, NSR, 512])
        self.bsr = [P.buf("sr%d" % k) for k in range(NSR)]
        self.SM = P.sbuf("SM", [128, 3, TM])
        self.bsm = [P.buf("sm%d" % k) for k in range(3)]
        self.RAW = [P.sbuf("RAW%d" % k, [128, 3 + TM]) for k in range(2)]
        self.bRAW = [P.buf("raw%d" % k) for k in range(2)]
        self.RS = [P.sbuf("RS%d" % k, [128, 16]) for k in range(2)]
        self.bRS = [P.buf("rs%d" % k) for k in range(2)]
        self.WT = [P.sbuf("WT%d" % k, [128, 2048], BF16) for k in range(3)]
        self.bWT = [P.buf("wt%d" % k) for k in range(3)]
        self.FM = P.sbuf("FM", [128, 4, 256])
        self.bfm = [P.buf("fm%d" % k) for k in range(4)]
        self.MR = P.sbuf("MR", [128, TM])
        self.bMR = P.buf("MR")
        self.cols = P.sbuf("cols", [128, NCOLS])
        self.bcols = P.buf("cols")
        self.c2 = P.sbuf("c2", [128, 8])
        self.bc2 = P.buf("c2")
        self.bcst = P.sbuf("bcst", [64, 1536])
        self.bbc = P.buf("bc")
        self.wdu = P.sbuf("wdu", [32, 512])
        self.wau = P.sbuf("wau", [32, 512])
        self.wgu = P.sbuf("wgu", [96, 512])
        self.bwsm = P.buf("wsm")
        self.ident = P.sbuf("ident", [128, 128])
        self.ones = P.sbuf("ones", [128, 128])
        self.onesb = P.sbuf("onesb", [128, 128], BF16)
        self.blk = P.sbuf("blk", [128, 128])
        self.blk2 = P.sbuf("blk2", [128, 2])
        self.sel = P.sbuf("sel", [4, 4, 128])
        self.msu = P.sbuf("msu", [64, 64])
        self.miu = P.sbuf("miu", [64, 64])
        self.msl = P.sbuf("msl", [64, 64])
        self.nsu = P.sbuf("nsu", [64, 64])
        self.niu = P.sbuf("niu", [64, 64])
        self.nsl = P.sbuf("nsl", [64, 64])
        self.bconst = P.buf("const")
        self.CR = P.sbuf("CR", [128, 27, 3])
        self.bCR = P.buf("CR")
        self.SSH = P.sbuf("SSH", [128, 15, 16])
        self.SCV = P.sbuf("SCV", [128, 12, 3, 16])
        self.bsst = P.buf("sst")
        self.OSH = P.sbuf("OSH", [128, 15, 17])
        self.bOSH = P.buf("OSH")
        self.OCV = P.sbuf("OCV", [128, 12, 3, 17])
        self.bOCV = P.buf("OCV")
        self.Bp = P.sbuf("Bp", [128, 4, 64])
        self.bBp = P.buf("Bp")
        self.Sp = P.sbuf("Sp", [128, 4, 128])
        self.bSp = P.buf("Sp")
        self.Bs = [P.sbuf("Bs%d" % k, [128, 4, 64]) for k in range(2)]
        self.bBs = [P.buf("Bs%d" % k) for k in range(2)]
        self.Ss = [P.sbuf("Ss%d" % k, [128, 4, 128]) for k in range(2)]
        self.bSs = [P.buf("Ss%d" % k) for k in range(2)]
        self.PCs = P.sbuf("PCs", [128, 4, 32])
        self.bPCs = P.buf("PCs")
        self.sm8 = P.sbuf("sm8", [64, 8, 8])
        self.bsm8 = [P.buf("sm8_%d" % k) for k in range(8)]
        self.ps = [P.psum("ps%d" % k, [128, 512]) for k in range(8)]
        self.bps = [P.buf("ps%d" % k) for k in range(8)]
        for b_ in self.bps:
            b_.excl = True

    def sc(self, i):
        return self.SC[:, i, :], self.bsc[i]

    def sr(self, i):
        return self.SR[:, i, :], self.bsr[i]

    def sr3(self, i, H):
        ap, b = self.sr(i)
        return ap[:, 0:H * 64].rearrange("p (h c) -> p h c", h=H), b

    def make_wlist(self):
        d = self.d
        wl = []
        for p in range(self.npass):
            for c in range(22):
                wl.append(("wgu1", d["wgu1"][c]))
            for k in range(22):
                wl.append(("wd1", d["wd1"][k]))
            for g in range(17):
                wl.append(("win", d["win"][g]))
            if self.do_tail:
                for k in range(8):
                    wl.append(("wout", d["wout"][k]))
                for c in range(22):
                    wl.append(("wgu2", d["wgu2"][c]))
                for k in range(22):
                    wl.append(("wd2", d["wd2"][k]))
        self.wlist = wl
        self.wpos = 0
        self.wnext = 0

    def setup_consts(self):
        d = self.d
        bc_ = [self.bconst]
        self.dma(self.cols[:, :], d["cols"], [], [self.bcols])
        self.dma(self.bcst[:, :], d["bc"], [], [self.bbc])
        self.dma(self.wdu[:, :], d["wdu"], [], [self.bwsm])
        self.dma(self.wau[:, :], d["wau"], [], [self.bwsm])
        self.dma(self.wgu[:, :], d["wgu"], [], [self.bwsm])
        self.dma(self.SSH[:, :, :].rearrange("p a s -> p (a s)"), d["ssh"], [], [self.bsst])
        self.dma(self.SCV[:, :, :, :].rearrange("p a t s -> p (a t s)"), d["scv"], [], [self.bsst])

        def pool(fn):
            self.P.op("pool", fn, bc_, bc_)

        pool(lambda e: e.memset(self.ones[:, :], 1.0))
        pool(lambda e: e.memset(self.onesb[:, :], 1.0))
        pool(lambda e: e.memset(self.ident[:, :], 1.0))
        pool(lambda e: e.affine_select(out=self.ident[:, :], in_=self.ident[:, :], pattern=[[-1, 128]],
                                       compare_op=ALU.is_equal, fill=0.0, base=0, channel_multiplier=1))
        pool(lambda e: e.memset(self.blk[:, :], 0.0))
        pool(lambda e: e.memset(self.blk[0:64, 0:64], 1.0))
        pool(lambda e: e.memset(self.blk[64:128, 64:128], 1.0))
        pool(lambda e: e.memset(self.blk2[:, :], 0.0))
        pool(lambda e: e.memset(self.blk2[0:64, 0:1], 1.0))
        pool(lambda e: e.memset(self.blk2[64:128, 1:2], 1.0))
        pool(lambda e: e.memset(self.sel[:, :, :], 1.0))
        pool(lambda e: e.affine_select(out=self.sel[:, :, :], in_=self.sel[:, :, :], pattern=[[-1, 4], [0, 128]],
                                       compare_op=ALU.is_equal, fill=0.0, base=0, channel_multiplier=1))
        for t_, cmp_, pat, cm, fill, base0 in (
            (self.msu, ALU.is_gt, 1, -1, 0.0, 1.0),
            (self.miu, ALU.is_ge, 1, -1, 0.0, 1.0),
            (self.msl, ALU.is_gt, -1, 1, 0.0, 1.0),
            (self.nsu, ALU.is_gt, 1, -1, NEG, 0.0),
            (self.niu, ALU.is_ge, 1, -1, NEG, 0.0),
            (self.nsl, ALU.is_gt, -1, 1, NEG, 0.0),
        ):
            pool(lambda e, t_=t_, base0=base0: e.memset(t_[:, :], base0))
            pool(lambda e, t_=t_, cmp_=cmp_, pat=pat, cm=cm, fill=fill: e.affine_select(
                out=t_[:, :], in_=t_[:, :], pattern=[[pat, 64]], compare_op=cmp_, fill=fill,
                base=0, channel_multiplier=cm))
        pool(lambda e: e.memset(self.CR[:, :, :], 0.0))
        pool(lambda e: e.memset(self.OSH[:, :, :], 0.0))
        pool(lambda e: e.memset(self.OCV[:, :, :, :], 0.0))
        self.P.op("pool", lambda e: e.memset(self.Bp[:, :, :], 0.0), (), [self.bBp])
        self.P.op("pool", lambda e: e.memset(self.Sp[:, :, :], 0.0), (), [self.bSp])
        for j in range(4):
            self.ts(self.c2[:, j:j + 1], self.col("ka%d" % j), -1.0, ALU.mult, [self.bcols], [self.bc2], s2=1.0, op1=ALU.add)
        self.act(self.c2[0:4, 4:5], self.col("alog", 4), AF.Exp, [self.bcols], [self.bc2])
        self.ts(self.c2[0:4, 5:6], self.c2[0:4, 4:5], -1.0, ALU.mult, [self.bc2], [self.bc2])

    def run_pass(self, p):
        ps_ = PASSES[p]
        self.cur = ps_
        self.pidx = p
        T = ps_["T"]
        col0 = ps_["col0"]
        d = self.d
        self.P.op("pool", lambda e: e.memset(self.MR[:, :], 1.0), (), [self.bMR])
        if ps_["nsamp"]:
            self.P.op("pool", lambda e: e.memset(self.MR[:, 0:17], 0.0), (), [self.bMR])
            self.P.op("pool", lambda e: e.memset(self.MR[:, 32:544:64], 0.0), (), [self.bMR])
        else:
            self.P.op("pool", lambda e: e.memset(self.MR[:, 0:512:64], 0.0), (), [self.bMR])
        src = d["xT"].rearrange("k p t -> p k t")[:, :, col0:col0 + T]
        self.dma(self.h[:, :, 0:T], src, [], list(self.bh))
        self.norm("gf1")
        if self.stop <= 1:
            return self.dbg_dump()
        self.ffn("wgu1", "wd1")
        if self.stop <= 2:
            return self.dbg_dump()
        self.norm("gmx")
        self.project_rwkv()
        if self.stop <= 3:
            return self.dbg_dump()
        if self.do_mix:
            self.rwkv_prep()
            if self.stop <= 4:
                return self.dbg_dump()
            self.rwkv_chunks()
            if self.stop <= 5:
                return self.dbg_dump()
        self.project_gdn()
        if self.stop <= 6:
            return self.dbg_dump()
        if self.do_mix:
            self.gdn_prep()
            if self.stop <= 7:
                return self.dbg_dump()
            self.gdn_chunks()
            if self.stop <= 8:
                return self.dbg_dump()
        if self.do_tail:
            self.down(lambda c: (self.mixT[:, c, :], self.bmx[c]), 8, "wout", 1.0)
            self.norm("gf2")
            self.ffn("wgu2", "wd2")
            self.final_norm_store()

    def dbg_dump(self):
        T = self.cur["T"]
        col0 = self.cur["col0"]
        for kc in range(8):
            t = self.dma(self.d["yT"][kc, :, col0:col0 + T], self.h[:, kc, 0:T], [self.bh[kc]], [])
            self.out_tokens.append(t)

    def rstd_tile(self):
        T = self.cur["T"]
        subs = self.cur["subs"]
        for si, (s0, n) in enumerate(subs):
            bank, bb = self.ps[6 + si], self.bps[6 + si]
            for kc in range(8):
                ti = 18 + (kc % 2)
                tmp, tb = self.PJb[:, ti * 2 * TM:ti * 2 * TM + TM], self.bpj[ti]
                self.act(tmp[:, 0:n], self.h[:, kc, s0:s0 + n], AF.Square, [self.bh[kc]], [tb])
                self.mm(bank[:, 0:n], self.onesb[:, :], tmp[:, 0:n], [tb, self.bconst], [bb], start=(kc == 0), stop=(kc == 7))
            r1, b1 = self.PJ[:, 17, :], self.bpj[17]
            self.act(r1[:, s0:s0 + n], bank[:, 0:n], AF.Ln, [bb, self.bc3], [b1], bias=self.epsn[:, 0:1], scale=1.0 / 1024.0)
        r2, b2 = self.PJ[:, 16, :], self.bpj[16]
        self.act(r2[:, 0:T], r1[:, 0:T], AF.Exp, [b1], [b2], scale=-0.5)
        return r2, b2

    def norm(self, gname):
        T = self.cur["T"]
        if not hasattr(self, "epsn"):
            self.epsn = self.P.sbuf("epsn", [128, 4])
            self.bc3 = self.P.buf("epsn")
            self.memset(self.epsn[:, 0:1], 1e-6, [self.bc3])
            self.memset(self.epsn[:, 1:2], 1e-12, [self.bc3])
            self.memset(self.epsn[:, 2:3], 64e-5, [self.bc3])
            self.memset(self.epsn[:, 3:4], 1.0, [self.bc3])
        r2, b2 = self.rstd_tile()
        for kc in range(8):
            self.stt(self.xn[:, kc, 0:T], self.h[:, kc, 0:T], self.col("%s%d" % (gname, kc)), r2[:, 0:T],
                     ALU.mult, ALU.mult, [self.bh[kc], b2, self.bcols], [self.bxn[kc]])

    def final_norm_store(self):
        T = self.cur["T"]
        col0 = self.cur["col0"]
        r2, b2 = self.rstd_tile()
        for kc in range(8):
            o, ob = self.PJ[:, kc, :], self.bpj[kc]
            self.stt(o[:, 0:T], self.h[:, kc, 0:T], self.col("gfn%d" % kc), r2[:, 0:T],
                     ALU.mult, ALU.mult, [self.bh[kc], b2, self.bcols], [ob])
            t = self.dma(self.d["yT"][kc, :, col0:col0 + T], o[:, 0:T], [ob], [])
            self.out_tokens.append(t)

    def ffn(self, wgu, wd):
        T = self.cur["T"]
        subs = self.cur["subs"]
        cnt = 0
        for c in range(22):
            wt, wb = self.get_w(wgu)
            v = wt[:, :].rearrange("p (k c) -> p k c", k=8)
            a_ap = self.PJb[:, c * TM:c * TM + T]
            a_buf = self.bpj[c // 2]
            for si, (s0, n) in enumerate(subs):
                st_ = cnt % 2
                cnt += 1
                gb, gbb = self.ps[2 * st_], self.bps[2 * st_]
                ub, ubb = self.ps[2 * st_ + 1], self.bps[2 * st_ + 1]
                for kc in range(8):
                    self.mm(gb[:, 0:n], v[:, kc, 0:128], self.xn[:, kc, s0:s0 + n],
                            [wb, self.bxn[kc]], [gbb], start=(kc == 0), stop=(kc == 7))
                for kc in range(8):
                    self.mm(ub[:, 0:n], v[:, kc, 128:256], self.xn[:, kc, s0:s0 + n],
                            [wb, self.bxn[kc]], [ubb], start=(kc == 0), stop=(kc == 7))
                sg, sgb = self.PJ[:, 16 + st_, :], self.bpj[16 + st_]
                self.act(sg[:, 0:n], gb[:, 0:n], AF.Silu, [gbb], [sgb])
                self.tt(a_ap[:, s0:s0 + n], sg[:, 0:n], ub[:, 0:n], ALU.mult, [sgb, ubb], [a_buf])
        self.down(lambda c: (self.PJb[:, c * TM:c * TM + TM], self.bpj[c // 2]), 22, wd, 0.5)

    def down(self, src, n_c, wname, scale):
        subs = self.cur["subs"]
        ns = len(subs)
        for half in range(2):
            for g in range(n_c // 2):
                wt, wb = self.get_w(wname)
                v = wt[:, 0:1024].rearrange("p (c n) -> p c n", c=2)
                for cc in range(2):
                    c = 2 * g + cc
                    s_ap, s_buf = src(c)
                    for jj in range(4):
                        for si, (s0, n) in enumerate(subs):
                            b = jj * ns + si + (4 * half if ns == 1 else 0)
                            self.mm(self.ps[b][:, 0:n], v[:, cc, jj * 128:(jj + 1) * 128], s_ap[:, s0:s0 + n],
                                    [wb, s_buf], [self.bps[b]], start=(c == 0), stop=(c == n_c - 1))
            for jj in range(4):
                j = half * 4 + jj
                for si, (s0, n) in enumerate(subs):
                    b = jj * ns + si + (4 * half if ns == 1 else 0)
                    self.stt(self.h[:, j, s0:s0 + n], self.ps[b][:, 0:n], scale, self.h[:, j, s0:s0 + n],
                             ALU.mult, ALU.add, [self.bps[b], self.bh[j]], [self.bh[j]])

    def project(self, g0, g1, post):
        subs = self.cur["subs"]
        cnt = 0
        for g in range(g0, g1):
            wt, wb = self.get_w("win")
            v = wt[:, :].rearrange("p (k c) -> p k c", k=8)
            for cc in range(2):
                oc = 2 * g + cc
                if OC[oc] is None:
                    continue
                M = OC[oc][1]
                banks = []
                for si, (s0, n) in enumerate(subs):
                    b = 4 + (cnt % 4)
                    cnt += 1
                    for kc in range(8):
                        self.mm(self.ps[b][0:M, 0:n], v[:, kc, cc * 128:cc * 128 + M], self.xn[:, kc, s0:s0 + n],
                                [wb, self.bxn[kc]], [self.bps[b]], start=(kc == 0), stop=(kc == 7))
                    banks.append(b)
                post(oc, M, banks)

    def evac_raw(self, oc, M, banks):
        subs = self.cur["subs"]
        ns_ = self.cur["nsamp"]
        k = oc % 2
        raw, rb = self.RAW[k], self.bRAW[k]
        rs, rsb = self.RS[k], self.bRS[k]
        for (s0, n), b in zip(subs, banks):
            a = max(s0, ns_)
            if a < s0 + n:
                self.cp(raw[0:M, 3 + a - ns_:3 + s0 + n - ns_], self.ps[b][0:M, a - s0:n], [self.bps[b]], [rb], eng="act")
            if s0 < ns_:
                self.cp(rs[0:M, s0:ns_], self.ps[b][0:M, 0:ns_ - s0], [self.bps[b]], [rsb], eng="act")
        return raw, rb, rs, rsb

    def project_rwkv(self):
        T = self.cur["T"]
        ns_ = self.cur["nsamp"]
        Tp = T - ns_

        def post(oc, M, banks):
            raw, rb, rs, rsb = self.evac_raw(oc, M, banks)
            if oc < 12:
                dst, db = self.PJ[:, oc, :], self.bpj[oc]
            else:
                dst, db = self.SM[:, oc - 12, :], self.bsm[oc - 12]
            mu = self.col("mu%d" % oc, M)
            self.cp(raw[0:M, 0:3], self.CR[0:M, oc, :], [self.bCR], [rb], eng="pool")
            tmp, tb = self.PJ[:, 12 + (oc % 2), :], self.bpj[12 + (oc % 2)]
            self.tt(tmp[0:M, 0:Tp], raw[0:M, 2:2 + Tp], raw[0:M, 3:3 + Tp], ALU.subtract, [rb], [tb])
            self.stt(dst[0:M, ns_:T], tmp[0:M, 0:Tp], mu, raw[0:M, 3:3 + Tp], ALU.mult, ALU.add, [tb, rb, self.bcols], [db])
            if ns_:
                t2, t2b = self.PJ[:, 14 + (oc % 2), :], self.bpj[14 + (oc % 2)]
                self.tt(t2[0:M, 0:16], self.SSH[0:M, oc, :], rs[0:M, :], ALU.subtract, [self.bsst, rsb], [t2b], eng="pool")
                self.stt(dst[0:M, 0:16], t2[0:M, 0:16], mu, rs[0:M, :], ALU.mult, ALU.add, [t2b, rsb, self.bcols], [db])
                self.cp(self.OSH[0:M, oc, 0:16], rs[0:M, :], [rsb], [self.bOSH], eng="pool")
            self.cp(self.CR[0:M, oc, :], raw[0:M, Tp:Tp + 3], [rb], [self.bCR], eng="pool")

        self.project(0, 8, post)

    def project_gdn(self):
        T = self.cur["T"]
        ns_ = self.cur["nsamp"]
        Tp = T - ns_
        subs = self.cur["subs"]

        def post(oc, M, banks):
            if oc >= 32:
                dst, db = self.SM[:, oc - 32, :], self.bsm[oc - 32]
                for (s0, n), b in zip(subs, banks):
                    self.cp(dst[0:4, s0:s0 + n], self.ps[b][0:4, 0:n], [self.bps[b]], [db], eng="act")
                return
            if oc >= 28:
                dst, db = self.PJ[:, 12 + (oc - 28), :], self.bpj[12 + (oc - 28)]
                for (s0, n), b in zip(subs, banks):
                    self.act(dst[:, s0:s0 + n], self.ps[b][:, 0:n], AF.Silu, [self.bps[b]], [db])
                return
            i = oc - 16
            raw, rb, rs, rsb = self.evac_raw(oc, M, banks)
            dst, db = self.PJ[:, i, :], self.bpj[i]
            self.cp(raw[:, 0:3], self.CR[:, 15 + i, :], [self.bCR], [rb], eng="pool")
            acc, ab = self.PJ[:, 16 + (i % 2), :], self.bpj[16 + (i % 2)]
            self.ts(acc[:, 0:Tp], raw[:, 0:Tp], self.col("cw%d_0" % i), ALU.mult, [rb, self.bcols], [ab])
            for t in range(1, 4):
                self.stt(acc[:, 0:Tp], raw[:, t:t + Tp], self.col("cw%d_%d" % (i, t)), acc[:, 0:Tp], ALU.mult, ALU.add,
                         [rb, ab, self.bcols], [ab])
            self.act(dst[:, ns_:T], acc[:, 0:Tp], AF.Silu, [ab], [db])
            if ns_:
                a2, a2b = self.PJ[:, 18 + (i % 2), :], self.bpj[18 + (i % 2)]
                self.ts(a2[:, 0:16], self.SCV[:, i, 0, :], self.col("cw%d_0" % i), ALU.mult, [self.bsst, self.bcols], [a2b])
                for t in range(1, 3):
                    self.stt(a2[:, 0:16], self.SCV[:, i, t, :], self.col("cw%d_%d" % (i, t)), a2[:, 0:16], ALU.mult, ALU.add,
                             [self.bsst, a2b, self.bcols], [a2b])
                self.stt(a2[:, 0:16], rs[:, :], self.col("cw%d_3" % i), a2[:, 0:16], ALU.mult, ALU.add,
                         [rsb, a2b, self.bcols], [a2b])
                self.act(dst[:, 0:16], a2[:, 0:16], AF.Silu, [a2b], [db])
                self.cp(self.OCV[:, i, 0:2, 0:16], self.SCV[:, i, 1:3, :], [self.bsst], [self.bOCV], eng="pool")
                self.cp(self.OCV[:, i, 2, 0:16], rs[:, :], [rsb], [self.bOCV], eng="pool")
            self.cp(self.CR[:, 15 + i, :], raw[:, Tp:Tp + 3], [rb], [self.bCR], eng="pool")

        self.project(8, 17, post)

    def rwkv_prep(self):
        T = self.cur["T"]
        subs = self.cur["subs"]
        XWD, bwd = self.SM[:, 0, :], self.bsm[0]
        XAD, bad = self.SM[:, 1, :], self.bsm[1]
        XGD, bgd = self.SM[:, 2, :], self.bsm[2]
        self.act(XWD[0:32, 0:T], XWD[0:32, 0:T], AF.Tanh, [bwd], [bwd])
        self.act(XGD[0:96, 0:T], XGD[0:96, 0:T], AF.Sigmoid, [bgd], [bgd])
        for j in range(4):
            KAP, bkap = self.PJ[:, 12 + j, :], self.bpj[12 + j]
            AH, bah = self.PJ[:, 16 + j, :], self.bpj[16 + j]
            jc = slice(j * 128, (j + 1) * 128)
            for si, (s0, n) in enumerate(subs):
                b = 4 + si
                self.mm(self.ps[b][:, 0:n], self.wdu[0:32, jc], XWD[0:32, s0:s0 + n], [self.bwsm, bwd], [self.bps[b]])
                self.act(KAP[:, s0:s0 + n], self.ps[b][:, 0:n], AF.Sigmoid, [self.bps[b], self.bcols], [bkap], bias=self.col("w0%d" % j))
                b2 = 6 + si
                self.mm(self.ps[b2][:, 0:n], self.wau[0:32, jc], XAD[0:32, s0:s0 + n], [self.bwsm, bad], [self.bps[b2]])
                self.act(AH[:, s0:s0 + n], self.ps[b2][:, 0:n], AF.Sigmoid, [self.bps[b2], self.bcols], [bah], bias=self.col("a0%d" % j))
        for j in range(4):
            XR, bxr = self.PJ[:, j, :], self.bpj[j]
            XK, bxk = self.PJ[:, 4 + j, :], self.bpj[4 + j]
            KAP, bkap = self.PJ[:, 12 + j, :], self.bpj[12 + j]
            AH, bah = self.PJ[:, 16 + j, :], self.bpj[16 + j]
            sig, bsig = KAP, bkap
            A, bA = AH, bah
            lnt, blnt = self.sc(0)
            lam, blam = self.sc(1)
            eP, beP = self.sc(2)
            eN, beN = self.sc(3)
            ePx, bePx = self.sc(4)
            kr, bkr = self.sc(6)
            t8, bt8 = self.sc(7)
            self.scan(lam[:, 0:T], self.MR[:, 0:T], sig[:, 0:T], [self.bMR, bsig], [blam])
            self.act(eP[:, 0:T], lam[:, 0:T], AF.Exp, [blam], [beP], scale=-EXPM05)
            self.act(eN[:, 0:T], lam[:, 0:T], AF.Exp, [blam], [beN], scale=EXPM05)
            self.tt(ePx[:, 0:T], lam[:, 0:T], sig[:, 0:T], ALU.subtract, [blam, bsig], [bePx])
            self.act(ePx[:, 0:T], ePx[:, 0:T], AF.Exp, [bePx], [bePx], scale=-EXPM05)
            if self.cur["nsamp"]:
                self.cp(self.PCs[:, j, 0:16], eP[:, 0:16], [beP], [self.bPCs])
                self.cp(self.PCs[:, j, 16:17], eP[:, 31:32], [beP], [self.bPCs])
                self.cp(self.PCs[:, j, 17:25], eP[:, 95:544:64], [beP], [self.bPCs])
            else:
                self.cp(self.PCs[:, j, 0:8], eP[:, 63:512:64], [beP], [self.bPCs])
            self.ts(kr[:, 0:T], XK[:, 0:T], self.col("kk%d" % j), ALU.mult, [bxk, self.bcols], [bkr])
            self.act(t8[:, 0:T], kr[:, 0:T], AF.Square, [bkr], [bt8])
            for si, (s0, n) in enumerate(subs):
                b = 4 + si
                self.mm(self.ps[b][:, 0:n], self.blk[:, :], t8[:, s0:s0 + n], [self.bconst, bt8], [self.bps[b]])
                self.act(lnt[:, s0:s0 + n], self.ps[b][:, 0:n], AF.Ln, [self.bps[b], self.bc3], [blnt], bias=self.epsn[:, 1:2])
            self.act(t8[:, 0:T], lnt[:, 0:T], AF.Exp, [blnt], [bt8], scale=-0.5)
            self.tt(kr[:, 0:T], kr[:, 0:T], t8[:, 0:T], ALU.mult, [bkr, bt8], [bkr])
            self.ts(t8[:, 0:T], A[:, 0:T], self.col("ka%d" % j), ALU.mult, [bA, self.bcols, self.bc2], [bt8],
                    s2=self.c2[:, j:j + 1], op1=ALU.add)
            self.tt(XK[:, 0:T], XK[:, 0:T], t8[:, 0:T], ALU.mult, [bxk, bt8], [bxk])
            self.tt(XK[:, 0:T], XK[:, 0:T], eN[:, 0:T], ALU.mult, [bxk, beN], [bxk])
            self.tt(AH[:, 0:T], kr[:, 0:T], A[:, 0:T], ALU.mult, [bkr, bA], [bah])
            self.tt(AH[:, 0:T], AH[:, 0:T], eN[:, 0:T], ALU.mult, [bah, beN], [bah])
            self.tt(KAP[:, 0:T], kr[:, 0:T], ePx[:, 0:T], ALU.mult, [bkr, bePx], [bkap])
            self.tt(XR[:, 0:T], XR[:, 0:T], eP[:, 0:T], ALU.mult, [bxr, beP], [bxr])

    def inverse(self, U, bU, L, bL, C, H, scr, banks):
        (Ub, bUb), (Lb, bLb), (IL, bIL), (Xa, bXa), (Xb, bXb) = scr
        pL, pU, pX = banks
        idb = self.ident[0:C, 0:C].unsqueeze(1).broadcast_to([C, H, C])
        self.stt(R(Xa[0:C, :, 0:C]), U[0:C, :, 0:C], -1.0, idb, ALU.mult, ALU.add, [self.bconst, bU], [bXa])
        nlev = {64: 5, 32: 4, 16: 3, 8: 2, 4: 1, 2: 0}[C]
        Uc, bUc, Lc, bLc = U, bU, L, bL
        Un, bUn, Ln, bLn = Ub, bUb, Lb, bLb
        X, bX, Xn, bXn = Xa, bXa, Xb, bXb
        for lev in range(nlev):
            last = lev == nlev - 1
            PL = self.ps[pL][0:C, 0:H * 64].rearrange("p (h c) -> p h c", h=H)
            PU = self.ps[pU][0:C, 0:H * 64].rearrange("p (h c) -> p h c", h=H)
            PX = self.ps[pX][0:C, 0:H * 64].rearrange("p (h c) -> p h c", h=H)
            for hh in range(H):
                self.mm(PL[:, hh, 0:C], Uc[0:C, hh, 0:C], Lc[0:C, hh, 0:C], [bUc, bLc], [self.bps[pL]], r=True)
            if not last:
                for hh in range(H):
                    self.mm(PU[:, hh, 0:C], Lc[0:C, hh, 0:C], Uc[0:C, hh, 0:C], [bUc, bLc], [self.bps[pU]], r=True)
            self.tt(R(IL[0:C, :, 0:C]), PL[:, :, 0:C], idb, ALU.add, [self.bps[pL], self.bconst], [bIL])
            if not last:
                self.cp(R(Ln[0:C, :, 0:C]), PL[:, :, 0:C], [self.bps[pL]], [bLn], eng="act")
                self.cp(R(Un[0:C, :, 0:C]), PU[:, :, 0:C], [self.bps[pU]], [bUn], eng="act")
            for hh in range(H):
                self.mm(PX[:, hh, 0:C], IL[0:C, hh, 0:C], X[0:C, hh, 0:C], [bIL, bX], [self.bps[pX]], r=True)
            self.cp(R(Xn[0:C, :, 0:C]), PX[:, :, 0:C], [self.bps[pX]], [bXn], eng="dve")
            X, bX, Xn, bXn = Xn, bXn, X, bX
            if not last:
                Uc, bUc, Un, bUn = Un, bUn, Uc, bUc
                Lc, bLc, Ln, bLn = Ln, bLn, Lc, bLc
        return X, bX

    def sc3(self, i, H):
        ap, b = self.sc(i)
        return ap[:, 0:H * 64].rearrange("p (h c) -> p h c", h=H), b

    def rwkv_chunks(self):
        d = self.d
        H = 8
        chunks = self.cur["chunks"]
        bR = [self.bpj[j] for j in range(4)]
        bK = [self.bpj[4 + j] for j in range(4)]
        bV = [self.bpj[8 + j] for j in range(4)]
        bKA = [self.bpj[12 + j] for j in range(4)]
        bAH = [self.bpj[16 + j] for j in range(4)]
        SG, bsg = self.SM[:, 2, :], self.bsm[2]
        lnw = self.bcst[:, 0:512]
        lnb = self.bcst[:, 512:1024]
        ctx = [dict(), dict()]

        def genA(ci):
            c0, C, sidx = chunks[ci]
            par = ci % 2
            cx = ctx[par]
            cx.clear()
            rr = C > 1

            def P3(b_, C_):
                return self.ps[b_][0:C_, 0:512].rearrange("p (h c) -> p h c", h=8)

            hm = self.blk2[:, :].unsqueeze(1).unsqueeze(3).broadcast_to([128, 4, 2, C])
            bdv = []
            for t, si_, bsrc in ((0, 15 if par == 0 else 20, bR), (1, 16, bK), (3, 17 if par == 0 else 21, bKA), (4, 18, bAH)):
                ap, bb = self.sr(si_)
                v = ap[:, 0:8 * C].rearrange("p (j e c) -> p j e c", j=4, e=2)
                srcv = self.PJ[:, t * 4:t * 4 + 4, c0:c0 + C].unsqueeze(2).broadcast_to([128, 4, 2, C])
                self.tt(R(v), srcv, hm, ALU.mult, list(bsrc) + [self.bconst], [bb])
                bdv.append((v, bb))
            (Rbd, bRbd), (Kbd, bKbd), (KAbd, bKAbd), (Abd, bAbd) = bdv
            yield
            U3, bU = self.sr3(5, 8)
            L3, bL = self.sr3(6, 8)
            Kk3, bKk = self.sr3(7 if par == 0 else 22, 8)
            Ma3, bMa = self.sr3(8 if par == 0 else 23, 8)
            Mk3, bMk = self.sr3(9 if par == 0 else 24, 8)
            for hh in range(H):
                j, e_ = hh // 2, hh % 2
                if C > 1:
                    self.mm(P3(0, C)[:, hh, 0:C], Abd[:, j, e_, :], KAbd[:, j, e_, :], [bAbd, bKAbd], [self.bps[0]], r=rr)
                    self.mm(P3(1, C)[:, hh, 0:C], KAbd[:, j, e_, :], Abd[:, j, e_, :], [bKAbd, bAbd], [self.bps[1]], r=rr)
                    self.mm(P3(2, C)[:, hh, 0:C], Kbd[:, j, e_, :], KAbd[:, j, e_, :], [bKbd, bKAbd], [self.bps[2]], r=rr)
                self.mm(P3(3, C)[:, hh, 0:C], Abd[:, j, e_, :], Rbd[:, j, e_, :], [bAbd, bRbd], [self.bps[3]], r=rr)
                self.mm(P3(4, C)[:, hh, 0:C], Kbd[:, j, e_, :], Rbd[:, j, e_, :], [bKbd, bRbd], [self.bps[4]], r=rr)
            yield

            def mask(m):
                return m[0:C, 0:C].unsqueeze(1).broadcast_to([C, 8, C])

            if C > 1:
                self.tt(R(U3[0:C, :, 0:C]), P3(0, C)[:, :, 0:C], mask(self.msu), ALU.mult, [self.bps[0], self.bconst], [bU])
                self.tt(R(L3[0:C, :, 0:C]), P3(1, C)[:, :, 0:C], mask(self.msl), ALU.mult, [self.bps[1], self.bconst], [bL])
                self.tt(R(Kk3[0:C, :, 0:C]), P3(2, C)[:, :, 0:C], mask(self.msu), ALU.mult, [self.bps[2], self.bconst], [bKk])
            self.tt(R(Ma3[0:C, :, 0:C]), P3(3, C)[:, :, 0:C], mask(self.miu), ALU.mult, [self.bps[3], self.bconst], [bMa])
            self.tt(R(Mk3[0:C, :, 0:C]), P3(4, C)[:, :, 0:C], mask(self.miu), ALU.mult, [self.bps[4], self.bconst], [bMk])
            yield
            if C > 1:
                xfin = self.sr3(14 if par == 0 else 19, 8)
                xtmp = self.sr3(13, 8)
                nlev = {64: 5, 32: 4, 16: 3, 8: 2, 4: 1, 2: 0}[C]
                xa, xb = (xtmp, xfin) if nlev % 2 == 1 else (xfin, xtmp)
                scr = [self.sr3(10, 8), self.sr3(11, 8), self.sr3(12, 8), xa, xb]
                res = {}
                yield from self.inverse_gen(U3, bU, L3, bL, C, 8, scr, (0, 1, 2), res)
                cx["X"] = res["X"]
            cx.update(Rbd=(Rbd, bRbd), KAbd=(KAbd, bKAbd), Kk=(Kk3, bKk), Ma=(Ma3, bMa), Mk=(Mk3, bMk))
            yield

        def genB(ci, slot=0):
            c0, C, sidx = chunks[ci]
            b0, b1, b2 = (5, 6, 7) if slot == 0 else (0, 1, 2)
            tA, tK, tV, tN = (0, 1, 2, 4) if slot == 0 else (5, 6, 10, 11)
            so = 0 if slot == 0 else 3
            par = ci % 2
            cx = ctx[par]
            rr = C > 1
            samp = sidx < 16
            Rbd, bRbd = cx["Rbd"]
            KAbd, bKAbd = cx["KAbd"]
            Kk3, bKk = cx["Kk"]
            Ma3, bMa = cx["Ma"]
            Mk3, bMk = cx["Mk"]
            if samp:
                Bt, bB = self.Bs[sidx % 2], self.bBs[sidx % 2]
                self.dma(Bt[:, :, :].rearrange("p j v -> p (j v)"), d["srw"][sidx], [], [bB])
            else:
                Bt, bB = self.Bp, self.bBp
            At, bAt = self.sr(tA)
            Kt, bKt = self.sr(tK)
            Vt, bVt = self.sr(tV)
            if rr:
                Br_t, bBr = self.sr(3)
                Br = Br_t[:, 0:256].rearrange("p (j v) -> p j v", j=4)
                self.cp(R(Br), Bt[:, :, :], [bB], [bBr], eng="dve")
            else:
                Br, bBr = Bt, bB
            trs = ((4, At, bAt, b0, bAH), (1, Kt, bKt, b1, bK), (2, Vt, bVt, b2, bV))
            for (t, dst, db, bank, bsrc) in trs:
                for j in range(4):
                    self.tr(self.ps[bank][0:C, j * 128:(j + 1) * 128], self.PJ[:, t * 4 + j, c0:c0 + C], self.ident[:, :],
                            [bsrc[j], self.bconst], [self.bps[bank]])
            yield
            for (t, dst, db, bank, bsrc) in trs:
                self.cp(R(dst[0:C, 0:512]), self.ps[bank][0:C, 0:512], [self.bps[bank]], [db], eng="act")
            yield
            for hh in range(H):
                j = hh // 2
                hc = slice(hh * 64, (hh + 1) * 64)
                self.mm(self.ps[b0][0:C, hc], KAbd[:, j, hh % 2, :], Br[:, j, :], [bKAbd, bBr], [self.bps[b0]],
                        start=(hh == 0), stop=True, r=rr, g=True)
            if C > 1:
                for hh in range(H):
                    hc = slice(hh * 64, (hh + 1) * 64)
                    self.mm(self.ps[b0][0:C, hc], Kk3[0:C, hh, 0:C], Vt[0:C, hc], [bKk, bVt], [self.bps[b0]],
                            start=False, stop=True, r=rr, g=True)
            yield
            nY, bnY = self.sr(tN)
            if C > 1:
                X3, bX = cx["X"]
                Rs, bRs = self.sr(tN)
                self.cp(R(Rs[0:C, 0:512]), self.ps[b0][0:C, 0:512], [self.bps[b0]], [bRs], eng="act")
                yield
                for hh in range(H):
                    hc = slice(hh * 64, (hh + 1) * 64)
                    self.mm(self.ps[b1][0:C, hc], X3[0:C, hh, 0:C], Rs[0:C, hc], [bX, bRs], [self.bps[b1]], r=rr)
                yield
                self.act(R(nY[0:C, 0:512]), self.ps[b1][0:C, 0:512], AF.Copy, [self.bps[b1]], [bnY], scale=-1.0)
            else:
                self.act(R(nY[0:C, 0:512]), self.ps[b0][0:C, 0:512], AF.Copy, [self.bps[b0]], [bnY], scale=-1.0)
            yield
            def sout(hh):
                j, p0 = hh // 2, (hh % 2) * 64
                return self.ps[b0][p0:p0 + 64, j * 64:(j + 1) * 64]

            for hh in range(H):
                j, p0 = hh // 2, (hh % 2) * 64
                self.mm(sout(hh), self.ident[:, p0:p0 + 64], Bt[:, j, :], [self.bconst, bB], [self.bps[b0]],
                        start=(hh < 2), stop=True, g=True)
            for hh in range(H):
                hc = slice(hh * 64, (hh + 1) * 64)
                self.mm(sout(hh), At[0:C, hc], nY[0:C, hc], [bAt, bnY], [self.bps[b0]], start=False, stop=True, g=True)
            for hh in range(H):
                hc = slice(hh * 64, (hh + 1) * 64)
                self.mm(sout(hh), Kt[0:C, hc], Vt[0:C, hc], [bKt, bVt], [self.bps[b0]], start=False, stop=True, g=True)
            for hh in range(H):
                j = hh // 2
                hc = slice(hh * 64, (hh + 1) * 64)
                self.mm(self.ps[b2][0:C, hc], Rbd[:, j, hh % 2, :], Br[:, j, :], [bRbd, bBr], [self.bps[b2]],
                        start=(hh == 0), stop=True, r=rr, g=True)
            for hh in range(H):
                hc = slice(hh * 64, (hh + 1) * 64)
                self.mm(self.ps[b2][0:C, hc], Ma3[0:C, hh, 0:C], nY[0:C, hc], [bMa, bnY], [self.bps[b2]], start=False, stop=True, r=rr, g=True)
            for hh in range(H):
                hc = slice(hh * 64, (hh + 1) * 64)
                self.mm(self.ps[b2][0:C, hc], Mk3[0:C, hh, 0:C], Vt[0:C, hc], [bMk, bVt], [self.bps[b2]], start=False, stop=True, r=rr, g=True)
            yield
            pcs = self.PCs[:, :, ci:ci + 1].broadcast_to([128, 4, 64])
            self.tt(Bt[:, :, :], self.ps[b0][:, 0:256].rearrange("p (j v) -> p j v", j=4), pcs, ALU.mult,
                    [self.bps[b0], self.bPCs], [bB])
            if samp:
                t = self.dma(d["orw"][sidx], Bt[:, :, :].rearrange("p j v -> p (j v)"), [bB], [])
                self.out_tokens.append(t)
            RKc, bRK = self.FM[:, slot, :], self.bfm[slot]
            for j in range(4):
                self.stt(RKc[:, j * 64:j * 64 + C], self.PJ[:, j, c0:c0 + C], self.col("rk%d" % j), self.PJ[:, 4 + j, c0:c0 + C],
                         ALU.mult, ALU.mult, [bR[j], bK[j], self.bcols], [bRK])
            yield
            for j in range(4):
                self.mm(self.ps[b1][0:C, 2 * j:2 * j + 2], RKc[:, j * 64:j * 64 + C], self.blk2[:, :], [bRK, self.bconst], [self.bps[b1]])
            yield
            rks, brks = self.sm8[:, so + 0, :], self.bsm8[so + 0]
            self.cp(rks[0:C, :], self.ps[b1][0:C, 0:8], [self.bps[b1]], [brks], eng="act")
            O3 = self.ps[b2][0:C, 0:512].rearrange("p (h v) -> p h v", h=8)
            s1, bs1 = self.sm8[:, so + 1, :], self.bsm8[so + 1]
            self.red(s1[0:C, :], O3, [self.bps[b2]], [bs1])
            self.ts(s1[0:C, :], s1[0:C, :], -1.0 / 64.0, ALU.mult, [bs1], [bs1])
            cen, bcen = self.sc(so + 0)
            cen3 = cen[0:C, 0:512].rearrange("p (h v) -> p h v", h=8)
            self.tt(cen3, O3, s1[0:C, :].unsqueeze(2).broadcast_to([C, 8, 64]), ALU.add, [self.bps[b2], bs1], [bcen])
            yield
            self.mm(self.ps[b1][0:C, 0:512], SG[0:96, c0:c0 + C], self.wgu[0:96, :], [bsg, self.bwsm], [self.bps[b1]])
            sq, bsq = self.sc(so + 1)
            self.act(sq[0:C, 0:512], cen[0:C, 0:512], AF.Square, [bcen], [bsq])
            yield
            s2, bs2 = self.sm8[:, so + 2, :], self.bsm8[so + 2]
            self.red(s2[0:C, :], sq[0:C, 0:512].rearrange("p (h v) -> p h v", h=8), [bsq], [bs2])
            self.act(s2[0:C, :], s2[0:C, :], AF.Sqrt, [bs2, self.bc3], [bs2], bias=self.epsn[0:C, 2:3], scale=1.0 / 64.0)
            self.recip(s2[0:C, :], s2[0:C, :], [bs2], [bs2])
            yield
            self.tt(cen3, cen3, s2[0:C, :].unsqueeze(2).broadcast_to([C, 8, 64]), ALU.mult, [bcen, bs2], [bcen])
            self.tt(cen[0:C, 0:512], cen[0:C, 0:512], lnw[0:C, :], ALU.mult, [bcen, self.bbc], [bcen], eng="pool")
            self.tt(cen[0:C, 0:512], cen[0:C, 0:512], lnb[0:C, :], ALU.add, [bcen, self.bbc], [bcen], eng="pool")
            bon, bbon = self.sc(so + 2)
            self.tt(bon[0:C, 0:512].rearrange("p (h v) -> p h v", h=8), Vt[0:C, 0:512].rearrange("p (h v) -> p h v", h=8),
                    rks[0:C, :].unsqueeze(2).broadcast_to([C, 8, 64]), ALU.mult, [bVt, brks], [bbon])
            yield
            self.tt(cen[0:C, 0:512], cen[0:C, 0:512], bon[0:C, 0:512], ALU.add, [bcen, bbon], [bcen], eng="pool")
            self.tt(sq[0:C, 0:512], cen[0:C, 0:512], self.ps[b1][0:C, 0:512], ALU.mult, [bcen, self.bps[b1]], [bsq])
            yield
            for j in range(4):
                self.tr(self.ps[b0][:, 256 + j * 64:256 + j * 64 + C], sq[0:C, j * 128:(j + 1) * 128], self.ident[0:C, 0:C],
                        [bsq, self.bconst], [self.bps[b0]])
            yield
            for j in range(4):
                self.cp(self.mixT[:, j, c0:c0 + C], self.ps[b0][:, 256 + j * 64:256 + j * 64 + C], [self.bps[b0]], [self.bmx[j]], eng="act")
            yield

        self.run_chunks(genA, genB, chunks)

    def gdn_prep(self):
        T = self.cur["T"]
        subs = self.cur["subs"]
        AR, bar = self.SM[:, 0, :], self.bsm[0]
        BR, bbr = self.SM[:, 1, :], self.bsm[1]
        G, bG = self.SM[:, 2, :], self.bsm[2]
        for i in range(8):
            X, bx = self.PJ[:, i, :], self.bpj[i]
            sq, bsq = self.sc(0 + (i % 2))
            nr, bnr = self.sc(2 + (i % 2))
            self.act(sq[:, 0:T], X[:, 0:T], AF.Square, [bx], [bsq])
            for si, (s0, n) in enumerate(subs):
                b = 4 + (2 * i + si) % 4
                self.mm(self.ps[b][:, 0:n], self.ones[:, :], sq[:, s0:s0 + n], [self.bconst, bsq], [self.bps[b]])
                self.act(nr[:, s0:s0 + n], self.ps[b][:, 0:n], AF.Ln, [self.bps[b], self.bc3], [bnr], bias=self.epsn[:, 1:2])
            self.act(sq[:, 0:T], nr[:, 0:T], AF.Exp, [bnr], [bsq], scale=-0.5)
            if i < 4:
                self.stt(X[:, 0:T], X[:, 0:T], 128.0 ** -0.5, sq[:, 0:T], ALU.mult, ALU.mult, [bx, bsq], [bx])
            else:
                self.tt(X[:, 0:T], X[:, 0:T], sq[:, 0:T], ALU.mult, [bx, bsq], [bx])
        self.act(BR[0:4, 0:T], BR[0:4, 0:T], AF.Sigmoid, [bbr], [bbr])
        self.act(AR[0:4, 0:T], AR[0:4, 0:T], AF.Exp, [bar, self.bcols], [bar], bias=self.col("dtb", 4))
        self.act(AR[0:4, 0:T], AR[0:4, 0:T], AF.Ln, [bar, self.bc3], [bar], bias=self.epsn[0:4, 3:4])
        self.ts(AR[0:4, 0:T], AR[0:4, 0:T], self.c2[0:4, 5:6], ALU.mult, [bar, self.bc2], [bar])
        self.scan(G[0:4, 0:T], self.MR[0:4, 0:T], AR[0:4, 0:T], [self.bMR, bar], [bG])

    def interleave(self, gens):
        alive = [g for g in gens if g is not None]
        while alive:
            for g in list(alive):
                try:
                    next(g)
                except StopIteration:
                    alive.remove(g)

    def run_chunks(self, genA, genB, chunks):
        idx = list(range(len(chunks)))
        if self.cf != "all":
            idx = [i for i in idx if {1: "samp", 16: "meta", 64: "big"}[chunks[i][1]] in self.cf]
        samp = [i for i in idx if chunks[i][2] < 16]
        rest = [i for i in idx if chunks[i][2] >= 16]
        for k in range(0, len(samp), 2):
            pair = samp[k:k + 2]
            for i in pair:
                self.interleave([genA(i)])
            self.interleave([genB(i, slot) for slot, i in enumerate(pair)])
        for k in range(len(rest) + 1):
            ga = genA(rest[k]) if k < len(rest) else None
            gb = genB(rest[k - 1], 0) if k >= 1 else None
            self.interleave([gb, ga])

    def pipeline(self, genA, genB, n):
        for i in range(n + 1):
            ga = genA(i) if i < n else None
            gb = genB(i - 1) if i >= 1 else None
            self.interleave([gb, ga])

    def inverse_gen(self, U, bU, L, bL, C, H, scr, banks, out):
        (Ub, bUb), (Lb, bLb), (IL, bIL), (Xa, bXa), (Xb, bXb) = scr
        pL, pU, pX = banks
        idb = self.ident[0:C, 0:C].unsqueeze(1).broadcast_to([C, H, C])
        self.stt(R(Xa[0:C, :, 0:C]), U[0:C, :, 0:C], -1.0, idb, ALU.mult, ALU.add, [self.bconst, bU], [bXa])
        nlev = {64: 5, 32: 4, 16: 3, 8: 2, 4: 1, 2: 0}[C]
        Uc, bUc, Lc, bLc = U, bU, L, bL
        Un, bUn, Ln, bLn = Ub, bUb, Lb, bLb
        X, bX, Xn, bXn = Xa, bXa, Xb, bXb
        for lev in range(nlev):
            last = lev == nlev - 1
            PL = self.ps[pL][0:C, 0:H * 64].rearrange("p (h c) -> p h c", h=H)
            PU = self.ps[pU][0:C, 0:H * 64].rearrange("p (h c) -> p h c", h=H)
            PX = self.ps[pX][0:C, 0:H * 64].rearrange("p (h c) -> p h c", h=H)
            for hh in range(H):
                self.mm(PL[:, hh, 0:C], Uc[0:C, hh, 0:C], Lc[0:C, hh, 0:C], [bUc, bLc], [self.bps[pL]], r=True)
            if not last:
                for hh in range(H):
                    self.mm(PU[:, hh, 0:C], Lc[0:C, hh, 0:C], Uc[0:C, hh, 0:C], [bUc, bLc], [self.bps[pU]], r=True)
            yield
            self.tt(R(IL[0:C, :, 0:C]), PL[:, :, 0:C], idb, ALU.add, [self.bps[pL], self.bconst], [bIL])
            if not last:
                self.cp(R(Ln[0:C, :, 0:C]), PL[:, :, 0:C], [self.bps[pL]], [bLn], eng="act")
                self.cp(R(Un[0:C, :, 0:C]), PU[:, :, 0:C], [self.bps[pU]], [bUn], eng="act")
            yield
            for hh in range(H):
                self.mm(PX[:, hh, 0:C], IL[0:C, hh, 0:C], X[0:C, hh, 0:C], [bIL, bX], [self.bps[pX]], r=True)
            yield
            self.cp(R(Xn[0:C, :, 0:C]), PX[:, :, 0:C], [self.bps[pX]], [bXn], eng="dve")
            yield
            X, bX, Xn, bXn = Xn, bXn, X, bX
            if not last:
                Uc, bUc, Un, bUn = Un, bUn, Uc, bUc
                Lc, bLc, Ln, bLn = Ln, bLn, Lc, bLc
        out["X"] = (X, bX)

    def gdn_chunks(self):
        d = self.d
        H = 4
        chunks = self.cur["chunks"]
        bQ = [self.bpj[h] for h in range(4)]
        bK = [self.bpj[4 + h] for h in range(4)]
        bV = [self.bpj[8 + h] for h in range(4)]
        bZ = [self.bpj[12 + h] for h in range(4)]
        BR, bbr = self.SM[:, 1, :], self.bsm[1]
        G, bG = self.SM[:, 2, :], self.bsm[2]
        nw = self.bcst[:, 1024:1536]
        ctx = [dict(), dict()]

        def fmt(i):
            return self.FM[:, i, :].rearrange("p (h c) -> p h c", h=4), self.bfm[i]

        def half(i, k):
            ap, b_ = self.sr(i)
            return ap[:, k * 256:(k + 1) * 256].rearrange("p (h c) -> p h c", h=4), b_

        def P3(b, C):
            return self.ps[b][0:C, 0:256].rearrange("p (h c) -> p h c", h=4)

        def genA(ci):
            c0, C, sidx = chunks[ci]
            par = ci % 2
            cx = ctx[par]
            cx.clear()
            rr = C > 1
            PG = self.ps[3][:, 0:256].rearrange("p (h c) -> p h c", h=4)
            PB = self.ps[3][:, 256:512].rearrange("p (h c) -> p h c", h=4)
            for hh in range(H):
                self.mm(PG[:, hh, 0:C], self.sel[0:4, hh, :], G[0:4, c0:c0 + C], [self.bconst, bG], [self.bps[3]])
                self.mm(PB[:, hh, 0:C], self.sel[0:4, hh, :], BR[0:4, c0:c0 + C], [self.bconst, bbr], [self.bps[3]])
            self.mm(self.ps[0][0:C, 0:4], G[0:4, c0:c0 + C], self.ident[0:4, 0:4], [bG, self.bconst], [self.bps[0]])
            self.mm(self.ps[0][0:C, 4:8], BR[0:4, c0:c0 + C], self.ident[0:4, 0:4], [bbr, self.bconst], [self.bps[0]])
            yield
            Gbc, bGbc = fmt(1)
            gam, bgam = fmt(2 if par == 0 else 0)
            bet, bbet = fmt(3)
            self.cp(Gbc[:, :, 0:C], PG[:, :, 0:C], [self.bps[3]], [bGbc], eng="act")
            self.act(gam[:, :, 0:C], PG[:, :, 0:C], AF.Exp, [self.bps[3]], [bgam])
            self.cp(bet[:, :, 0:C], PB[:, :, 0:C], [self.bps[3]], [bbet], eng="dve")
            cl, bcl = self.sm8[:, 3, :], self.bsm8[3]
            self.cp(cl[0:C, :], self.ps[0][0:C, 0:8], [self.bps[0]], [bcl], eng="dve")
            dc, bdc = self.sm8[:, 4, :], self.bsm8[4]
            self.tt(dc[0:C, 0:4], Gbc[0:C, :, C - 1], cl[0:C, 0:4], ALU.subtract, [bGbc, bcl], [bdc])
            self.act(dc[0:C, 0:4], dc[0:C, 0:4], AF.Exp, [bdc], [bdc])
            yield
            Kv = self.PJ[:, 4:8, c0:c0 + C]
            Qv = self.PJ[:, 0:4, c0:c0 + C]
            kb, bkb = half(3 if par == 0 else 16, 0)
            kbg, bkbg = half(3 if par == 0 else 16, 1)
            qg, bqg = half(4 if par == 0 else 17, 0)
            Kc, bKc = half(4 if par == 0 else 17, 1)
            Qc, bQc = half(15, par)
            self.tt(R(kb[:, :, 0:C]), Kv, bet[:, :, 0:C], ALU.mult, bK + [bbet], [bkb])
            self.tt(R(kbg[:, :, 0:C]), kb[:, :, 0:C], gam[:, :, 0:C], ALU.mult, [bkb, bgam], [bkbg])
            self.tt(R(qg[:, :, 0:C]), Qv, gam[:, :, 0:C], ALU.mult, bQ + [bgam], [bqg])
            self.cp(R(Kc[:, :, 0:C]), Kv, bK, [bKc], eng="dve")
            self.cp(R(Qc[:, :, 0:C]), Qv, bQ, [bQc], eng="dve")
            QK3, bQK = self.sr3(7 if par == 0 else 18, 4)
            if C > 1:
                D1, bD1 = self.sc3(5, 4)
                D2, bD2 = self.sc3(6, 4)
                D3, bD3 = self.sc3(7, 4)
                gcol_b = cl[0:C, 0:4].unsqueeze(2).broadcast_to([C, 4, C])
                self.tt(D1[0:C, :, 0:C], Gbc[0:C, :, 0:C], gcol_b, ALU.subtract, [bGbc, bcl], [bD1])
                self.stt(D3[0:C, :, 0:C], Gbc[0:C, :, 0:C], -1.0, gcol_b, ALU.mult, ALU.add, [bGbc, bcl], [bD3])

                def nmask(m):
                    return m[0:C, 0:C].unsqueeze(1).broadcast_to([C, 4, C])

                self.tt(D2[0:C, :, 0:C], D1[0:C, :, 0:C], nmask(self.niu), ALU.add, [bD1, self.bconst], [bD2], eng="pool")
                self.tt(D1[0:C, :, 0:C], D1[0:C, :, 0:C], nmask(self.nsu), ALU.add, [bD1, self.bconst], [bD1], eng="pool")
                self.tt(D3[0:C, :, 0:C], D3[0:C, :, 0:C], nmask(self.nsl), ALU.add, [bD3, self.bconst], [bD3], eng="pool")
                self.act(D1[0:C, :, 0:C], D1[0:C, :, 0:C], AF.Exp, [bD1], [bD1])
                self.act(D2[0:C, :, 0:C], D2[0:C, :, 0:C], AF.Exp, [bD2], [bD2])
                self.act(D3[0:C, :, 0:C], D3[0:C, :, 0:C], AF.Exp, [bD3], [bD3])
            yield
            U3, bU = self.sr3(5, 4)
            L3, bL = self.sr3(6, 4)
            for hh in range(H):
                Kh = Kc[:, hh, 0:C]
                Qh = Qc[:, hh, 0:C]
                if C > 1:
                    self.mm(P3(0, C)[:, hh, 0:C], Kh, kb[:, hh, 0:C], [bKc, bkb], [self.bps[0]], r=rr)
                    self.mm(P3(1, C)[:, hh, 0:C], kb[:, hh, 0:C], Kh, [bKc, bkb], [self.bps[1]], r=rr)
                self.mm(P3(2, C)[:, hh, 0:C], Kh, Qh, [bKc, bQc], [self.bps[2]], r=rr)
            yield
            if C > 1:
                self.tt(R(U3[0:C, :, 0:C]), P3(0, C)[:, :, 0:C], D1[0:C, :, 0:C], ALU.mult, [self.bps[0], bD1], [bU])
                self.tt(R(L3[0:C, :, 0:C]), P3(1, C)[:, :, 0:C], D3[0:C, :, 0:C], ALU.mult, [self.bps[1], bD3], [bL])
                self.tt(R(QK3[0:C, :, 0:C]), P3(2, C)[:, :, 0:C], D2[0:C, :, 0:C], ALU.mult, [self.bps[2], bD2], [bQK])
                yield
                xfin = self.sr3(14 if par == 0 else 9, 4)
                xtmp = self.sr3(13, 4)
                nlev = {64: 5, 32: 4, 16: 3, 8: 2, 4: 1, 2: 0}[C]
                xa, xb = (xtmp, xfin) if nlev % 2 == 1 else (xfin, xtmp)
                scr = [self.sr3(10, 4), self.sr3(11, 4), self.sr3(12, 4), xa, xb]
                res = {}
                yield from self.inverse_gen(U3, bU, L3, bL, C, 4, scr, (0, 1, 2), res)
                cx["X"] = res["X"]
            else:
                self.cp(R(QK3[0:C, :, 0:C]), P3(2, C)[:, :, 0:C], [self.bps[2]], [bQK], eng="dve")
                yield
            bV_, bbV = self.sc(1 if par == 0 else 0)
            Kd, bKd = self.sr(0 if par == 0 else 19)
            Zt, bZt = self.sc(2 if par == 0 else 3)
            for hh in range(H):
                self.tr(self.ps[3][0:C, hh * 128:(hh + 1) * 128], self.PJ[:, 8 + hh, c0:c0 + C], self.ident[:, :], [bV[hh], self.bconst], [self.bps[3]])
            for hh in range(H):
                self.tr(self.ps[0][0:C, hh * 128:(hh + 1) * 128], self.PJ[:, 4 + hh, c0:c0 + C], self.ident[:, :], [bK[hh], self.bconst], [self.bps[0]])
            for hh in range(H):
                self.tr(self.ps[1][0:C, hh * 128:(hh + 1) * 128], self.PJ[:, 12 + hh, c0:c0 + C], self.ident[:, :], [bZ[hh], self.bconst], [self.bps[1]])
            yield

            def T3(ap):
                return ap[0:C, 0:512].rearrange("p (h v) -> p h v", h=4)

            self.tt(T3(bV_), T3(self.ps[3]), cl[0:C, 4:8].unsqueeze(2).broadcast_to([C, 4, 128]), ALU.mult, [self.bps[3], bcl], [bbV])
            self.tt(R(T3(Kd)), T3(self.ps[0]), dc[0:C, 0:4].unsqueeze(2).broadcast_to([C, 4, 128]), ALU.mult, [self.bps[0], bdc], [bKd])
            self.cp(Zt[0:C, 0:512], self.ps[1][0:C, 0:512], [self.bps[1]], [bZt], eng="act")
            cx.update(kbg=(kbg, bkbg), qg=(qg, bqg), QK=(QK3, bQK), bV=(bV_, bbV), Kd=(Kd, bKd), Zt=(Zt, bZt), gam=(gam, bgam))
            yield

        def genB(ci, slot=0):
            c0, C, sidx = chunks[ci]
            b4, b5, b6, b7 = (4, 5, 6, 7) if slot == 0 else (0, 1, 2, 3)
            par = ci % 2
            cx = ctx[par]
            rr = C > 1
            samp = sidx < 16
            kbg, bkbg = cx["kbg"]
            qg, bqg = cx["qg"]
            QK3, bQK = cx["QK"]
            bV_, bbV = cx["bV"]
            Kd, bKd = cx["Kd"]
            Zt, bZt = cx["Zt"]
            gam, bgam = cx["gam"]

            def T3(ap):
                return ap[0:C, 0:512].rearrange("p (h v) -> p h v", h=4)

            if samp:
                St, bS = self.Ss[sidx % 2], self.bSs[sidx % 2]
                self.dma(St[:, :, :].rearrange("p h v -> p (h v)"), d["sgd"][sidx], [], [bS])
            else:
                St, bS = self.Sp, self.bSp
            if rr:
                Sr_t, bSr = self.sr(8)
                Sr = Sr_t[:, 0:512].rearrange("p (h v) -> p h v", h=4)
                self.cp(R(Sr), St[:, :, :], [bS], [bSr], eng="dve")
            else:
                Sr, bSr = St, bS
            for hh in range(H):
                self.mm(self.ps[b4][0:C, hh * 128:(hh + 1) * 128], kbg[:, hh, 0:C], Sr[:, hh, :], [bkbg, bSr], [self.bps[b4]], r=rr)
            yield
            Rs, bRs = self.sr(1 if slot == 0 else 20)
            self.tt(R(Rs[0:C, 0:512]), bV_[0:C, 0:512], self.ps[b4][0:C, 0:512], ALU.subtract, [bbV, self.bps[b4]], [bRs])
            yield
            if C > 1:
                X3, bX = cx["X"]
                VN, bVN = self.sr(2)
                for hh in range(H):
                    hc = slice(hh * 128, (hh + 1) * 128)
                    self.mm(self.ps[b5][0:C, hc], X3[0:C, hh, 0:C], Rs[0:C, hc], [bX, bRs], [self.bps[b5]], r=rr)
                yield
                self.cp(R(VN[0:C, 0:512]), self.ps[b5][0:C, 0:512], [self.bps[b5]], [bVN], eng="act")
                yield
            else:
                VN, bVN = Rs, bRs
            for hh in range(H):
                hc = slice(hh * 128, (hh + 1) * 128)
                self.mm(self.ps[b7][:, hc], Kd[0:C, hc], VN[0:C, hc], [bKd, bVN], [self.bps[b7]], r=rr)
            for hh in range(H):
                hc = slice(hh * 128, (hh + 1) * 128)
                self.mm(self.ps[b6][0:C, hc], qg[:, hh, 0:C], Sr[:, hh, :], [bqg, bSr], [self.bps[b6]], start=(hh == 0), stop=True, r=rr, g=True)
            for hh in range(H):
                hc = slice(hh * 128, (hh + 1) * 128)
                self.mm(self.ps[b6][0:C, hc], QK3[0:C, hh, 0:C], VN[0:C, hc], [bQK, bVN], [self.bps[b6]], start=False, stop=True, r=rr, g=True)
            yield
            for hh in range(H):
                hc = slice(hh * 128, (hh + 1) * 128)
                self.stt(St[:, hh, :], St[:, hh, :], gam[:, hh, C - 1:C], self.ps[b7][:, hc], ALU.mult, ALU.add,
                         [bS, bgam, self.bps[b7]], [bS])
            if samp:
                t = self.dma(d["ogd"][sidx], St[:, :, :].rearrange("p h v -> p (h v)"), [bS], [])
                self.out_tokens.append(t)
            yield
            sq, bsq = self.sc(4 if slot == 0 else 5)
            self.act(sq[0:C, 0:512], self.ps[b6][0:C, 0:512], AF.Square, [self.bps[b6]], [bsq])
            s2, bs2 = self.sm8[:, 5 + slot, :], self.bsm8[5 + slot]
            self.red(s2[0:C, 0:4], T3(sq), [bsq], [bs2])
            self.act(s2[0:C, 0:4], s2[0:C, 0:4], AF.Ln, [bs2, self.bc3], [bs2], bias=self.epsn[0:C, 0:1], scale=1.0 / 128.0)
            self.act(s2[0:C, 0:4], s2[0:C, 0:4], AF.Exp, [bs2], [bs2], scale=-0.5)
            yield
            o2, bo2 = sq, bsq
            self.tt(T3(o2), T3(self.ps[b6]), s2[0:C, 0:4].unsqueeze(2).broadcast_to([C, 4, 128]), ALU.mult, [self.bps[b6], bs2], [bo2])
            self.tt(o2[0:C, 0:512], o2[0:C, 0:512], nw[0:C, :], ALU.mult, [bo2, self.bbc], [bo2], eng="pool")
            self.tt(o2[0:C, 0:512], o2[0:C, 0:512], Zt[0:C, 0:512], ALU.mult, [bo2, bZt], [bo2], eng="pool")
            yield
            for hh in range(H):
                self.tr(self.ps[b4][:, hh * 64:hh * 64 + C], o2[0:C, hh * 128:(hh + 1) * 128], self.ident[0:C, 0:C],
                        [bo2, self.bconst], [self.bps[b4]])
            yield
            for hh in range(H):
                self.cp(self.mixT[:, 4 + hh, c0:c0 + C], self.ps[b4][:, hh * 64:hh * 64 + C], [self.bps[b4]], [self.bmx[4 + hh]], eng="act")
            yield

        self.run_chunks(genA, genB, chunks)

    def finish_outputs(self):
        d = self.d
        self.cp(self.OSH[:, :, 16], self.CR[:, 0:15, 2], [self.bCR], [self.bOSH])
        self.cp(self.OCV[:, :, :, 16], self.CR[:, 15:27, :], [self.bCR], [self.bOCV])
        self.out_tokens.append(self.dma(d["osh"], self.OSH[:, :, :].rearrange("p a s -> p (a s)"), [self.bOSH], []))
        self.out_tokens.append(self.dma(d["ocv"], self.OCV[:, :, :, :].rearrange("p a t s -> p (a t s)"), [self.bOCV], []))
        self.out_tokens.append(self.dma(d["orw"][16], self.Bp[:, :, :].rearrange("p j v -> p (j v)"), [self.bBp], []))
        self.out_tokens.append(self.dma(d["ogd"][16], self.Sp[:, :, :].rearrange("p h v -> p (h v)"), [self.bSp], []))


def _prep_shared(inp):
    f = np.float32
    sh = {}

    def gate(w):
        return np.ascontiguousarray(w.reshape(8, 128, 11, 256).transpose(2, 1, 0, 3)).reshape(11, 128, 2048)

    def down(w, ng):
        return np.ascontiguousarray(w.reshape(ng, 2, 128, 2, 512).transpose(3, 0, 2, 1, 4)).reshape(2 * ng, 128, 1024)

    def gateup(wg, wu):
        g = wg.reshape(8, 128, 22, 128).transpose(2, 1, 0, 3)
        u = wu.reshape(8, 128, 22, 128).transpose(2, 1, 0, 3)
        return np.ascontiguousarray(np.concatenate([g, u], axis=3)).reshape(22, 128, 2048)

    sh["wgu1"] = gateup(inp["w_gate1"][0], inp["w_up1"][0])
    sh["wgu2"] = gateup(inp["w_gate2"][0], inp["w_up2"][0])
    sh["wd1"] = down(inp["w_down1"][0], 11)
    sh["wd2"] = down(inp["w_down2"][0], 11)
    sh["wout"] = down(inp["w_out"][0], 4)
    W = inp["w_in"][0]
    win = np.zeros((17, 128, 8, 256), f)
    for oc in range(NOC):
        if OC[oc] is None:
            continue
        s, M = OC[oc]
        blk = W[:, s:s + M].reshape(8, 128, M).transpose(1, 0, 2)
        win[oc // 2, :, :, (oc % 2) * 128:(oc % 2) * 128 + M] = blk
    sh["win"] = win.reshape(17, 128, 2048)
    cols = np.zeros((128, NCOLS), f)

    def put(name, vec):
        cols[:len(vec), COLS[name]] = vec

    for nm, key in (("gf1", "g_ffn1"), ("gmx", "g_mix"), ("gf2", "g_ffn2")):
        for k in range(8):
            put("%s%d" % (nm, k), inp[key][0][k * 128:(k + 1) * 128])
    for k in range(8):
        put("gfn%d" % k, inp["g_final"][k * 128:(k + 1) * 128])
    for oc in range(15):
        s, M = OC[oc]
        put("mu%d" % oc, inp["mu_shift"][0][s:s + M])
    for nm, key in (("w0", "w0"), ("a0", "a0"), ("kk", "k_k"), ("ka", "k_a")):
        for j in range(4):
            put("%s%d" % (nm, j), inp[key][0][j * 128:(j + 1) * 128])
    rk = inp["r_k"][0].reshape(512)
    for j in range(4):
        put("rk%d" % j, rk[j * 128:(j + 1) * 128])
    for i in range(12):
        for t in range(4):
            put("cw%d_%d" % (i, t), inp["conv_w"][0][t, i * 128:(i + 1) * 128])
    put("dtb", inp["dt_bias"][0])
    put("alog", inp["a_log"][0])
    sh["cols"] = cols
    bc = np.concatenate([inp["lnx_w"][0], inp["lnx_b"][0], np.tile(inp["gdn_norm_w"][0], 4)]).astype(f)
    sh["bc"] = np.ascontiguousarray(np.tile(bc[None, :], (64, 1)))
    sh["wdu"] = np.ascontiguousarray(inp["w_decay_up"][0])
    sh["wau"] = np.ascontiguousarray(inp["w_a_up"][0])
    sh["wgu"] = np.ascontiguousarray(inp["w_g_up"][0])
    return sh


def _prep_core(inp, c):
    f = np.float32
    m = {}
    s0 = 16 * c
    rows = np.concatenate([inp["x_sample"][s0:s0 + 16, 0, :], inp["meta_tokens"], inp["x_prompt"][c]], axis=0)
    m["xT"] = np.ascontiguousarray(rows.T).reshape(8, 128, TTOT)
    st = inp["state_rwkv"][0, s0:s0 + 16]
    m["srw"] = np.ascontiguousarray(st.reshape(16, 4, 2, 64, 64).transpose(0, 2, 4, 1, 3)).reshape(16, 128, 256)
    sg = inp["state_gdn"][0, s0:s0 + 16]
    m["sgd"] = np.ascontiguousarray(sg.transpose(0, 2, 1, 3)).reshape(16, 128, 512)
    ss = inp["state_shift"][0, s0:s0 + 16]
    ssh = np.zeros((128, 15, 16), f)
    for oc in range(15):
        s, M = OC[oc]
        ssh[:M, oc, :] = ss[:, s:s + M].T
    m["ssh"] = ssh.reshape(128, 240)
    cv = inp["state_conv"][0, s0:s0 + 16]
    m["scv"] = np.ascontiguousarray(cv.reshape(16, 3, 12, 128).transpose(3, 2, 1, 0)).reshape(128, 576)
    return m


_NC_CACHE = {}
_BUILD_KW = {}


def _get_nc(**kw):
    key = tuple(sorted(kw.items()))
    if key not in _NC_CACHE:
        _NC_CACHE[key] = Builder(**kw).build()
    return _NC_CACHE[key]


def kernel(**inputs):
    inp = {k: np.asarray(v, dtype=np.float32) for k, v in inputs.items()}
    nc = _get_nc(**_BUILD_KW)
    sh = _prep_shared(inp)
    in_maps = []
    for c in range(NCORE):
        m = dict(sh)
        m.update(_prep_core(inp, c))
        in_maps.append(m)
    res = run_bass_kernel_spmd(nc, in_maps, core_ids=list(range(NCORE)))
    f = np.float32
    y_prompt = np.zeros((8, 2048, 1024), f)
    y_sample = np.zeros((128, 1, 1024), f)
    rw_p = np.zeros((1, 8, 8, 64, 64), f)
    sh_p = np.zeros((1, 8, 1696), f)
    gd_p = np.zeros((1, 8, 4, 128, 128), f)
    cv_p = np.zeros((1, 8, 3, 1536), f)
    rw_s = np.zeros((1, 128, 8, 64, 64), f)
    sh_s = np.zeros((1, 128, 1696), f)
    gd_s = np.zeros((1, 128, 4, 128, 128), f)
    cv_s = np.zeros((1, 128, 3, 1536), f)
    for c in range(NCORE):
        r = res.results[c]
        y = np.asarray(r["yT"]).reshape(1024, TTOT).T
        y_sample[16 * c:16 * c + 16, 0, :] = y[0:16]
        y_prompt[c] = y[32:]
        orw = np.asarray(r["orw"]).reshape(17, 2, 64, 4, 64).transpose(0, 3, 1, 4, 2).reshape(17, 8, 64, 64)
        rw_s[0, 16 * c:16 * c + 16] = orw[0:16]
        rw_p[0, c] = orw[16]
        ogd = np.asarray(r["ogd"]).reshape(17, 128, 4, 128).transpose(0, 2, 1, 3)
        gd_s[0, 16 * c:16 * c + 16] = ogd[0:16]
        gd_p[0, c] = ogd[16]
        osh = np.asarray(r["osh"]).reshape(128, 15, 17)
        full = np.zeros((17, 1696), f)
        for oc in range(15):
            s, M = OC[oc]
            full[:, s:s + M] = osh[:M, oc, :].T
        sh_s[0, 16 * c:16 * c + 16] = full[0:16]
        sh_p[0, c] = full[16]
        ocv = np.asarray(r["ocv"]).reshape(128, 12, 3, 17).transpose(3, 2, 1, 0).reshape(17, 3, 1536)
        cv_s[0, 16 * c:16 * c + 16] = ocv[0:16]
        cv_p[0, c] = ocv[16]
    return (y_prompt, y_sample, rw_p, sh_p, gd_p, cv_p, rw_s, sh_s, gd_s, cv_s)
```

```python
import contextlib
import numpy as np
import concourse.bass as bass
import concourse.mybir as mybir
from concourse.bass_utils import run_bass_kernel_spmd

F32 = mybir.dt.float32
BF16 = mybir.dt.bfloat16
R32 = mybir.dt.float32r


def R(ap):
    return ap.bitcast(R32)
AF = mybir.ActivationFunctionType
ALU = mybir.AluOpType
AX = mybir.AxisListType

ENGS = ("pe", "act", "dve", "pool", "sp")
NCORE = 8
TTOT = 2080
TM = 544
NSAMP = 16
EXPM05 = 0.6065306597126334
NEG = -1.0e30
STRICT = False
USE_R32 = True


class Buf:
    __slots__ = ("name", "w", "r", "dsem", "dcount", "excl")

    def __init__(self, name):
        self.name = name
        self.excl = False
        self.w = None
        self.r = []
        self.dsem = None
        self.dcount = 0


class Prog:
    def __init__(self, nc, stack):
        self.nc = nc
        self.stack = stack
        self.ops = {e: [] for e in ENGS}
        self.count = {e: 0 for e in ENGS}
        self.seen = {e: {} for e in ENGS}
        self.sems = {}
        for e in ENGS:
            self.sems[e] = stack.enter_context(nc.semaphore("s_" + e))
        self.nbuf = 0
        self.final_tokens = []
        self.final_eng = "sp"

    def sbuf(self, name, shape, dt=F32):
        return self.stack.enter_context(self.nc.sbuf_tensor("sb_" + name, list(shape), dt))

    def psum(self, name, shape, dt=F32):
        return self.stack.enter_context(self.nc.psum_tensor("pp_" + name, list(shape), dt))

    def buf(self, name=None):
        self.nbuf += 1
        return Buf(name or "b%d" % self.nbuf)

    def _dsem(self, b):
        if b.dsem is None:
            self.nbuf += 1
            key = "d%d" % self.nbuf
            self.sems[key] = self.stack.enter_context(self.nc.semaphore(key))
            b.dsem = key
        return b.dsem

    def _waits(self, eng, reads, writes):
        need = {}

        def add(tok, raw):
            if tok is None:
                return
            k, v = tok
            if k == eng:
                if eng == "pe":
                    return
                if not STRICT and (not raw or v < self.count[eng] - 1):
                    return
            if v > need.get(k, 0):
                need[k] = v

        for b in reads:
            add(b.w, True)
            if b.excl:
                for t in b.r:
                    if t[0] != eng:
                        add(t, False)
        for b in writes:
            add(b.w, False)
            for t in b.r:
                add(t, False)
        out = []
        seen = self.seen[eng]
        for k, v in need.items():
            if seen.get(k, 0) >= v:
                continue
            seen[k] = v
            out.append((k, v))
        return out

    def _record(self, tok, reads, writes):
        for b in reads:
            if len(b.r) > 24:
                best = {}
                for k, v in b.r:
                    if v > best.get(k, 0):
                        best[k] = v
                b.r = list(best.items())
            b.r.append(tok)
        for b in writes:
            b.w = tok
            b.r = []

    def op(self, eng, fn, reads=(), writes=()):
        waits = self._waits(eng, reads, writes)
        self.count[eng] += 1
        tok = (eng, self.count[eng])
        self.ops[eng].append((waits, fn, eng, 1))
        self._record(tok, reads, writes)
        return tok

    def dma(self, eng, fn, reads=(), writes=(), sem_buf=None):
        waits = self._waits(eng, reads, writes)
        sb = sem_buf or (writes[0] if writes else reads[0])
        key = self._dsem(sb)
        sb.dcount += 16
        tok = (key, sb.dcount)
        self.ops[eng].append((waits, fn, key, 16))
        self._record(tok, reads, writes)
        return tok

    def emit(self):
        nc = self.nc
        engobj = {"pe": "tensor", "act": "scalar", "dve": "vector", "pool": "gpsimd", "sp": "sync"}
        with nc.Block() as block:
            for e in ENGS:
                ops = self.ops[e]
                fin = self.final_tokens if self.final_eng == e else []
                if not ops and not fin:
                    continue

                def body(eobj, ops=ops, fin=fin):
                    for waits, fn, key, inc in ops:
                        for k, v in waits:
                            eobj.wait_ge(self.sems[k], v)
                        ins = fn(eobj)
                        ins.then_inc(self.sems[key], inc)
                    for k, v in fin:
                        eobj.wait_ge(self.sems[k], v)

                getattr(block, engobj[e])(body)


OC = []
for i in range(12):
    OC.append((i * 128, 128))
OC.append((1536, 32))
OC.append((1568, 32))
OC.append((1600, 96))
OC.append(None)
for i in range(16):
    OC.append((1696 + i * 128, 128))
OC.append((1696 + 2048, 4))
OC.append((1696 + 2052, 4))
NOC = 34

PASSES = []
_ch0 = [(s, 1, s) for s in range(16)] + [(16, 16, 16)] + [(32 + 64 * i, 64, 16) for i in range(8)]
PASSES.append(dict(col0=0, T=544, nsamp=16, chunks=_ch0, subs=[(0, 272), (272, 272)]))
for _p in range(3):
    PASSES.append(dict(col0=544 + 512 * _p, T=512, nsamp=0,
                       chunks=[(64 * i, 64, 16) for i in range(8)], subs=[(0, 512)]))

COLS = {}


def _build_cols_index():
    n = 0
    for nm in ("gf1", "gmx", "gf2", "gfn"):
        for k in range(8):
            COLS["%s%d" % (nm, k)] = n
            n += 1
    for oc in range(15):
        COLS["mu%d" % oc] = n
        n += 1
    for nm in ("w0", "a0", "kk", "ka", "rk"):
        for j in range(4):
            COLS["%s%d" % (nm, j)] = n
            n += 1
    for i in range(12):
        for t in range(4):
            COLS["cw%d_%d" % (i, t)] = n
            n += 1
    COLS["dtb"] = n
    n += 1
    COLS["alog"] = n
    n += 1
    return n


NCOLS = _build_cols_index()


class Builder:
    def __init__(self, npass=4, do_mix=True, do_tail=True, stop=99, cf="all", sub=99):
        self.sub = sub
        self.cf = cf
        self.stop = stop
        self.npass = npass
        self.do_mix = do_mix
        self.do_tail = do_tail

    def din(self, name, shape):
        return self.nc.dram_tensor(name, list(shape), F32, kind="ExternalInput").ap()

    def dout(self, name, shape):
        return self.nc.dram_tensor(name, list(shape), F32, kind="ExternalOutput").ap()

    def mm(self, out, lhsT, rhs, rd, wr, start=True, stop=True, r=False, g=False):
        if r and USE_R32:
            lhsT = R(lhsT)
            rhs = R(rhs)
        if g:
            self.P.op("pe", lambda e: e.matmul(out, lhsT=lhsT, rhs=rhs, start=start, stop=stop, skip_group_check=True), rd, wr)
        else:
            self.P.op("pe", lambda e: e.matmul(out, lhsT=lhsT, rhs=rhs, start=start, stop=stop), rd, wr)

    def tr(self, out, in_, ident, rd, wr):
        self.P.op("pe", lambda e: e.transpose(out=out, in_=in_, identity=ident), rd, wr)

    def act(self, out, in_, func, rd, wr, bias=None, scale=None):
        kw = {}
        if bias is not None:
            kw["bias"] = bias
        if scale is not None:
            kw["scale"] = scale
        self.P.op("act", lambda e: e.activation(out=out, in_=in_, func=func, **kw), rd, wr)

    def tt(self, out, in0, in1, op, rd, wr, eng="dve"):
        self.P.op(eng, lambda e: e.tensor_tensor(out=out, in0=in0, in1=in1, op=op), rd, wr)

    def ts(self, out, in0, s1, op0, rd, wr, s2=None, op1=None, eng="dve"):
        if op1 is None:
            self.P.op(eng, lambda e: e.tensor_scalar(out=out, in0=in0, scalar1=s1, scalar2=None, op0=op0), rd, wr)
        else:
            self.P.op(eng, lambda e: e.tensor_scalar(out=out, in0=in0, scalar1=s1, scalar2=s2, op0=op0, op1=op1), rd, wr)

    def stt(self, out, in0, scalar, in1, op0, op1, rd, wr):
        self.P.op("dve", lambda e: e.scalar_tensor_tensor(out=out, in0=in0, scalar=scalar, in1=in1, op0=op0, op1=op1), rd, wr)

    def cp(self, out, in_, rd, wr, eng="dve"):
        if eng == "act":
            self.P.op("act", lambda e: e.activation(out=out, in_=in_, func=AF.Copy), rd, wr)
        elif eng == "dve":
            self.P.op("dve", lambda e: e.tensor_scalar(out=out, in0=in_, scalar1=1.0, scalar2=None, op0=ALU.mult), rd, wr)
        else:
            self.P.op(eng, lambda e: e.tensor_copy(out=out, in_=in_), rd, wr)

    def red(self, out, in_, rd, wr):
        self.P.op("dve", lambda e: e.tensor_reduce(out=out, in_=in_, axis=AX.X, op=ALU.add), rd, wr)

    def recip(self, out, in_, rd, wr):
        self.P.op("dve", lambda e: e.reciprocal(out=out, in_=in_), rd, wr)

    def scan(self, out, d0, d1, rd, wr):
        self.P.op("dve", lambda e: e.tensor_tensor_scan(out=out, data0=d0, data1=d1, initial=0.0, op0=ALU.mult, op1=ALU.add), rd, wr)

    def memset(self, ap, val, wr, eng="dve"):
        self.P.op(eng, lambda e: e.memset(ap, val), (), wr)

    def dma(self, out, in_, rd, wr, eng="sp", sem_buf=None):
        return self.P.dma(eng, lambda e: e.dma_start(out=out, in_=in_), rd, wr, sem_buf=sem_buf)

    def col(self, name, m=128):
        i = COLS[name]
        return self.cols[0:m, i:i + 1]

    def get_w(self, expect):
        i = self.wpos
        assert self.wlist[i][0] == expect, (self.wlist[i][0], expect)
        self.wpos += 1
        NB = len(self.WT)
        while self.wnext < len(self.wlist) and self.wnext <= i + NB - 1:
            k = self.wnext
            t = self.WT[k % NB]
            src = self.wlist[k][1]
            self.dma(t[:, 0:src.shape[1]], src, [], [self.bWT[k % NB]], eng="pool")
            self.wnext += 1
        return self.WT[i % NB], self.bWT[i % NB]

    def build(self):
        nc = bass.Bass("TRN2", target_bir_lowering=False)
        self.nc = nc
        d = {}
        d["xT"] = self.din("xT", [8, 128, TTOT])
        for nm in ("wgu1", "wgu2"):
            d[nm] = self.din(nm, [22, 128, 2048])
        for nm in ("wd1", "wd2"):
            d[nm] = self.din(nm, [22, 128, 1024])
        d["win"] = self.din("win", [17, 128, 2048])
        d["wout"] = self.din("wout", [8, 128, 1024])
        d["cols"] = self.din("cols", [128, NCOLS])
        d["bc"] = self.din("bc", [64, 1536])
        d["wdu"] = self.din("wdu", [32, 512])
        d["wau"] = self.din("wau", [32, 512])
        d["wgu"] = self.din("wgu", [96, 512])
        d["srw"] = self.din("srw", [16, 128, 256])
        d["sgd"] = self.din("sgd", [16, 128, 512])
        d["ssh"] = self.din("ssh", [128, 15 * 16])
        d["scv"] = self.din("scv", [128, 12 * 48])
        d["yT"] = self.dout("yT", [8, 128, TTOT])
        d["orw"] = self.dout("orw", [17, 128, 256])
        d["ogd"] = self.dout("ogd", [17, 128, 512])
        d["osh"] = self.dout("osh", [128, 15 * 17])
        d["ocv"] = self.dout("ocv", [128, 12 * 51])
        self.d = d

        with contextlib.ExitStack() as st:
            P = Prog(nc, st)
            self.P = P
            self.alloc()
            self.make_wlist()
            NB = len(self.WT)
            while self.wnext < min(NB, len(self.wlist)):
                k = self.wnext
                src = self.wlist[k][1]
                self.dma(self.WT[k % NB][:, 0:src.shape[1]], src, [], [self.bWT[k % NB]], eng="pool")
                self.wnext += 1
            self.setup_consts()
            self.out_tokens = []
            for p in range(self.npass):
                self.run_pass(p)
            self.finish_outputs()
            P.final_tokens = [t for t in self.out_tokens if t is not None]
            P.emit()
        return nc

    def alloc(self):
        P = self.P
        self.h = P.sbuf("h", [128, 8, TM])
        self.bh = [P.buf("h%d" % k) for k in range(8)]
        self.xn = P.sbuf("xn", [128, 8, TM], BF16)
        self.bxn = [P.buf("xn%d" % k) for k in range(8)]
        self.mixT = P.sbuf("mixT", [128, 8, TM], BF16)
        self.bmx = [P.buf("mx%d" % k) for k in range(8)]
        self.PJ = P.sbuf("PJ", [128, 20, TM])
        self.bpj = [P.buf("pj%d" % k) for k in range(20)]
        self.PJb = self.PJ[:, :, :].rearrange("p a t -> p (a t)").bitcast(BF16)
        NSC = 8
        self.SC = P.sbuf("SC", [128, NSC, TM])
        self.bsc = [P.buf("sc%d" % k) for k in range(NSC)]
        NSR = 25
        self.SR = P.sbuf("SR", [128, NSR, 512])
        self.bsr = [P.buf("sr%d" % k) for k in range(NSR)]
        self.SM = P.sbuf("SM", [128, 3, TM])
        self.bsm = [P.buf("sm%d" % k) for k in range(3)]
        self.RAW = [P.sbuf("RAW%d" % k, [128, 3 + TM]) for k in range(2)]
        self.bRAW = [P.buf("raw%d" % k) for k in range(2)]
        self.RS = [P.sbuf("RS%d" % k, [128, 16]) for k in range(2)]
        self.bRS = [P.buf("rs%d" % k) for k in range(2)]
        self.WT = [P.sbuf("WT%d" % k, [128, 2048], BF16) for k in range(3)]
        self.bWT = [P.buf("wt%d" % k) for k in range(3)]
        self.FM = P.sbuf("FM", [128, 4, 256])
        self.bfm = [P.buf("fm%d" % k) for k in range(4)]
        self.MR = P.sbuf("MR", [128, TM])
        self.bMR = P.buf("MR")
        self.cols = P.sbuf("cols", [128, NCOLS])
        self.bcols = P.buf("cols")
        self.c2 = P.sbuf("c2", [128, 8])
        self.bc2 = P.buf("c2")
        self.bcst = P.sbuf("bcst", [64, 1536])
        self.bbc = P.buf("bc")
        self.wdu = P.sbuf("wdu", [32, 512])
        self.wau = P.sbuf("wau", [32, 512])
        self.wgu = P.sbuf("wgu", [96, 512])
        self.bwsm = P.buf("wsm")
        self.ident = P.sbuf("ident", [128, 128])
        self.ones = P.sbuf("ones", [128, 128])
        self.onesb = P.sbuf("onesb", [128, 128], BF16)
        self.blk = P.sbuf("blk", [128, 128])
        self.blk2 = P.sbuf("blk2", [128, 2])
        self.sel = P.sbuf("sel", [4, 4, 128])
        self.msu = P.sbuf("msu", [64, 64])
        self.miu = P.sbuf("miu", [64, 64])
        self.msl = P.sbuf("msl", [64, 64])
        self.nsu = P.sbuf("nsu", [64, 64])
        self.niu = P.sbuf("niu", [64, 64])
        self.nsl = P.sbuf("nsl", [64, 64])
        self.bconst = P.buf("const")
        self.CR = P.sbuf("CR", [128, 27, 3])
        self.bCR = P.buf("CR")
        self.SSH = P.sbuf("SSH", [128, 15, 16])
        self.SCV = P.sbuf("SCV", [128, 12, 3, 16])
        self.bsst = P.buf("sst")
        self.OSH = P.sbuf("OSH", [128, 15, 17])
        self.bOSH = P.buf("OSH")
        self.OCV = P.sbuf("OCV", [128, 12, 3, 17])
        self.bOCV = P.buf("OCV")
        self.Bp = P.sbuf("Bp", [128, 4, 64])
        self.bBp = P.buf("Bp")
        self.Sp = P.sbuf("Sp", [128, 4, 128])
        self.bSp = P.buf("Sp")
        self.Bs = [P.sbuf("Bs%d" % k, [128, 4, 64]) for k in range(2)]
        self.bBs = [P.buf("Bs%d" % k) for k in range(2)]
        self.Ss = [P.sbuf("Ss%d" % k, [128, 4, 128]) for k in range(2)]
        self.bSs = [P.buf("Ss%d" % k) for k in range(2)]
        self.PCs = P.sbuf("PCs", [128, 4, 32])
        self.bPCs = P.buf("PCs")
        self.sm8 = P.sbuf("sm8", [64, 8, 8])
        self.bsm8 = [P.buf("sm8_%d" % k) for k in range(8)]
        self.ps = [P.psum("ps%d" % k, [128, 512]) for k in range(8)]
        self.bps = [P.buf("ps%d" % k) for k in range(8)]
        for b_ in self.bps:
            b_.excl = True

    def sc(self, i):
        return self.SC[:, i, :], self.bsc[i]

    def sr(self, i):
        return self.SR[:, i, :], self.bsr[i]

    def sr3(self, i, H):
        ap, b = self.sr(i)
        return ap[:, 0:H * 64].rearrange("p (h c) -> p h c", h=H), b

    def make_wlist(self):
        d = self.d
        wl = []
        for p in range(self.npass):
            for c in range(22):
                wl.append(("wgu1", d["wgu1"][c]))
            for k in range(22):
                wl.append(("wd1", d["wd1"][k]))
            for g in range(17):
                wl.append(("win", d["win"][g]))
            if self.do_tail:
                for k in range(8):
                    wl.append(("wout", d["wout"][k]))
                for c in range(22):
                    wl.append(("wgu2", d["wgu2"][c]))
                for k in range(22):
                    wl.append(("wd2", d["wd2"][k]))
        self.wlist = wl
        self.wpos = 0
        self.wnext = 0

    def setup_consts(self):
        d = self.d
        bc_ = [self.bconst]
        self.dma(self.cols[:, :], d["cols"], [], [self.bcols])
        self.dma(self.bcst[:, :], d["bc"], [], [self.bbc])
        self.dma(self.wdu[:, :], d["wdu"], [], [self.bwsm])
        self.dma(self.wau[:, :], d["wau"], [], [self.bwsm])
        self.dma(self.wgu[:, :], d["wgu"], [], [self.bwsm])
        self.dma(self.SSH[:, :, :].rearrange("p a s -> p (a s)"), d["ssh"], [], [self.bsst])
        self.dma(self.SCV[:, :, :, :].rearrange("p a t s -> p (a t s)"), d["scv"], [], [self.bsst])

        def pool(fn):
            self.P.op("pool", fn, bc_, bc_)

        pool(lambda e: e.memset(self.ones[:, :], 1.0))
        pool(lambda e: e.memset(self.onesb[:, :], 1.0))
        pool(lambda e: e.memset(self.ident[:, :], 1.0))
        pool(lambda e: e.affine_select(out=self.ident[:, :], in_=self.ident[:, :], pattern=[[-1, 128]],
                                       compare_op=ALU.is_equal, fill=0.0, base=0, channel_multiplier=1))
        pool(lambda e: e.memset(self.blk[:, :], 0.0))
        pool(lambda e: e.memset(self.blk[0:64, 0:64], 1.0))
        pool(lambda e: e.memset(self.blk[64:128, 64:128], 1.0))
        pool(lambda e: e.memset(self.blk2[:, :], 0.0))
        pool(lambda e: e.memset(self.blk2[0:64, 0:1], 1.0))
        pool(lambda e: e.memset(self.blk2[64:128, 1:2], 1.0))
        pool(lambda e: e.memset(self.sel[:, :, :], 1.0))
        pool(lambda e: e.affine_select(out=self.sel[:, :, :], in_=self.sel[:, :, :], pattern=[[-1, 4], [0, 128]],
                                       compare_op=ALU.is_equal, fill=0.0, base=0, channel_multiplier=1))
        for t_, cmp_, pat, cm, fill, base0 in (
            (self.msu, ALU.is_gt, 1, -1, 0.0, 1.0),
            (self.miu, ALU.is_ge, 1, -1, 0.0, 1.0),
            (self.msl, ALU.is_gt, -1, 1, 0.0, 1.0),
            (self.nsu, ALU.is_gt, 1, -1, NEG, 0.0),
            (self.niu, ALU.is_ge, 1, -1, NEG, 0.0),
            (self.nsl, ALU.is_gt, -1, 1, NEG, 0.0),
        ):
            pool(lambda e, t_=t_, base0=base0: e.memset(t_[:, :], base0))
            pool(lambda e, t_=t_, cmp_=cmp_, pat=pat, cm=cm, fill=fill: e.affine_select(
                out=t_[:, :], in_=t_[:, :], pattern=[[pat, 64]], compare_op=cmp_, fill=fill,
                base=0, channel_multiplier=cm))
        pool(lambda e: e.memset(self.CR[:, :, :], 0.0))
        pool(lambda e: e.memset(self.OSH[:, :, :], 0.0))
        pool(lambda e: e.memset(self.OCV[:, :, :, :], 0.0))
        self.P.op("pool", lambda e: e.memset(self.Bp[:, :, :], 0.0), (), [self.bBp])
        self.P.op("pool", lambda e: e.memset(self.Sp[:, :, :], 0.0), (), [self.bSp])
        for j in range(4):
            self.ts(self.c2[:, j:j + 1], self.col("ka%d" % j), -1.0, ALU.mult, [self.bcols], [self.bc2], s2=1.0, op1=ALU.add)
        self.act(self.c2[0:4, 4:5], self.col("alog", 4), AF.Exp, [self.bcols], [self.bc2])
        self.ts(self.c2[0:4, 5:6], self.c2[0:4, 4:5], -1.0, ALU.mult, [self.bc2], [self.bc2])

    def run_pass(self, p):
        ps_ = PASSES[p]
        self.cur = ps_
        self.pidx = p
        T = ps_["T"]
        col0 = ps_["col0"]
        d = self.d
        self.P.op("pool", lambda e: e.memset(self.MR[:, :], 1.0), (), [self.bMR])
        if ps_["nsamp"]:
            self.P.op("pool", lambda e: e.memset(self.MR[:, 0:17], 0.0), (), [self.bMR])
            self.P.op("pool", lambda e: e.memset(self.MR[:, 32:544:64], 0.0), (), [self.bMR])
        else:
            self.P.op("pool", lambda e: e.memset(self.MR[:, 0:512:64], 0.0), (), [self.bMR])
        src = d["xT"].rearrange("k p t -> p k t")[:, :, col0:col0 + T]
        self.dma(self.h[:, :, 0:T], src, [], list(self.bh))
        self.norm("gf1")
        if self.stop <= 1:
            return self.dbg_dump()
        self.ffn("wgu1", "wd1")
        if self.stop <= 2:
            return self.dbg_dump()
        self.norm("gmx")
        self.project_rwkv()
        if self.stop <= 3:
            return self.dbg_dump()
        if self.do_mix:
            self.rwkv_prep()
            if self.stop <= 4:
                return self.dbg_dump()
            self.rwkv_chunks()
            if self.stop <= 5:
                return self.dbg_dump()
        self.project_gdn()
        if self.stop <= 6:
            return self.dbg_dump()
        if self.do_mix:
            self.gdn_prep()
            if self.stop <= 7:
                return self.dbg_dump()
            self.gdn_chunks()
            if self.stop <= 8:
                return self.dbg_dump()
        if self.do_tail:
            self.down(lambda c: (self.mixT[:, c, :], self.bmx[c]), 8, "wout", 1.0)
            self.norm("gf2")
            self.ffn("wgu2", "wd2")
            self.final_norm_store()

    def dbg_dump(self):
        T = self.cur["T"]
        col0 = self.cur["col0"]
        for kc in range(8):
            t = self.dma(self.d["yT"][kc, :, col0:col0 + T], self.h[:, kc, 0:T], [self.bh[kc]], [])
            self.out_tokens.append(t)

    def rstd_tile(self):
        T = self.cur["T"]
        subs = self.cur["subs"]
        for si, (s0, n) in enumerate(subs):
            bank, bb = self.ps[6 + si], self.bps[6 + si]
            for kc in range(8):
                ti = 18 + (kc % 2)
                tmp, tb = self.PJb[:, ti * 2 * TM:ti * 2 * TM + TM], self.bpj[ti]
                self.act(tmp[:, 0:n], self.h[:, kc, s0:s0 + n], AF.Square, [self.bh[kc]], [tb])
                self.mm(bank[:, 0:n], self.onesb[:, :], tmp[:, 0:n], [tb, self.bconst], [bb], start=(kc == 0), stop=(kc == 7))
            r1, b1 = self.PJ[:, 17, :], self.bpj[17]
            self.act(r1[:, s0:s0 + n], bank[:, 0:n], AF.Ln, [bb, self.bc3], [b1], bias=self.epsn[:, 0:1], scale=1.0 / 1024.0)
        r2, b2 = self.PJ[:, 16, :], self.bpj[16]
        self.act(r2[:, 0:T], r1[:, 0:T], AF.Exp, [b1], [b2], scale=-0.5)
        return r2, b2

    def norm(self, gname):
        T = self.cur["T"]
        if not hasattr(self, "epsn"):
            self.epsn = self.P.sbuf("epsn", [128, 4])
            self.bc3 = self.P.buf("epsn")
            self.memset(self.epsn[:, 0:1], 1e-6, [self.bc3])
            self.memset(self.epsn[:, 1:2], 1e-12, [self.bc3])
            self.memset(self.epsn[:, 2:3], 64e-5, [self.bc3])
            self.memset(self.epsn[:, 3:4], 1.0, [self.bc3])
        r2, b2 = self.rstd_tile()
        for kc in range(8):
            self.stt(self.xn[:, kc, 0:T], self.h[:, kc, 0:T], self.col("%s%d" % (gname, kc)), r2[:, 0:T],
                     ALU.mult, ALU.mult, [self.bh[kc], b2, self.bcols], [self.bxn[kc]])

    def final_norm_store(self):
        T = self.cur["T"]
        col0 = self.cur["col0"]
        r2, b2 = self.rstd_tile()
        for kc in range(8):
            o, ob = self.PJ[:, kc, :], self.bpj[kc]
            self.stt(o[:, 0:T], self.h[:, kc, 0:T], self.col("gfn%d" % kc), r2[:, 0:T],
                     ALU.mult, ALU.mult, [self.bh[kc], b2, self.bcols], [ob])
            t = self.dma(self.d["yT"][kc, :, col0:col0 + T], o[:, 0:T], [ob], [])
            self.out_tokens.append(t)

    def ffn(self, wgu, wd):
        T = self.cur["T"]
        subs = self.cur["subs"]
        cnt = 0
        for c in range(22):
            wt, wb = self.get_w(wgu)
            v = wt[:, :].rearrange("p (k c) -> p k c", k=8)
            a_ap = self.PJb[:, c * TM:c * TM + T]
            a_buf = self.bpj[c // 2]
            for si, (s0, n) in enumerate(subs):
                st_ = cnt % 2
                cnt += 1
                gb, gbb = self.ps[2 * st_], self.bps[2 * st_]
                ub, ubb = self.ps[2 * st_ + 1], self.bps[2 * st_ + 1]
                for kc in range(8):
                    self.mm(gb[:, 0:n], v[:, kc, 0:128], self.xn[:, kc, s0:s0 + n],
                            [wb, self.bxn[kc]], [gbb], start=(kc == 0), stop=(kc == 7))
                for kc in range(8):
                    self.mm(ub[:, 0:n], v[:, kc, 128:256], self.xn[:, kc, s0:s0 + n],
                            [wb, self.bxn[kc]], [ubb], start=(kc == 0), stop=(kc == 7))
                sg, sgb = self.PJ[:, 16 + st_, :], self.bpj[16 + st_]
                self.act(sg[:, 0:n], gb[:, 0:n], AF.Silu, [gbb], [sgb])
                self.tt(a_ap[:, s0:s0 + n], sg[:, 0:n], ub[:, 0:n], ALU.mult, [sgb, ubb], [a_buf])
        self.down(lambda c: (self.PJb[:, c * TM:c * TM + TM], self.bpj[c // 2]), 22, wd, 0.5)

    def down(self, src, n_c, wname, scale):
        subs = self.cur["subs"]
        ns = len(subs)
        for half in range(2):
            for g in range(n_c // 2):
                wt, wb = self.get_w(wname)
                v = wt[:, 0:1024].rearrange("p (c n) -> p c n", c=2)
                for cc in range(2):
                    c = 2 * g + cc
                    s_ap, s_buf = src(c)
                    for jj in range(4):
                        for si, (s0, n) in enumerate(subs):
                            b = jj * ns + si + (4 * half if ns == 1 else 0)
                            self.mm(self.ps[b][:, 0:n], v[:, cc, jj * 128:(jj + 1) * 128], s_ap[:, s0:s0 + n],
                                    [wb, s_buf], [self.bps[b]], start=(c == 0), stop=(c == n_c - 1))
            for jj in range(4):
                j = half * 4 + jj
                for si, (s0, n) in enumerate(subs):
                    b = jj * ns + si + (4 * half if ns == 1 else 0)
                    self.stt(self.h[:, j, s0:s0 + n], self.ps[b][:, 0:n], scale, self.h[:, j, s0:s0 + n],
                             ALU.mult, ALU.add, [self.bps[b], self.bh[j]], [self.bh[j]])

    def project(self, g0, g1, post):
        subs = self.cur["subs"]
        cnt = 0
        for g in range(g0, g1):
            wt, wb = self.get_w("win")
            v = wt[:, :].rearrange("p (k c) -> p k c", k=8)
            for cc in range(2):
                oc = 2 * g + cc
                if OC[oc] is None:
                    continue
                M = OC[oc][1]
                banks = []
                for si, (s0, n) in enumerate(subs):
                    b = 4 + (cnt % 4)
                    cnt += 1
                    for kc in range(8):
                        self.mm(self.ps[b][0:M, 0:n], v[:, kc, cc * 128:cc * 128 + M], self.xn[:, kc, s0:s0 + n],
                                [wb, self.bxn[kc]], [self.bps[b]], start=(kc == 0), stop=(kc == 7))
                    banks.append(b)
                post(oc, M, banks)

    def evac_raw(self, oc, M, banks):
        subs = self.cur["subs"]
        ns_ = self.cur["nsamp"]
        k = oc % 2
        raw, rb = self.RAW[k], self.bRAW[k]
        rs, rsb = self.RS[k], self.bRS[k]
        for (s0, n), b in zip(subs, banks):
            a = max(s0, ns_)
            if a < s0 + n:
                self.cp(raw[0:M, 3 + a - ns_:3 + s0 + n - ns_], self.ps[b][0:M, a - s0:n], [self.bps[b]], [rb], eng="act")
            if s0 < ns_:
                self.cp(rs[0:M, s0:ns_], self.ps[b][0:M, 0:ns_ - s0], [self.bps[b]], [rsb], eng="act")
        return raw, rb, rs, rsb

    def project_rwkv(self):
        T = self.cur["T"]
        ns_ = self.cur["nsamp"]
        Tp = T - ns_

        def post(oc, M, banks):
            raw, rb, rs, rsb = self.evac_raw(oc, M, banks)
            if oc < 12:
                dst, db = self.PJ[:, oc, :], self.bpj[oc]
            else:
                dst, db = self.SM[:, oc - 12, :], self.bsm[oc - 12]
            mu = self.col("mu%d" % oc, M)
            self.cp(raw[0:M, 0:3], self.CR[0:M, oc, :], [self.bCR], [rb], eng="pool")
            tmp, tb = self.PJ[:, 12 + (oc % 2), :], self.bpj[12 + (oc % 2)]
            self.tt(tmp[0:M, 0:Tp], raw[0:M, 2:2 + Tp], raw[0:M, 3:3 + Tp], ALU.subtract, [rb], [tb])
            self.stt(dst[0:M, ns_:T], tmp[0:M, 0:Tp], mu, raw[0:M, 3:3 + Tp], ALU.mult, ALU.add, [tb, rb, self.bcols], [db])
            if ns_:
                t2, t2b = self.PJ[:, 14 + (oc % 2), :], self.bpj[14 + (oc % 2)]
                self.tt(t2[0:M, 0:16], self.SSH[0:M, oc, :], rs[0:M, :], ALU.subtract, [self.bsst, rsb], [t2b], eng="pool")
                self.stt(dst[0:M, 0:16], t2[0:M, 0:16], mu, rs[0:M, :], ALU.mult, ALU.add, [t2b, rsb, self.bcols], [db])
                self.cp(self.OSH[0:M, oc, 0:16], rs[0:M, :], [rsb], [self.bOSH], eng="pool")
            self.cp(self.CR[0:M, oc, :], raw[0:M, Tp:Tp + 3], [rb], [self.bCR], eng="pool")

        self.project(0, 8, post)

    def project_gdn(self):
        T = self.cur["T"]
        ns_ = self.cur["nsamp"]
        Tp = T - ns_
        subs = self.cur["subs"]

        def post(oc, M, banks):
            if oc >= 32:
                dst, db = self.SM[:, oc - 32, :], self.bsm[oc - 32]
                for (s0, n), b in zip(subs, banks):
                    self.cp(dst[0:4, s0:s0 + n], self.ps[b][0:4, 0:n], [self.bps[b]], [db], eng="act")
                return
            if oc >= 28:
                dst, db = self.PJ[:, 12 + (oc - 28), :], self.bpj[12 + (oc - 28)]
                for (s0, n), b in zip(subs, banks):
                    self.act(dst[:, s0:s0 + n], self.ps[b][:, 0:n], AF.Silu, [self.bps[b]], [db])
                return
            i = oc - 16
            raw, rb, rs, rsb = self.evac_raw(oc, M, banks)
            dst, db = self.PJ[:, i, :], self.bpj[i]
            self.cp(raw[:, 0:3], self.CR[:, 15 + i, :], [self.bCR], [rb], eng="pool")
            acc, ab = self.PJ[:, 16 + (i % 2), :], self.bpj[16 + (i % 2)]
            self.ts(acc[:, 0:Tp], raw[:, 0:Tp], self.col("cw%d_0" % i), ALU.mult, [rb, self.bcols], [ab])
            for t in range(1, 4):
                self.stt(acc[:, 0:Tp], raw[:, t:t + Tp], self.col("cw%d_%d" % (i, t)), acc[:, 0:Tp], ALU.mult, ALU.add,
                         [rb, ab, self.bcols], [ab])
            self.act(dst[:, ns_:T], acc[:, 0:Tp], AF.Silu, [ab], [db])
            if ns_:
                a2, a2b = self.PJ[:, 18 + (i % 2), :], self.bpj[18 + (i % 2)]
                self.ts(a2[:, 0:16], self.SCV[:, i, 0, :], self.col("cw%d_0" % i), ALU.mult, [self.bsst, self.bcols], [a2b])
                for t in range(1, 3):
                    self.stt(a2[:, 0:16], self.SCV[:, i, t, :], self.col("cw%d_%d" % (i, t)), a2[:, 0:16], ALU.mult, ALU.add,
                             [self.bsst, a2b, self.bcols], [a2b])
                self.stt(a2[:, 0:16], rs[:, :], self.col("cw%d_3" % i), a2[:, 0:16], ALU.mult, ALU.add,
                         [rsb, a2b, self.bcols], [a2b])
                self.act(dst[:, 0:16], a2[:, 0:16], AF.Silu, [a2b], [db])
                self.cp(self.OCV[:, i, 0:2, 0:16], self.SCV[:, i, 1:3, :], [self.bsst], [self.bOCV], eng="pool")
                self.cp(self.OCV[:, i, 2, 0:16], rs[:, :], [rsb], [self.bOCV], eng="pool")
            self.cp(self.CR[:, 15 + i, :], raw[:, Tp:Tp + 3], [rb], [self.bCR], eng="pool")

        self.project(8, 17, post)

    def rwkv_prep(self):
        T = self.cur["T"]
        subs = self.cur["subs"]
        chunks = self.cur["chunks"]
        XWD, bwd = self.SM[:, 0, :], self.bsm[0]
        XAD, bad = self.SM[:, 1, :], self.bsm[1]
        XGD, bgd = self.SM[:, 2, :], self.bsm[2]
        self.act(XWD[0:32, 0:T], XWD[0:32, 0:T], AF.Tanh, [bwd], [bwd])
        self.act(XGD[0:96, 0:T], XGD[0:96, 0:T], AF.Sigmoid, [bgd], [bgd])
        for j in range(4):
            XR, bxr = self.PJ[:, j, :], self.bpj[j]
            XK, bxk = self.PJ[:, 4 + j, :], self.bpj[4 + j]
            KAP, bkap = self.PJ[:, 12 + j, :], self.bpj[12 + j]
            AH, bah = self.PJ[:, 16 + j, :], self.bpj[16 + j]
            sig, bsig = self.sc(0)
            lam, blam = self.sc(1)
            eP, beP = self.sc(2)
            eN, beN = self.sc(3)
            ePx, bePx = self.sc(4)
            A, bA = self.sc(5)
            kr, bkr = self.sc(6)
            t8, bt8 = self.sc(7)
            jc = slice(j * 128, (j + 1) * 128)
            for si, (s0, n) in enumerate(subs):
                b = 4 + si
                self.mm(self.ps[b][:, 0:n], self.wdu[0:32, jc], XWD[0:32, s0:s0 + n], [self.bwsm, bwd], [self.bps[b]])
                self.act(sig[:, s0:s0 + n], self.ps[b][:, 0:n], AF.Sigmoid, [self.bps[b], self.bcols], [bsig], bias=self.col("w0%d" % j))
                b2 = 6 + si
                self.mm(self.ps[b2][:, 0:n], self.wau[0:32, jc], XAD[0:32, s0:s0 + n], [self.bwsm, bad], [self.bps[b2]])
                self.act(A[:, s0:s0 + n], self.ps[b2][:, 0:n], AF.Sigmoid, [self.bps[b2], self.bcols], [bA], bias=self.col("a0%d" % j))
            self.scan(lam[:, 0:T], self.MR[:, 0:T], sig[:, 0:T], [self.bMR, bsig], [blam])
            self.act(eP[:, 0:T], lam[:, 0:T], AF.Exp, [blam], [beP], scale=-EXPM05)
            self.act(eN[:, 0:T], lam[:, 0:T], AF.Exp, [blam], [beN], scale=EXPM05)
            self.tt(ePx[:, 0:T], lam[:, 0:T], sig[:, 0:T], ALU.subtract, [blam, bsig], [bePx])
            self.act(ePx[:, 0:T], ePx[:, 0:T], AF.Exp, [bePx], [bePx], scale=-EXPM05)
            if self.cur["nsamp"]:
                self.cp(self.PCs[:, j, 0:16], eP[:, 0:16], [beP], [self.bPCs])
                self.cp(self.PCs[:, j, 16:17], eP[:, 31:32], [beP], [self.bPCs])
                self.cp(self.PCs[:, j, 17:25], eP[:, 95:544:64], [beP], [self.bPCs])
            else:
                self.cp(self.PCs[:, j, 0:8], eP[:, 63:512:64], [beP], [self.bPCs])
            self.ts(kr[:, 0:T], XK[:, 0:T], self.col("kk%d" % j), ALU.mult, [bxk, self.bcols], [bkr])
            self.act(t8[:, 0:T], kr[:, 0:T], AF.Square, [bkr], [bt8])
            for si, (s0, n) in enumerate(subs):
                b = 4 + si
                self.mm(self.ps[b][:, 0:n], self.blk[:, :], t8[:, s0:s0 + n], [self.bconst, bt8], [self.bps[b]])
                self.act(sig[:, s0:s0 + n], self.ps[b][:, 0:n], AF.Ln, [self.bps[b], self.bc3], [bsig], bias=self.epsn[:, 1:2])
            self.act(t8[:, 0:T], sig[:, 0:T], AF.Exp, [bsig], [bt8], scale=-0.5)
            self.tt(kr[:, 0:T], kr[:, 0:T], t8[:, 0:T], ALU.mult, [bkr, bt8], [bkr])
            self.tt(AH[:, 0:T], kr[:, 0:T], A[:, 0:T], ALU.mult, [bkr, bA], [bah])
            self.tt(AH[:, 0:T], AH[:, 0:T], eN[:, 0:T], ALU.mult, [bah, beN], [bah])
            self.tt(KAP[:, 0:T], kr[:, 0:T], ePx[:, 0:T], ALU.mult, [bkr, bePx], [bkap])
            self.ts(t8[:, 0:T], A[:, 0:T], self.col("ka%d" % j), ALU.mult, [bA, self.bcols, self.bc2], [bt8],
                    s2=self.c2[:, j:j + 1], op1=ALU.add)
            self.tt(XK[:, 0:T], XK[:, 0:T], t8[:, 0:T], ALU.mult, [bxk, bt8], [bxk])
            self.tt(XK[:, 0:T], XK[:, 0:T], eN[:, 0:T], ALU.mult, [bxk, beN], [bxk])
            self.tt(XR[:, 0:T], XR[:, 0:T], eP[:, 0:T], ALU.mult, [bxr, beP], [bxr])

    def inverse(self, U, bU, L, bL, C, H, scr, banks):
        (Ub, bUb), (Lb, bLb), (IL, bIL), (Xa, bXa), (Xb, bXb) = scr
        pL, pU, pX = banks
        idb = self.ident[0:C, 0:C].unsqueeze(1).broadcast_to([C, H, C])
        self.stt(R(Xa[0:C, :, 0:C]), U[0:C, :, 0:C], -1.0, idb, ALU.mult, ALU.add, [self.bconst, bU], [bXa])
        nlev = {64: 5, 32: 4, 16: 3, 8: 2, 4: 1, 2: 0}[C]
        Uc, bUc, Lc, bLc = U, bU, L, bL
        Un, bUn, Ln, bLn = Ub, bUb, Lb, bLb
        X, bX, Xn, bXn = Xa, bXa, Xb, bXb
        for lev in range(nlev):
            last = lev == nlev - 1
            PL = self.ps[pL][0:C, 0:H * 64].rearrange("p (h c) -> p h c", h=H)
            PU = self.ps[pU][0:C, 0:H * 64].rearrange("p (h c) -> p h c", h=H)
            PX = self.ps[pX][0:C, 0:H * 64].rearrange("p (h c) -> p h c", h=H)
            for hh in range(H):
                self.mm(PL[:, hh, 0:C], Uc[0:C, hh, 0:C], Lc[0:C, hh, 0:C], [bUc, bLc], [self.bps[pL]], r=True)
            if not last:
                for hh in range(H):
                    self.mm(PU[:, hh, 0:C], Lc[0:C, hh, 0:C], Uc[0:C, hh, 0:C], [bUc, bLc], [self.bps[pU]], r=True)
            self.tt(R(IL[0:C, :, 0:C]), PL[:, :, 0:C], idb, ALU.add, [self.bps[pL], self.bconst], [bIL])
            if not last:
                self.cp(R(Ln[0:C, :, 0:C]), PL[:, :, 0:C], [self.bps[pL]], [bLn], eng="act")
                self.cp(R(Un[0:C, :, 0:C]), PU[:, :, 0:C], [self.bps[pU]], [bUn], eng="act")
            for hh in range(H):
                self.mm(PX[:, hh, 0:C], IL[0:C, hh, 0:C], X[0:C, hh, 0:C], [bIL, bX], [self.bps[pX]], r=True)
            self.cp(R(Xn[0:C, :, 0:C]), PX[:, :, 0:C], [self.bps[pX]], [bXn], eng="dve")
            X, bX, Xn, bXn = Xn, bXn, X, bX
            if not last:
                Uc, bUc, Un, bUn = Un, bUn, Uc, bUc
                Lc, bLc, Ln, bLn = Ln, bLn, Lc, bLc
        return X, bX

    def sc3(self, i, H):
        ap, b = self.sc(i)
        return ap[:, 0:H * 64].rearrange("p (h c) -> p h c", h=H), b

    def rwkv_chunks(self):
        d = self.d
        H = 8
        chunks = self.cur["chunks"]
        bR = [self.bpj[j] for j in range(4)]
        bK = [self.bpj[4 + j] for j in range(4)]
        bV = [self.bpj[8 + j] for j in range(4)]
        bKA = [self.bpj[12 + j] for j in range(4)]
        bAH = [self.bpj[16 + j] for j in range(4)]
        SG, bsg = self.SM[:, 2, :], self.bsm[2]
        lnw = self.bcst[:, 0:512]
        lnb = self.bcst[:, 512:1024]
        ctx = [dict(), dict()]

        def genA(ci):
            c0, C, sidx = chunks[ci]
            par = ci % 2
            cx = ctx[par]
            cx.clear()
            rr = C > 1

            def P3(b_, C_):
                return self.ps[b_][0:C_, 0:512].rearrange("p (h c) -> p h c", h=8)

            hm = self.blk2[:, :].unsqueeze(1).unsqueeze(3).broadcast_to([128, 4, 2, C])
            bdv = []
            for t, si_, bsrc in ((0, 15 if par == 0 else 20, bR), (1, 16, bK), (3, 17 if par == 0 else 21, bKA), (4, 18, bAH)):
                ap, bb = self.sr(si_)
                v = ap[:, 0:8 * C].rearrange("p (j e c) -> p j e c", j=4, e=2)
                srcv = self.PJ[:, t * 4:t * 4 + 4, c0:c0 + C].unsqueeze(2).broadcast_to([128, 4, 2, C])
                self.tt(R(v), srcv, hm, ALU.mult, list(bsrc) + [self.bconst], [bb])
                bdv.append((v, bb))
            (Rbd, bRbd), (Kbd, bKbd), (KAbd, bKAbd), (Abd, bAbd) = bdv
            yield
            U3, bU = self.sr3(5, 8)
            L3, bL = self.sr3(6, 8)
            Kk3, bKk = self.sr3(7 if par == 0 else 22, 8)
            Ma3, bMa = self.sr3(8 if par == 0 else 23, 8)
            Mk3, bMk = self.sr3(9 if par == 0 else 24, 8)
            for hh in range(H):
                j, e_ = hh // 2, hh % 2
                if C > 1:
                    self.mm(P3(0, C)[:, hh, 0:C], Abd[:, j, e_, :], KAbd[:, j, e_, :], [bAbd, bKAbd], [self.bps[0]], r=rr)
                    self.mm(P3(1, C)[:, hh, 0:C], KAbd[:, j, e_, :], Abd[:, j, e_, :], [bKAbd, bAbd], [self.bps[1]], r=rr)
                    self.mm(P3(2, C)[:, hh, 0:C], Kbd[:, j, e_, :], KAbd[:, j, e_, :], [bKbd, bKAbd], [self.bps[2]], r=rr)
                self.mm(P3(3, C)[:, hh, 0:C], Abd[:, j, e_, :], Rbd[:, j, e_, :], [bAbd, bRbd], [self.bps[3]], r=rr)
                self.mm(P3(4, C)[:, hh, 0:C], Kbd[:, j, e_, :], Rbd[:, j, e_, :], [bKbd, bRbd], [self.bps[4]], r=rr)
            yield

            def mask(m):
                return m[0:C, 0:C].unsqueeze(1).broadcast_to([C, 8, C])

            if C > 1:
                self.tt(R(U3[0:C, :, 0:C]), P3(0, C)[:, :, 0:C], mask(self.msu), ALU.mult, [self.bps[0], self.bconst], [bU])
                self.tt(R(L3[0:C, :, 0:C]), P3(1, C)[:, :, 0:C], mask(self.msl), ALU.mult, [self.bps[1], self.bconst], [bL])
                self.tt(R(Kk3[0:C, :, 0:C]), P3(2, C)[:, :, 0:C], mask(self.msu), ALU.mult, [self.bps[2], self.bconst], [bKk])
            self.tt(R(Ma3[0:C, :, 0:C]), P3(3, C)[:, :, 0:C], mask(self.miu), ALU.mult, [self.bps[3], self.bconst], [bMa])
            self.tt(R(Mk3[0:C, :, 0:C]), P3(4, C)[:, :, 0:C], mask(self.miu), ALU.mult, [self.bps[4], self.bconst], [bMk])
            yield
            if C > 1:
                xfin = self.sr3(14 if par == 0 else 19, 8)
                xtmp = self.sr3(13, 8)
                nlev = {64: 5, 32: 4, 16: 3, 8: 2, 4: 1, 2: 0}[C]
                xa, xb = (xtmp, xfin) if nlev % 2 == 1 else (xfin, xtmp)
                scr = [self.sr3(10, 8), self.sr3(11, 8), self.sr3(12, 8), xa, xb]
                res = {}
                yield from self.inverse_gen(U3, bU, L3, bL, C, 8, scr, (0, 1, 2), res)
                cx["X"] = res["X"]
            cx.update(Rbd=(Rbd, bRbd), KAbd=(KAbd, bKAbd), Kk=(Kk3, bKk), Ma=(Ma3, bMa), Mk=(Mk3, bMk))
            yield

        def genB(ci, slot=0):
            c0, C, sidx = chunks[ci]
            b0, b1, b2 = (5, 6, 7) if slot == 0 else (0, 1, 2)
            tA, tK, tV, tN = (0, 1, 2, 4) if slot == 0 else (5, 6, 10, 11)
            so = 0 if slot == 0 else 3
            par = ci % 2
            cx = ctx[par]
            rr = C > 1
            samp = sidx < 16
            Rbd, bRbd = cx["Rbd"]
            KAbd, bKAbd = cx["KAbd"]
            Kk3, bKk = cx["Kk"]
            Ma3, bMa = cx["Ma"]
            Mk3, bMk = cx["Mk"]
            if samp:
                Bt, bB = self.Bs[sidx % 2], self.bBs[sidx % 2]
                self.dma(Bt[:, :, :].rearrange("p j v -> p (j v)"), d["srw"][sidx], [], [bB])
            else:
                Bt, bB = self.Bp, self.bBp
            At, bAt = self.sr(tA)
            Kt, bKt = self.sr(tK)
            Vt, bVt = self.sr(tV)
            if rr:
                Br_t, bBr = self.sr(3)
                Br = Br_t[:, 0:256].rearrange("p (j v) -> p j v", j=4)
                self.cp(R(Br), Bt[:, :, :], [bB], [bBr], eng="dve")
            else:
                Br, bBr = Bt, bB
            trs = ((4, At, bAt, b0, bAH), (1, Kt, bKt, b1, bK), (2, Vt, bVt, b2, bV))
            for (t, dst, db, bank, bsrc) in trs:
                for j in range(4):
                    self.tr(self.ps[bank][0:C, j * 128:(j + 1) * 128], self.PJ[:, t * 4 + j, c0:c0 + C], self.ident[:, :],
                            [bsrc[j], self.bconst], [self.bps[bank]])
            yield
            for (t, dst, db, bank, bsrc) in trs:
                self.cp(R(dst[0:C, 0:512]), self.ps[bank][0:C, 0:512], [self.bps[bank]], [db], eng="act")
            yield
            for hh in range(H):
                j = hh // 2
                hc = slice(hh * 64, (hh + 1) * 64)
                self.mm(self.ps[b0][0:C, hc], KAbd[:, j, hh % 2, :], Br[:, j, :], [bKAbd, bBr], [self.bps[b0]],
                        start=(hh == 0), stop=True, r=rr, g=True)
            if C > 1:
                for hh in range(H):
                    hc = slice(hh * 64, (hh + 1) * 64)
                    self.mm(self.ps[b0][0:C, hc], Kk3[0:C, hh, 0:C], Vt[0:C, hc], [bKk, bVt], [self.bps[b0]],
                            start=False, stop=True, r=rr, g=True)
            yield
            nY, bnY = self.sr(tN)
            if C > 1:
                X3, bX = cx["X"]
                Rs, bRs = self.sr(tN)
                self.cp(R(Rs[0:C, 0:512]), self.ps[b0][0:C, 0:512], [self.bps[b0]], [bRs], eng="act")
                yield
                for hh in range(H):
                    hc = slice(hh * 64, (hh + 1) * 64)
                    self.mm(self.ps[b1][0:C, hc], X3[0:C, hh, 0:C], Rs[0:C, hc], [bX, bRs], [self.bps[b1]], r=rr)
                yield
                self.act(R(nY[0:C, 0:512]), self.ps[b1][0:C, 0:512], AF.Copy, [self.bps[b1]], [bnY], scale=-1.0)
            else:
                self.act(R(nY[0:C, 0:512]), self.ps[b0][0:C, 0:512], AF.Copy, [self.bps[b0]], [bnY], scale=-1.0)
            yield
            def sout(hh):
                j, p0 = hh // 2, (hh % 2) * 64
                return self.ps[b0][p0:p0 + 64, j * 64:(j + 1) * 64]

            for hh in range(H):
                j, p0 = hh // 2, (hh % 2) * 64
                self.mm(sout(hh), self.ident[:, p0:p0 + 64], Bt[:, j, :], [self.bconst, bB], [self.bps[b0]],
                        start=(hh < 2), stop=True, g=True)
            for hh in range(H):
                hc = slice(hh * 64, (hh + 1) * 64)
                self.mm(sout(hh), At[0:C, hc], nY[0:C, hc], [bAt, bnY], [self.bps[b0]], start=False, stop=True, g=True)
            for hh in range(H):
                hc = slice(hh * 64, (hh + 1) * 64)
                self.mm(sout(hh), Kt[0:C, hc], Vt[0:C, hc], [bKt, bVt], [self.bps[b0]], start=False, stop=True, g=True)
            for hh in range(H):
                j = hh // 2
                hc = slice(hh * 64, (hh + 1) * 64)
                self.mm(self.ps[b2][0:C, hc], Rbd[:, j, hh % 2, :], Br[:, j, :], [bRbd, bBr], [self.bps[b2]],
                        start=(hh == 0), stop=True, r=rr, g=True)
            for hh in range(H):
                hc = slice(hh * 64, (hh + 1) * 64)
                self.mm(self.ps[b2][0:C, hc], Ma3[0:C, hh, 0:C], nY[0:C, hc], [bMa, bnY], [self.bps[b2]], start=False, stop=True, r=rr, g=True)
            for hh in range(H):
                hc = slice(hh * 64, (hh + 1) * 64)
                self.mm(self.ps[b2][0:C, hc], Mk3[0:C, hh, 0:C], Vt[0:C, hc], [bMk, bVt], [self.bps[b2]], start=False, stop=True, r=rr, g=True)
            yield
            pcs = self.PCs[:, :, ci:ci + 1].broadcast_to([128, 4, 64])
            self.tt(Bt[:, :, :], self.ps[b0][:, 0:256].rearrange("p (j v) -> p j v", j=4), pcs, ALU.mult,
                    [self.bps[b0], self.bPCs], [bB])
            if samp:
                t = self.dma(d["orw"][sidx], Bt[:, :, :].rearrange("p j v -> p (j v)"), [bB], [])
                self.out_tokens.append(t)
            RKc, bRK = self.FM[:, slot, :], self.bfm[slot]
            for j in range(4):
                self.stt(RKc[:, j * 64:j * 64 + C], self.PJ[:, j, c0:c0 + C], self.col("rk%d" % j), self.PJ[:, 4 + j, c0:c0 + C],
                         ALU.mult, ALU.mult, [bR[j], bK[j], self.bcols], [bRK])
            yield
            for j in range(4):
                self.mm(self.ps[b1][0:C, 2 * j:2 * j + 2], RKc[:, j * 64:j * 64 + C], self.blk2[:, :], [bRK, self.bconst], [self.bps[b1]])
            yield
            rks, brks = self.sm8[:, so + 0, :], self.bsm8[so + 0]
            self.cp(rks[0:C, :], self.ps[b1][0:C, 0:8], [self.bps[b1]], [brks], eng="act")
            O3 = self.ps[b2][0:C, 0:512].rearrange("p (h v) -> p h v", h=8)
            s1, bs1 = self.sm8[:, so + 1, :], self.bsm8[so + 1]
            self.red(s1[0:C, :], O3, [self.bps[b2]], [bs1])
            self.ts(s1[0:C, :], s1[0:C, :], -1.0 / 64.0, ALU.mult, [bs1], [bs1])
            cen, bcen = self.sc(so + 0)
            cen3 = cen[0:C, 0:512].rearrange("p (h v) -> p h v", h=8)
            self.tt(cen3, O3, s1[0:C, :].unsqueeze(2).broadcast_to([C, 8, 64]), ALU.add, [self.bps[b2], bs1], [bcen])
            yield
            self.mm(self.ps[b1][0:C, 0:512], SG[0:96, c0:c0 + C], self.wgu[0:96, :], [bsg, self.bwsm], [self.bps[b1]])
            sq, bsq = self.sc(so + 1)
            self.act(sq[0:C, 0:512], cen[0:C, 0:512], AF.Square, [bcen], [bsq])
            yield
            s2, bs2 = self.sm8[:, so + 2, :], self.bsm8[so + 2]
            self.red(s2[0:C, :], sq[0:C, 0:512].rearrange("p (h v) -> p h v", h=8), [bsq], [bs2])
            self.act(s2[0:C, :], s2[0:C, :], AF.Sqrt, [bs2, self.bc3], [bs2], bias=self.epsn[0:C, 2:3], scale=1.0 / 64.0)
            self.recip(s2[0:C, :], s2[0:C, :], [bs2], [bs2])
            yield
            self.tt(cen3, cen3, s2[0:C, :].unsqueeze(2).broadcast_to([C, 8, 64]), ALU.mult, [bcen, bs2], [bcen])
            self.tt(cen[0:C, 0:512], cen[0:C, 0:512], lnw[0:C, :], ALU.mult, [bcen, self.bbc], [bcen], eng="pool")
            self.tt(cen[0:C, 0:512], cen[0:C, 0:512], lnb[0:C, :], ALU.add, [bcen, self.bbc], [bcen], eng="pool")
            bon, bbon = self.sc(so + 2)
            self.tt(bon[0:C, 0:512].rearrange("p (h v) -> p h v", h=8), Vt[0:C, 0:512].rearrange("p (h v) -> p h v", h=8),
                    rks[0:C, :].unsqueeze(2).broadcast_to([C, 8, 64]), ALU.mult, [bVt, brks], [bbon])
            yield
            self.tt(cen[0:C, 0:512], cen[0:C, 0:512], bon[0:C, 0:512], ALU.add, [bcen, bbon], [bcen], eng="pool")
            self.tt(sq[0:C, 0:512], cen[0:C, 0:512], self.ps[b1][0:C, 0:512], ALU.mult, [bcen, self.bps[b1]], [bsq])
            yield
            for j in range(4):
                self.tr(self.ps[b0][:, 256 + j * 64:256 + j * 64 + C], sq[0:C, j * 128:(j + 1) * 128], self.ident[0:C, 0:C],
                        [bsq, self.bconst], [self.bps[b0]])
            yield
            for j in range(4):
                self.cp(self.mixT[:, j, c0:c0 + C], self.ps[b0][:, 256 + j * 64:256 + j * 64 + C], [self.bps[b0]], [self.bmx[j]], eng="act")
            yield

        self.run_chunks(genA, genB, chunks)

    def gdn_prep(self):
        T = self.cur["T"]
        subs = self.cur["subs"]
        AR, bar = self.SM[:, 0, :], self.bsm[0]
        BR, bbr = self.SM[:, 1, :], self.bsm[1]
        G, bG = self.SM[:, 2, :], self.bsm[2]
        for i in range(8):
            X, bx = self.PJ[:, i, :], self.bpj[i]
            sq, bsq = self.sc(0 + (i % 2))
            nr, bnr = self.sc(2 + (i % 2))
            self.act(sq[:, 0:T], X[:, 0:T], AF.Square, [bx], [bsq])
            for si, (s0, n) in enumerate(subs):
                b = 4 + (2 * i + si) % 4
                self.mm(self.ps[b][:, 0:n], self.ones[:, :], sq[:, s0:s0 + n], [self.bconst, bsq], [self.bps[b]])
                self.act(nr[:, s0:s0 + n], self.ps[b][:, 0:n], AF.Ln, [self.bps[b], self.bc3], [bnr], bias=self.epsn[:, 1:2])
            self.act(sq[:, 0:T], nr[:, 0:T], AF.Exp, [bnr], [bsq], scale=-0.5)
            if i < 4:
                self.stt(X[:, 0:T], X[:, 0:T], 128.0 ** -0.5, sq[:, 0:T], ALU.mult, ALU.mult, [bx, bsq], [bx])
            else:
                self.tt(X[:, 0:T], X[:, 0:T], sq[:, 0:T], ALU.mult, [bx, bsq], [bx])
        self.act(BR[0:4, 0:T], BR[0:4, 0:T], AF.Sigmoid, [bbr], [bbr])
        self.act(AR[0:4, 0:T], AR[0:4, 0:T], AF.Exp, [bar, self.bcols], [bar], bias=self.col("dtb", 4))
        self.act(AR[0:4, 0:T], AR[0:4, 0:T], AF.Ln, [bar, self.bc3], [bar], bias=self.epsn[0:4, 3:4])
        self.ts(AR[0:4, 0:T], AR[0:4, 0:T], self.c2[0:4, 5:6], ALU.mult, [bar, self.bc2], [bar])
        self.scan(G[0:4, 0:T], self.MR[0:4, 0:T], AR[0:4, 0:T], [self.bMR, bar], [bG])

    def interleave(self, gens):
        alive = [g for g in gens if g is not None]
        while alive:
            for g in list(alive):
                try:
                    next(g)
                except StopIteration:
                    alive.remove(g)

    def run_chunks(self, genA, genB, chunks):
        idx = list(range(len(chunks)))
        if self.cf != "all":
            idx = [i for i in idx if {1: "samp", 16: "meta", 64: "big"}[chunks[i][1]] in self.cf]
        samp = [i for i in idx if chunks[i][2] < 16]
        rest = [i for i in idx if chunks[i][2] >= 16]
        for k in range(0, len(samp), 2):
            pair = samp[k:k + 2]
            for i in pair:
                self.interleave([genA(i)])
            self.interleave([genB(i, slot) for slot, i in enumerate(pair)])
        for k in range(len(rest) + 1):
            ga = genA(rest[k]) if k < len(rest) else None
            gb = genB(rest[k - 1], 0) if k >= 1 else None
            self.interleave([gb, ga])

    def pipeline(self, genA, genB, n):
        for i in range(n + 1):
            ga = genA(i) if i < n else None
            gb = genB(i - 1) if i >= 1 else None
            self.interleave([gb, ga])

    def inverse_gen(self, U, bU, L, bL, C, H, scr, banks, out):
        (Ub, bUb), (Lb, bLb), (IL, bIL), (Xa, bXa), (Xb, bXb) = scr
        pL, pU, pX = banks
        idb = self.ident[0:C, 0:C].unsqueeze(1).broadcast_to([C, H, C])
        self.stt(R(Xa[0:C, :, 0:C]), U[0:C, :, 0:C], -1.0, idb, ALU.mult, ALU.add, [self.bconst, bU], [bXa])
        nlev = {64: 5, 32: 4, 16: 3, 8: 2, 4: 1, 2: 0}[C]
        Uc, bUc, Lc, bLc = U, bU, L, bL
        Un, bUn, Ln, bLn = Ub, bUb, Lb, bLb
        X, bX, Xn, bXn = Xa, bXa, Xb, bXb
        for lev in range(nlev):
            last = lev == nlev - 1
            PL = self.ps[pL][0:C, 0:H * 64].rearrange("p (h c) -> p h c", h=H)
            PU = self.ps[pU][0:C, 0:H * 64].rearrange("p (h c) -> p h c", h=H)
            PX = self.ps[pX][0:C, 0:H * 64].rearrange("p (h c) -> p h c", h=H)
            for hh in range(H):
                self.mm(PL[:, hh, 0:C], Uc[0:C, hh, 0:C], Lc[0:C, hh, 0:C], [bUc, bLc], [self.bps[pL]], r=True)
            if not last:
                for hh in range(H):
                    self.mm(PU[:, hh, 0:C], Lc[0:C, hh, 0:C], Uc[0:C, hh, 0:C], [bUc, bLc], [self.bps[pU]], r=True)
            yield
            self.tt(R(IL[0:C, :, 0:C]), PL[:, :, 0:C], idb, ALU.add, [self.bps[pL], self.bconst], [bIL])
            if not last:
                self.cp(R(Ln[0:C, :, 0:C]), PL[:, :, 0:C], [self.bps[pL]], [bLn], eng="act")
                self.cp(R(Un[0:C, :, 0:C]), PU[:, :, 0:C], [self.bps[pU]], [bUn], eng="act")
            yield
            for hh in range(H):
                self.mm(PX[:, hh, 0:C], IL[0:C, hh, 0:C], X[0:C, hh, 0:C], [bIL, bX], [self.bps[pX]], r=True)
            yield
            self.cp(R(Xn[0:C, :, 0:C]), PX[:, :, 0:C], [self.bps[pX]], [bXn], eng="dve")
            yield
            X, bX, Xn, bXn = Xn, bXn, X, bX
            if not last:
                Uc, bUc, Un, bUn = Un, bUn, Uc, bUc
                Lc, bLc, Ln, bLn = Ln, bLn, Lc, bLc
        out["X"] = (X, bX)

    def gdn_chunks(self):
        d = self.d
        H = 4
        chunks = self.cur["chunks"]
        bQ = [self.bpj[h] for h in range(4)]
        bK = [self.bpj[4 + h] for h in range(4)]
        bV = [self.bpj[8 + h] for h in range(4)]
        bZ = [self.bpj[12 + h] for h in range(4)]
        BR, bbr = self.SM[:, 1, :], self.bsm[1]
        G, bG = self.SM[:, 2, :], self.bsm[2]
        nw = self.bcst[:, 1024:1536]
        ctx = [dict(), dict()]

        def fmt(i):
            return self.FM[:, i, :].rearrange("p (h c) -> p h c", h=4), self.bfm[i]

        def half(i, k):
            ap, b_ = self.sr(i)
            return ap[:, k * 256:(k + 1) * 256].rearrange("p (h c) -> p h c", h=4), b_

        def P3(b, C):
            return self.ps[b][0:C, 0:256].rearrange("p (h c) -> p h c", h=4)

        def genA(ci):
            c0, C, sidx = chunks[ci]
            par = ci % 2
            cx = ctx[par]
            cx.clear()
            rr = C > 1
            PG = self.ps[3][:, 0:256].rearrange("p (h c) -> p h c", h=4)
            PB = self.ps[3][:, 256:512].rearrange("p (h c) -> p h c", h=4)
            for hh in range(H):
                self.mm(PG[:, hh, 0:C], self.sel[0:4, hh, :], G[0:4, c0:c0 + C], [self.bconst, bG], [self.bps[3]])
                self.mm(PB[:, hh, 0:C], self.sel[0:4, hh, :], BR[0:4, c0:c0 + C], [self.bconst, bbr], [self.bps[3]])
            self.mm(self.ps[0][0:C, 0:4], G[0:4, c0:c0 + C], self.ident[0:4, 0:4], [bG, self.bconst], [self.bps[0]])
            self.mm(self.ps[0][0:C, 4:8], BR[0:4, c0:c0 + C], self.ident[0:4, 0:4], [bbr, self.bconst], [self.bps[0]])
            yield
            Gbc, bGbc = fmt(1)
            gam, bgam = fmt(2 if par == 0 else 0)
            bet, bbet = fmt(3)
            self.cp(Gbc[:, :, 0:C], PG[:, :, 0:C], [self.bps[3]], [bGbc], eng="act")
            self.act(gam[:, :, 0:C], PG[:, :, 0:C], AF.Exp, [self.bps[3]], [bgam])
            self.cp(bet[:, :, 0:C], PB[:, :, 0:C], [self.bps[3]], [bbet], eng="dve")
            cl, bcl = self.sm8[:, 3, :], self.bsm8[3]
            self.cp(cl[0:C, :], self.ps[0][0:C, 0:8], [self.bps[0]], [bcl], eng="dve")
            dc, bdc = self.sm8[:, 4, :], self.bsm8[4]
            self.tt(dc[0:C, 0:4], Gbc[0:C, :, C - 1], cl[0:C, 0:4], ALU.subtract, [bGbc, bcl], [bdc])
            self.act(dc[0:C, 0:4], dc[0:C, 0:4], AF.Exp, [bdc], [bdc])
            yield
            Kv = self.PJ[:, 4:8, c0:c0 + C]
            Qv = self.PJ[:, 0:4, c0:c0 + C]
            kb, bkb = half(3 if par == 0 else 16, 0)
            kbg, bkbg = half(3 if par == 0 else 16, 1)
            qg, bqg = half(4 if par == 0 else 17, 0)
            Kc, bKc = half(4 if par == 0 else 17, 1)
            Qc, bQc = half(15, par)
            self.tt(R(kb[:, :, 0:C]), Kv, bet[:, :, 0:C], ALU.mult, bK + [bbet], [bkb])
            self.tt(R(kbg[:, :, 0:C]), kb[:, :, 0:C], gam[:, :, 0:C], ALU.mult, [bkb, bgam], [bkbg])
            self.tt(R(qg[:, :, 0:C]), Qv, gam[:, :, 0:C], ALU.mult, bQ + [bgam], [bqg])
            self.cp(R(Kc[:, :, 0:C]), Kv, bK, [bKc], eng="dve")
            self.cp(R(Qc[:, :, 0:C]), Qv, bQ, [bQc], eng="dve")
            QK3, bQK = self.sr3(7 if par == 0 else 18, 4)
            if C > 1:
                D1, bD1 = self.sc3(5, 4)
                D2, bD2 = self.sc3(6, 4)
                D3, bD3 = self.sc3(7, 4)
                gcol_b = cl[0:C, 0:4].unsqueeze(2).broadcast_to([C, 4, C])
                self.tt(D1[0:C, :, 0:C], Gbc[0:C, :, 0:C], gcol_b, ALU.subtract, [bGbc, bcl], [bD1])
                self.stt(D3[0:C, :, 0:C], Gbc[0:C, :, 0:C], -1.0, gcol_b, ALU.mult, ALU.add, [bGbc, bcl], [bD3])

                def nmask(m):
                    return m[0:C, 0:C].unsqueeze(1).broadcast_to([C, 4, C])

                self.tt(D2[0:C, :, 0:C], D1[0:C, :, 0:C], nmask(self.niu), ALU.add, [bD1, self.bconst], [bD2], eng="pool")
                self.tt(D1[0:C, :, 0:C], D1[0:C, :, 0:C], nmask(self.nsu), ALU.add, [bD1, self.bconst], [bD1], eng="pool")
                self.tt(D3[0:C, :, 0:C], D3[0:C, :, 0:C], nmask(self.nsl), ALU.add, [bD3, self.bconst], [bD3], eng="pool")
                self.act(D1[0:C, :, 0:C], D1[0:C, :, 0:C], AF.Exp, [bD1], [bD1])
                self.act(D2[0:C, :, 0:C], D2[0:C, :, 0:C], AF.Exp, [bD2], [bD2])
                self.act(D3[0:C, :, 0:C], D3[0:C, :, 0:C], AF.Exp, [bD3], [bD3])
            yield
            U3, bU = self.sr3(5, 4)
            L3, bL = self.sr3(6, 4)
            for hh in range(H):
                Kh = Kc[:, hh, 0:C]
                Qh = Qc[:, hh, 0:C]
                if C > 1:
                    self.mm(P3(0, C)[:, hh, 0:C], Kh, kb[:, hh, 0:C], [bKc, bkb], [self.bps[0]], r=rr)
                    self.mm(P3(1, C)[:, hh, 0:C], kb[:, hh, 0:C], Kh, [bKc, bkb], [self.bps[1]], r=rr)
                self.mm(P3(2, C)[:, hh, 0:C], Kh, Qh, [bKc, bQc], [self.bps[2]], r=rr)
            yield
            if C > 1:
                self.tt(R(U3[0:C, :, 0:C]), P3(0, C)[:, :, 0:C], D1[0:C, :, 0:C], ALU.mult, [self.bps[0], bD1], [bU])
                self.tt(R(L3[0:C, :, 0:C]), P3(1, C)[:, :, 0:C], D3[0:C, :, 0:C], ALU.mult, [self.bps[1], bD3], [bL])
                self.tt(R(QK3[0:C, :, 0:C]), P3(2, C)[:, :, 0:C], D2[0:C, :, 0:C], ALU.mult, [self.bps[2], bD2], [bQK])
                yield
                xfin = self.sr3(14 if par == 0 else 9, 4)
                xtmp = self.sr3(13, 4)
                nlev = {64: 5, 32: 4, 16: 3, 8: 2, 4: 1, 2: 0}[C]
                xa, xb = (xtmp, xfin) if nlev % 2 == 1 else (xfin, xtmp)
                scr = [self.sr3(10, 4), self.sr3(11, 4), self.sr3(12, 4), xa, xb]
                res = {}
                yield from self.inverse_gen(U3, bU, L3, bL, C, 4, scr, (0, 1, 2), res)
                cx["X"] = res["X"]
            else:
                self.cp(R(QK3[0:C, :, 0:C]), P3(2, C)[:, :, 0:C], [self.bps[2]], [bQK], eng="dve")
                yield
            bV_, bbV = self.sc(1 if par == 0 else 0)
            Kd, bKd = self.sr(0 if par == 0 else 19)
            Zt, bZt = self.sc(2 if par == 0 else 3)
            for hh in range(H):
                self.tr(self.ps[3][0:C, hh * 128:(hh + 1) * 128], self.PJ[:, 8 + hh, c0:c0 + C], self.ident[:, :], [bV[hh], self.bconst], [self.bps[3]])
            for hh in range(H):
                self.tr(self.ps[0][0:C, hh * 128:(hh + 1) * 128], self.PJ[:, 4 + hh, c0:c0 + C], self.ident[:, :], [bK[hh], self.bconst], [self.bps[0]])
            for hh in range(H):
                self.tr(self.ps[1][0:C, hh * 128:(hh + 1) * 128], self.PJ[:, 12 + hh, c0:c0 + C], self.ident[:, :], [bZ[hh], self.bconst], [self.bps[1]])
            yield

            def T3(ap):
                return ap[0:C, 0:512].rearrange("p (h v) -> p h v", h=4)

            self.tt(T3(bV_), T3(self.ps[3]), cl[0:C, 4:8].unsqueeze(2).broadcast_to([C, 4, 128]), ALU.mult, [self.bps[3], bcl], [bbV])
            self.tt(R(T3(Kd)), T3(self.ps[0]), dc[0:C, 0:4].unsqueeze(2).broadcast_to([C, 4, 128]), ALU.mult, [self.bps[0], bdc], [bKd])
            self.cp(Zt[0:C, 0:512], self.ps[1][0:C, 0:512], [self.bps[1]], [bZt], eng="act")
            cx.update(kbg=(kbg, bkbg), qg=(qg, bqg), QK=(QK3, bQK), bV=(bV_, bbV), Kd=(Kd, bKd), Zt=(Zt, bZt), gam=(gam, bgam))
            yield

        def genB(ci, slot=0):
            c0, C, sidx = chunks[ci]
            b4, b5, b6, b7 = (4, 5, 6, 7) if slot == 0 else (0, 1, 2, 3)
            par = ci % 2
            cx = ctx[par]
            rr = C > 1
            samp = sidx < 16
            kbg, bkbg = cx["kbg"]
            qg, bqg = cx["qg"]
            QK3, bQK = cx["QK"]
            bV_, bbV = cx["bV"]
            Kd, bKd = cx["Kd"]
            Zt, bZt = cx["Zt"]
            gam, bgam = cx["gam"]

            def T3(ap):
                return ap[0:C, 0:512].rearrange("p (h v) -> p h v", h=4)

            if samp:
                St, bS = self.Ss[sidx % 2], self.bSs[sidx % 2]
                self.dma(St[:, :, :].rearrange("p h v -> p (h v)"), d["sgd"][sidx], [], [bS])
            else:
                St, bS = self.Sp, self.bSp
            if rr:
                Sr_t, bSr = self.sr(8)
                Sr = Sr_t[:, 0:512].rearrange("p (h v) -> p h v", h=4)
                self.cp(R(Sr), St[:, :, :], [bS], [bSr], eng="dve")
            else:
                Sr, bSr = St, bS
            for hh in range(H):
                self.mm(self.ps[b4][0:C, hh * 128:(hh + 1) * 128], kbg[:, hh, 0:C], Sr[:, hh, :], [bkbg, bSr], [self.bps[b4]], r=rr)
            yield
            Rs, bRs = self.sr(1 if slot == 0 else 20)
            self.tt(R(Rs[0:C, 0:512]), bV_[0:C, 0:512], self.ps[b4][0:C, 0:512], ALU.subtract, [bbV, self.bps[b4]], [bRs])
            yield
            if C > 1:
                X3, bX = cx["X"]
                VN, bVN = self.sr(2)
                for hh in range(H):
                    hc = slice(hh * 128, (hh + 1) * 128)
                    self.mm(self.ps[b5][0:C, hc], X3[0:C, hh, 0:C], Rs[0:C, hc], [bX, bRs], [self.bps[b5]], r=rr)
                yield
                self.cp(R(VN[0:C, 0:512]), self.ps[b5][0:C, 0:512], [self.bps[b5]], [bVN], eng="act")
                yield
            else:
                VN, bVN = Rs, bRs
            for hh in range(H):
                hc = slice(hh * 128, (hh + 1) * 128)
                self.mm(self.ps[b7][:, hc], Kd[0:C, hc], VN[0:C, hc], [bKd, bVN], [self.bps[b7]], r=rr)
            for hh in range(H):
                hc = slice(hh * 128, (hh + 1) * 128)
                self.mm(self.ps[b6][0:C, hc], qg[:, hh, 0:C], Sr[:, hh, :], [bqg, bSr], [self.bps[b6]], start=(hh == 0), stop=True, r=rr, g=True)
            for hh in range(H):
                hc = slice(hh * 128, (hh + 1) * 128)
                self.mm(self.ps[b6][0:C, hc], QK3[0:C, hh, 0:C], VN[0:C, hc], [bQK, bVN], [self.bps[b6]], start=False, stop=True, r=rr, g=True)
            yield
            for hh in range(H):
                hc = slice(hh * 128, (hh + 1) * 128)
                self.stt(St[:, hh, :], St[:, hh, :], gam[:, hh, C - 1:C], self.ps[b7][:, hc], ALU.mult, ALU.add,
                         [bS, bgam, self.bps[b7]], [bS])
            if samp:
                t = self.dma(d["ogd"][sidx], St[:, :, :].rearrange("p h v -> p (h v)"), [bS], [])
                self.out_tokens.append(t)
            yield
            sq, bsq = self.sc(4 if slot == 0 else 5)
            self.act(sq[0:C, 0:512], self.ps[b6][0:C, 0:512], AF.Square, [self.bps[b6]], [bsq])
            s2, bs2 = self.sm8[:, 5 + slot, :], self.bsm8[5 + slot]
            self.red(s2[0:C, 0:4], T3(sq), [bsq], [bs2])
            self.act(s2[0:C, 0:4], s2[0:C, 0:4], AF.Ln, [bs2, self.bc3], [bs2], bias=self.epsn[0:C, 0:1], scale=1.0 / 128.0)
            self.act(s2[0:C, 0:4], s2[0:C, 0:4], AF.Exp, [bs2], [bs2], scale=-0.5)
            yield
            o2, bo2 = sq, bsq
            self.tt(T3(o2), T3(self.ps[b6]), s2[0:C, 0:4].unsqueeze(2).broadcast_to([C, 4, 128]), ALU.mult, [self.bps[b6], bs2], [bo2])
            self.tt(o2[0:C, 0:512], o2[0:C, 0:512], nw[0:C, :], ALU.mult, [bo2, self.bbc], [bo2], eng="pool")
            self.tt(o2[0:C, 0:512], o2[0:C, 0:512], Zt[0:C, 0:512], ALU.mult, [bo2, bZt], [bo2], eng="pool")
            yield
            for hh in range(H):
                self.tr(self.ps[b4][:, hh * 64:hh * 64 + C], o2[0:C, hh * 128:(hh + 1) * 128], self.ident[0:C, 0:C],
                        [bo2, self.bconst], [self.bps[b4]])
            yield
            for hh in range(H):
                self.cp(self.mixT[:, 4 + hh, c0:c0 + C], self.ps[b4][:, hh * 64:hh * 64 + C], [self.bps[b4]], [self.bmx[4 + hh]], eng="act")
            yield

        self.run_chunks(genA, genB, chunks)

    def finish_outputs(self):
        d = self.d
        self.cp(self.OSH[:, :, 16], self.CR[:, 0:15, 2], [self.bCR], [self.bOSH])
        self.cp(self.OCV[:, :, :, 16], self.CR[:, 15:27, :], [self.bCR], [self.bOCV])
        self.out_tokens.append(self.dma(d["osh"], self.OSH[:, :, :].rearrange("p a s -> p (a s)"), [self.bOSH], []))
        self.out_tokens.append(self.dma(d["ocv"], self.OCV[:, :, :, :].rearrange("p a t s -> p (a t s)"), [self.bOCV], []))
        self.out_tokens.append(self.dma(d["orw"][16], self.Bp[:, :, :].rearrange("p j v -> p (j v)"), [self.bBp], []))
        self.out_tokens.append(self.dma(d["ogd"][16], self.Sp[:, :, :].rearrange("p h v -> p (h v)"), [self.bSp], []))


def _prep_shared(inp):
    f = np.float32
    sh = {}

    def gate(w):
        return np.ascontiguousarray(w.reshape(8, 128, 11, 256).transpose(2, 1, 0, 3)).reshape(11, 128, 2048)

    def down(w, ng):
        return np.ascontiguousarray(w.reshape(ng, 2, 128, 2, 512).transpose(3, 0, 2, 1, 4)).reshape(2 * ng, 128, 1024)

    def gateup(wg, wu):
        g = wg.reshape(8, 128, 22, 128).transpose(2, 1, 0, 3)
        u = wu.reshape(8, 128, 22, 128).transpose(2, 1, 0, 3)
        return np.ascontiguousarray(np.concatenate([g, u], axis=3)).reshape(22, 128, 2048)

    sh["wgu1"] = gateup(inp["w_gate1"][0], inp["w_up1"][0])
    sh["wgu2"] = gateup(inp["w_gate2"][0], inp["w_up2"][0])
    sh["wd1"] = down(inp["w_down1"][0], 11)
    sh["wd2"] = down(inp["w_down2"][0], 11)
    sh["wout"] = down(inp["w_out"][0], 4)
    W = inp["w_in"][0]
    win = np.zeros((17, 128, 8, 256), f)
    for oc in range(NOC):
        if OC[oc] is None:
            continue
        s, M = OC[oc]
        blk = W[:, s:s + M].reshape(8, 128, M).transpose(1, 0, 2)
        win[oc // 2, :, :, (oc % 2) * 128:(oc % 2) * 128 + M] = blk
    sh["win"] = win.reshape(17, 128, 2048)
    cols = np.zeros((128, NCOLS), f)

    def put(name, vec):
        cols[:len(vec), COLS[name]] = vec

    for nm, key in (("gf1", "g_ffn1"), ("gmx", "g_mix"), ("gf2", "g_ffn2")):
        for k in range(8):
            put("%s%d" % (nm, k), inp[key][0][k * 128:(k + 1) * 128])
    for k in range(8):
        put("gfn%d" % k, inp["g_final"][k * 128:(k + 1) * 128])
    for oc in range(15):
        s, M = OC[oc]
        put("mu%d" % oc, inp["mu_shift"][0][s:s + M])
    for nm, key in (("w0", "w0"), ("a0", "a0"), ("kk", "k_k"), ("ka", "k_a")):
        for j in range(4):
            put("%s%d" % (nm, j), inp[key][0][j * 128:(j + 1) * 128])
    rk = inp["r_k"][0].reshape(512)
    for j in range(4):
        put("rk%d" % j, rk[j * 128:(j + 1) * 128])
    for i in range(12):
        for t in range(4):
            put("cw%d_%d" % (i, t), inp["conv_w"][0][t, i * 128:(i + 1) * 128])
    put("dtb", inp["dt_bias"][0])
    put("alog", inp["a_log"][0])
    sh["cols"] = cols
    bc = np.concatenate([inp["lnx_w"][0], inp["lnx_b"][0], np.tile(inp["gdn_norm_w"][0], 4)]).astype(f)
    sh["bc"] = np.ascontiguousarray(np.tile(bc[None, :], (64, 1)))
    sh["wdu"] = np.ascontiguousarray(inp["w_decay_up"][0])
    sh["wau"] = np.ascontiguousarray(inp["w_a_up"][0])
    sh["wgu"] = np.ascontiguousarray(inp["w_g_up"][0])
    return sh


def _prep_core(inp, c):
    f = np.float32
    m = {}
    s0 = 16 * c
    rows = np.concatenate([inp["x_sample"][s0:s0 + 16, 0, :], inp["meta_tokens"], inp["x_prompt"][c]], axis=0)
    m["xT"] = np.ascontiguousarray(rows.T).reshape(8, 128, TTOT)
    st = inp["state_rwkv"][0, s0:s0 + 16]
    m["srw"] = np.ascontiguousarray(st.reshape(16, 4, 2, 64, 64).transpose(0, 2, 4, 1, 3)).reshape(16, 128, 256)
    sg = inp["state_gdn"][0, s0:s0 + 16]
    m["sgd"] = np.ascontiguousarray(sg.transpose(0, 2, 1, 3)).reshape(16, 128, 512)
    ss = inp["state_shift"][0, s0:s0 + 16]
    ssh = np.zeros((128, 15, 16), f)
    for oc in range(15):
        s, M = OC[oc]
        ssh[:M, oc, :] = ss[:, s:s + M].T
    m["ssh"] = ssh.reshape(128, 240)
    cv = inp["state_conv"][0, s0:s0 + 16]
    m["scv"] = np.ascontiguousarray(cv.reshape(16, 3, 12, 128).transpose(3, 2, 1, 0)).reshape(128, 576)
    return m


_NC_CACHE = {}
_BUILD_KW = {}


def _get_nc(**kw):
    key = tuple(sorted(kw.items()))
    if key not in _NC_CACHE:
        _NC_CACHE[key] = Builder(**kw).build()
    return _NC_CACHE[key]


def kernel(**inputs):
    inp = {k: np.asarray(v, dtype=np.float32) for k, v in inputs.items()}
    nc = _get_nc(**_BUILD_KW)
    sh = _prep_shared(inp)
    in_maps = []
    for c in range(NCORE):
        m = dict(sh)
        m.update(_prep_core(inp, c))
        in_maps.append(m)
    res = run_bass_kernel_spmd(nc, in_maps, core_ids=list(range(NCORE)))
    f = np.float32
    y_prompt = np.zeros((8, 2048, 1024), f)
    y_sample = np.zeros((128, 1, 1024), f)
    rw_p = np.zeros((1, 8, 8, 64, 64), f)
    sh_p = np.zeros((1, 8, 1696), f)
    gd_p = np.zeros((1, 8, 4, 128, 128), f)
    cv_p = np.zeros((1, 8, 3, 1536), f)
    rw_s = np.zeros((1, 128, 8, 64, 64), f)
    sh_s = np.zeros((1, 128, 1696), f)
    gd_s = np.zeros((1, 128, 4, 128, 128), f)
    cv_s = np.zeros((1, 128, 3, 1536), f)
    for c in range(NCORE):
        r = res.results[c]
        y = np.asarray(r["yT"]).reshape(1024, TTOT).T
        y_sample[16 * c:16 * c + 16, 0, :] = y[0:16]
        y_prompt[c] = y[32:]
        orw = np.asarray(r["orw"]).reshape(17, 2, 64, 4, 64).transpose(0, 3, 1, 4, 2).reshape(17, 8, 64, 64)
        rw_s[0, 16 * c:16 * c + 16] = orw[0:16]
        rw_p[0, c] = orw[16]
        ogd = np.asarray(r["ogd"]).reshape(17, 128, 4, 128).transpose(0, 2, 1, 3)
        gd_s[0, 16 * c:16 * c + 16] = ogd[0:16]
        gd_p[0, c] = ogd[16]
        osh = np.asarray(r["osh"]).reshape(128, 15, 17)
        full = np.zeros((17, 1696), f)
        for oc in range(15):
            s, M = OC[oc]
            full[:, s:s + M] = osh[:M, oc, :].T
        sh_s[0, 16 * c:16 * c + 16] = full[0:16]
        sh_p[0, c] = full[16]
        ocv = np.asarray(r["ocv"]).reshape(128, 12, 3, 17).transpose(3, 2, 1, 0).reshape(17, 3, 1536)
        cv_s[0, 16 * c:16 * c + 16] = ocv[0:16]
        cv_p[0, c] = ocv[16]
    return (y_prompt, y_sample, rw_p, sh_p, gd_p, cv_p, rw_s, sh_s, gd_s, cv_s)
```

```python
import contextlib
import numpy as np
import concourse.bass as bass
import concourse.mybir as mybir
from concourse.bass_utils import run_bass_kernel_spmd

F32 = mybir.dt.float32
BF16 = mybir.dt.bfloat16
R32 = mybir.dt.float32r


def R(ap):
    return ap.bitcast(R32)
AF = mybir.ActivationFunctionType
ALU = mybir.AluOpType
AX = mybir.AxisListType

ENGS = ("pe", "act", "dve", "pool", "sp")
NCORE = 8
TTOT = 2080
TM = 544
NSAMP = 16
EXPM05 = 0.6065306597126334
NEG = -1.0e30
STRICT = False
USE_R32 = True


class Buf:
    __slots__ = ("name", "w", "r", "dsem", "dcount", "excl")

    def __init__(self, name):
        self.name = name
        self.excl = False
        self.w = None
        self.r = []
        self.dsem = None
        self.dcount = 0


class Prog:
    def __init__(self, nc, stack):
        self.nc = nc
        self.stack = stack
        self.ops = {e: [] for e in ENGS}
        self.count = {e: 0 for e in ENGS}
        self.seen = {e: {} for e in ENGS}
        self.sems = {}
        for e in ENGS:
            self.sems[e] = stack.enter_context(nc.semaphore("s_" + e))
        self.nbuf = 0
        self.final_tokens = []
        self.final_eng = "sp"

    def sbuf(self, name, shape, dt=F32):
        return self.stack.enter_context(self.nc.sbuf_tensor("sb_" + name, list(shape), dt))

    def psum(self, name, shape, dt=F32):
        return self.stack.enter_context(self.nc.psum_tensor("pp_" + name, list(shape), dt))

    def buf(self, name=None):
        self.nbuf += 1
        return Buf(name or "b%d" % self.nbuf)

    def _dsem(self, b):
        if b.dsem is None:
            self.nbuf += 1
            key = "d%d" % self.nbuf
            self.sems[key] = self.stack.enter_context(self.nc.semaphore(key))
            b.dsem = key
        return b.dsem

    def _waits(self, eng, reads, writes):
        need = {}

        def add(tok, raw):
            if tok is None:
                return
            k, v = tok
            if k == eng:
                if eng == "pe":
                    return
                if not STRICT and (not raw or v < self.count[eng] - 1):
                    return
            if v > need.get(k, 0):
                need[k] = v

        for b in reads:
            add(b.w, True)
            if b.excl:
                for t in b.r:
                    if t[0] != eng:
                        add(t, False)
        for b in writes:
            add(b.w, False)
            for t in b.r:
                add(t, False)
        out = []
        seen = self.seen[eng]
        for k, v in need.items():
            if seen.get(k, 0) >= v:
                continue
            seen[k] = v
            out.append((k, v))
        return out

    def _record(self, tok, reads, writes):
        for b in reads:
            if len(b.r) > 24:
                best = {}
                for k, v in b.r:
                    if v > best.get(k, 0):
                        best[k] = v
                b.r = list(best.items())
            b.r.append(tok)
        for b in writes:
            b.w = tok
            b.r = []

    def op(self, eng, fn, reads=(), writes=()):
        waits = self._waits(eng, reads, writes)
        self.count[eng] += 1
        tok = (eng, self.count[eng])
        self.ops[eng].append((waits, fn, eng, 1))
        self._record(tok, reads, writes)
        return tok

    def dma(self, eng, fn, reads=(), writes=(), sem_buf=None):
        waits = self._waits(eng, reads, writes)
        sb = sem_buf or (writes[0] if writes else reads[0])
        key = self._dsem(sb)
        sb.dcount += 16
        tok = (key, sb.dcount)
        self.ops[eng].append((waits, fn, key, 16))
        self._record(tok, reads, writes)
        return tok

    def emit(self):
        nc = self.nc
        engobj = {"pe": "tensor", "act": "scalar", "dve": "vector", "pool": "gpsimd", "sp": "sync"}
        with nc.Block() as block:
            for e in ENGS:
                ops = self.ops[e]
                fin = self.final_tokens if self.final_eng == e else []
                if not ops and not fin:
                    continue

                def body(eobj, ops=ops, fin=fin):
                    for waits, fn, key, inc in ops:
                        for k, v in waits:
                            eobj.wait_ge(self.sems[k], v)
                        ins = fn(eobj)
                        ins.then_inc(self.sems[key], inc)
                    for k, v in fin:
                        eobj.wait_ge(self.sems[k], v)

                getattr(block, engobj[e])(body)


OC = []
for i in range(12):
    OC.append((i * 128, 128))
OC.append((1536, 32))
OC.append((1568, 32))
OC.append((1600, 96))
OC.append(None)
for i in range(16):
    OC.append((1696 + i * 128, 128))
OC.append((1696 + 2048, 4))
OC.append((1696 + 2052, 4))
NOC = 34

PASSES = []
_ch0 = [(s, 1, s) for s in range(16)] + [(16, 16, 16)] + [(32 + 64 * i, 64, 16) for i in range(8)]
PASSES.append(dict(col0=0, T=544, nsamp=16, chunks=_ch0, subs=[(0, 272), (272, 272)]))
for _p in range(3):
    PASSES.append(dict(col0=544 + 512 * _p, T=512, nsamp=0,
                       chunks=[(64 * i, 64, 16) for i in range(8)], subs=[(0, 512)]))

COLS = {}


def _build_cols_index():
    n = 0
    for nm in ("gf1", "gmx", "gf2", "gfn"):
        for k in range(8):
            COLS["%s%d" % (nm, k)] = n
            n += 1
    for oc in range(15):
        COLS["mu%d" % oc] = n
        n += 1
    for nm in ("w0", "a0", "kk", "ka", "rk"):
        for j in range(4):
            COLS["%s%d" % (nm, j)] = n
            n += 1
    for i in range(12):
        for t in range(4):
            COLS["cw%d_%d" % (i, t)] = n
            n += 1
    COLS["dtb"] = n
    n += 1
    COLS["alog"] = n
    n += 1
    return n


NCOLS = _build_cols_index()


class Builder:
    def __init__(self, npass=4, do_mix=True, do_tail=True, stop=99, cf="all", sub=99):
        self.sub = sub
        self.cf = cf
        self.stop = stop
        self.npass = npass
        self.do_mix = do_mix
        self.do_tail = do_tail

    def din(self, name, shape):
        return self.nc.dram_tensor(name, list(shape), F32, kind="ExternalInput").ap()

    def dout(self, name, shape):
        return self.nc.dram_tensor(name, list(shape), F32, kind="ExternalOutput").ap()

    def mm(self, out, lhsT, rhs, rd, wr, start=True, stop=True, r=False, g=False):
        if r and USE_R32:
            lhsT = R(lhsT)
            rhs = R(rhs)
        if g:
            self.P.op("pe", lambda e: e.matmul(out, lhsT=lhsT, rhs=rhs, start=start, stop=stop, skip_group_check=True), rd, wr)
        else:
            self.P.op("pe", lambda e: e.matmul(out, lhsT=lhsT, rhs=rhs, start=start, stop=stop), rd, wr)

    def tr(self, out, in_, ident, rd, wr):
        self.P.op("pe", lambda e: e.transpose(out=out, in_=in_, identity=ident), rd, wr)

    def act(self, out, in_, func, rd, wr, bias=None, scale=None):
        kw = {}
        if bias is not None:
            kw["bias"] = bias
        if scale is not None:
            kw["scale"] = scale
        self.P.op("act", lambda e: e.activation(out=out, in_=in_, func=func, **kw), rd, wr)

    def tt(self, out, in0, in1, op, rd, wr, eng="dve"):
        self.P.op(eng, lambda e: e.tensor_tensor(out=out, in0=in0, in1=in1, op=op), rd, wr)

    def ts(self, out, in0, s1, op0, rd, wr, s2=None, op1=None, eng="dve"):
        if op1 is None:
            self.P.op(eng, lambda e: e.tensor_scalar(out=out, in0=in0, scalar1=s1, scalar2=None, op0=op0), rd, wr)
        else:
            self.P.op(eng, lambda e: e.tensor_scalar(out=out, in0=in0, scalar1=s1, scalar2=s2, op0=op0, op1=op1), rd, wr)

    def stt(self, out, in0, scalar, in1, op0, op1, rd, wr):
        self.P.op("dve", lambda e: e.scalar_tensor_tensor(out=out, in0=in0, scalar=scalar, in1=in1, op0=op0, op1=op1), rd, wr)

    def cp(self, out, in_, rd, wr, eng="dve"):
        if eng == "act":
            self.P.op("act", lambda e: e.activation(out=out, in_=in_, func=AF.Copy), rd, wr)
        elif eng == "dve":
            self.P.op("dve", lambda e: e.tensor_scalar(out=out, in0=in_, scalar1=1.0, scalar2=None, op0=ALU.mult), rd, wr)
        else:
            self.P.op(eng, lambda e: e.tensor_copy(out=out, in_=in_), rd, wr)

    def red(self, out, in_, rd, wr):
        self.P.op("dve", lambda e: e.tensor_reduce(out=out, in_=in_, axis=AX.X, op=ALU.add), rd, wr)

    def recip(self, out, in_, rd, wr):
        self.P.op("dve", lambda e: e.reciprocal(out=out, in_=in_), rd, wr)

    def scan(self, out, d0, d1, rd, wr):
        self.P.op("dve", lambda e: e.tensor_tensor_scan(out=out, data0=d0, data1=d1, initial=0.0, op0=ALU.mult, op1=ALU.add), rd, wr)

    def memset(self, ap, val, wr, eng="dve"):
        self.P.op(eng, lambda e: e.memset(ap, val), (), wr)

    def dma(self, out, in_, rd, wr, eng="sp", sem_buf=None):
        return self.P.dma(eng, lambda e: e.dma_start(out=out, in_=in_), rd, wr, sem_buf=sem_buf)

    def col(self, name, m=128):
        i = COLS[name]
        return self.cols[0:m, i:i + 1]

    def get_w(self, expect):
        i = self.wpos
        assert self.wlist[i][0] == expect, (self.wlist[i][0], expect)
        self.wpos += 1
        NB = len(self.WT)
        while self.wnext < len(self.wlist) and self.wnext <= i + NB - 1:
            k = self.wnext
            t = self.WT[k % NB]
            src = self.wlist[k][1]
            self.dma(t[:, 0:src.shape[1]], src, [], [self.bWT[k % NB]], eng="pool")
            self.wnext += 1
        return self.WT[i % NB], self.bWT[i % NB]

    def build(self):
        nc = bass.Bass("TRN2", target_bir_lowering=False)
        self.nc = nc
        d = {}
        d["xT"] = self.din("xT", [8, 128, TTOT])
        for nm in ("wgu1", "wgu2"):
            d[nm] = self.din(nm, [22, 128, 2048])
        for nm in ("wd1", "wd2"):
            d[nm] = self.din(nm, [22, 128, 1024])
        d["win"] = self.din("win", [17, 128, 2048])
        d["wout"] = self.din("wout", [8, 128, 1024])
        d["cols"] = self.din("cols", [128, NCOLS])
        d["bc"] = self.din("bc", [128, 1536])
        d["wdu"] = self.din("wdu", [32, 512])
        d["wau"] = self.din("wau", [32, 512])
        d["wgu"] = self.din("wgu", [96, 512])
        d["srw"] = self.din("srw", [16, 128, 256])
        d["sgd"] = self.din("sgd", [16, 128, 512])
        d["ssh"] = self.din("ssh", [128, 15 * 16])
        d["scv"] = self.din("scv", [128, 12 * 48])
        d["yT"] = self.dout("yT", [8, 128, TTOT])
        d["orw"] = self.dout("orw", [17, 128, 256])
        d["ogd"] = self.dout("ogd", [17, 128, 512])
        d["osh"] = self.dout("osh", [128, 15 * 17])
        d["ocv"] = self.dout("ocv", [128, 12 * 51])
        self.d = d

        with contextlib.ExitStack() as st:
            P = Prog(nc, st)
            self.P = P
            self.alloc()
            self.make_wlist()
            self.setup_consts()
            self.out_tokens = []
            for p in range(self.npass):
                self.run_pass(p)
            self.finish_outputs()
            P.final_tokens = [t for t in self.out_tokens if t is not None]
            P.emit()
        return nc

    def alloc(self):
        P = self.P
        self.h = P.sbuf("h", [128, 8, TM])
        self.bh = [P.buf("h%d" % k) for k in range(8)]
        self.xn = P.sbuf("xn", [128, 8, TM], BF16)
        self.bxn = [P.buf("xn%d" % k) for k in range(8)]
        self.mixT = P.sbuf("mixT", [128, 8, TM], BF16)
        self.bmx = [P.buf("mx%d" % k) for k in range(8)]
        self.PJ = P.sbuf("PJ", [128, 20, TM])
        self.bpj = [P.buf("pj%d" % k) for k in range(20)]
        self.PJb = self.PJ[:, :, :].rearrange("p a t -> p (a t)").bitcast(BF16)
        NSC = 8
        self.SC = P.sbuf("SC", [128, NSC, TM])
        self.bsc = [P.buf("sc%d" % k) for k in range(NSC)]
        NSR = 25
        self.SR = P.sbuf("SR", [128, NSR, 512])
        self.bsr = [P.buf("sr%d" % k) for k in range(NSR)]
        self.SM = P.sbuf("SM", [128, 3, TM])
        self.bsm = [P.buf("sm%d" % k) for k in range(3)]
        self.RAW = [P.sbuf("RAW%d" % k, [128, 3 + TM]) for k in range(2)]
        self.bRAW = [P.buf("raw%d" % k) for k in range(2)]
        self.RS = [P.sbuf("RS%d" % k, [128, 16]) for k in range(2)]
        self.bRS = [P.buf("rs%d" % k) for k in range(2)]
        self.WT = [P.sbuf("WT%d" % k, [128, 2048], BF16) for k in range(3)]
        self.bWT = [P.buf("wt%d" % k) for k in range(3)]
        self.FM = P.sbuf("FM", [128, 4, 256])
        self.bfm = [P.buf("fm%d" % k) for k in range(4)]
        self.MR = P.sbuf("MR", [128, TM])
        self.bMR = P.buf("MR")
        self.cols = P.sbuf("cols", [128, NCOLS])
        self.bcols = P.buf("cols")
        self.c2 = P.sbuf("c2", [128, 8])
        self.bc2 = P.buf("c2")
        self.bcst = P.sbuf("bcst", [128, 1536])
        self.bbc = P.buf("bc")
        self.wdu = P.sbuf("wdu", [32, 512])
        self.wau = P.sbuf("wau", [32, 512])
        self.wgu = P.sbuf("wgu", [96, 512])
        self.bwsm = P.buf("wsm")
        self.ident = P.sbuf("ident", [128, 128])
        self.ones = P.sbuf("ones", [128, 128])
        self.onesb = P.sbuf("onesb", [128, 128], BF16)
        self.blk = P.sbuf("blk", [128, 128])
        self.blk2 = P.sbuf("blk2", [128, 2])
        self.sel = P.sbuf("sel", [4, 4, 128])
        self.msu = P.sbuf("msu", [64, 64])
        self.miu = P.sbuf("miu", [64, 64])
        self.msl = P.sbuf("msl", [64, 64])
        self.nsu = P.sbuf("nsu", [128, 128])
        self.niu = P.sbuf("niu", [128, 128])
        self.nsl = P.sbuf("nsl", [128, 128])
        self.bconst = P.buf("const")
        self.CR = P.sbuf("CR", [128, 27, 3])
        self.bCR = P.buf("CR")
        self.SSH = P.sbuf("SSH", [128, 15, 16])
        self.SCV = P.sbuf("SCV", [128, 12, 3, 16])
        self.bsst = P.buf("sst")
        self.OSH = P.sbuf("OSH", [128, 15, 17])
        self.bOSH = P.buf("OSH")
        self.OCV = P.sbuf("OCV", [128, 12, 3, 17])
        self.bOCV = P.buf("OCV")
        self.Bp = P.sbuf("Bp", [128, 4, 64])
        self.bBp = P.buf("Bp")
        self.Sp = P.sbuf("Sp", [128, 4, 128])
        self.bSp = P.buf("Sp")
        self.Bs = [P.sbuf("Bs%d" % k, [128, 4, 64]) for k in range(2)]
        self.bBs = [P.buf("Bs%d" % k) for k in range(2)]
        self.Ss = [P.sbuf("Ss%d" % k, [128, 4, 128]) for k in range(2)]
        self.bSs = [P.buf("Ss%d" % k) for k in range(2)]
        self.PCs = P.sbuf("PCs", [128, 4, 32])
        self.bPCs = P.buf("PCs")
        self.sm8 = P.sbuf("sm8", [128, 8, 8])
        self.bsm8 = [P.buf("sm8_%d" % k) for k in range(8)]
        self.ps = [P.psum("ps%d" % k, [128, 512]) for k in range(8)]
        self.bps = [P.buf("ps%d" % k) for k in range(8)]
        for b_ in self.bps:
            b_.excl = True

    def sc(self, i):
        return self.SC[:, i, :], self.bsc[i]

    def sr(self, i):
        return self.SR[:, i, :], self.bsr[i]

    def sr3(self, i, H, w=64):
        ap, b = self.sr(i)
        return ap[:, 0:H * w].rearrange("p (h c) -> p h c", h=H), b

    def make_wlist(self):
        d = self.d
        wl = []
        for p in range(self.npass):
            for c in range(22):
                wl.append(("wgu1", d["wgu1"][c]))
            for k in range(22):
                wl.append(("wd1", d["wd1"][k]))
            for g in range(17):
                wl.append(("win", d["win"][g]))
            if self.do_tail:
                for k in range(8):
                    wl.append(("wout", d["wout"][k]))
                for c in range(22):
                    wl.append(("wgu2", d["wgu2"][c]))
                for k in range(22):
                    wl.append(("wd2", d["wd2"][k]))
        self.wlist = wl
        self.wpos = 0
        self.wnext = 0

    def setup_consts(self):
        d = self.d
        bc_ = [self.bconst]
        self.dma(self.cols[:, :], d["cols"], [], [self.bcols])
        self.dma(self.bcst[:, :], d["bc"], [], [self.bbc])
        self.dma(self.wdu[:, :], d["wdu"], [], [self.bwsm])
        self.dma(self.wau[:, :], d["wau"], [], [self.bwsm])
        self.dma(self.wgu[:, :], d["wgu"], [], [self.bwsm])
        self.dma(self.SSH[:, :, :].rearrange("p a s -> p (a s)"), d["ssh"], [], [self.bsst])
        self.dma(self.SCV[:, :, :, :].rearrange("p a t s -> p (a t s)"), d["scv"], [], [self.bsst])

        def pool(fn):
            self.P.op("pool", fn, bc_, bc_)

        pool(lambda e: e.memset(self.ones[:, :], 1.0))
        pool(lambda e: e.memset(self.onesb[:, :], 1.0))
        pool(lambda e: e.memset(self.ident[:, :], 1.0))
        pool(lambda e: e.affine_select(out=self.ident[:, :], in_=self.ident[:, :], pattern=[[-1, 128]],
                                       compare_op=ALU.is_equal, fill=0.0, base=0, channel_multiplier=1))
        pool(lambda e: e.memset(self.blk[:, :], 0.0))
        pool(lambda e: e.memset(self.blk[0:64, 0:64], 1.0))
        pool(lambda e: e.memset(self.blk[64:128, 64:128], 1.0))
        pool(lambda e: e.memset(self.blk2[:, :], 0.0))
        pool(lambda e: e.memset(self.blk2[0:64, 0:1], 1.0))
        pool(lambda e: e.memset(self.blk2[64:128, 1:2], 1.0))
        pool(lambda e: e.memset(self.sel[:, :, :], 1.0))
        pool(lambda e: e.affine_select(out=self.sel[:, :, :], in_=self.sel[:, :, :], pattern=[[-1, 4], [0, 128]],
                                       compare_op=ALU.is_equal, fill=0.0, base=0, channel_multiplier=1))
        for t_, cmp_, pat, cm, fill, base0 in (
            (self.msu, ALU.is_gt, 1, -1, 0.0, 1.0),
            (self.miu, ALU.is_ge, 1, -1, 0.0, 1.0),
            (self.msl, ALU.is_gt, -1, 1, 0.0, 1.0),
            (self.nsu, ALU.is_gt, 1, -1, NEG, 0.0),
            (self.niu, ALU.is_ge, 1, -1, NEG, 0.0),
            (self.nsl, ALU.is_gt, -1, 1, NEG, 0.0),
        ):
            pool(lambda e, t_=t_, base0=base0: e.memset(t_[:, :], base0))
            pool(lambda e, t_=t_, cmp_=cmp_, pat=pat, cm=cm, fill=fill: e.affine_select(
                out=t_[:, :], in_=t_[:, :], pattern=[[pat, t_.shape[1]]], compare_op=cmp_, fill=fill,
                base=0, channel_multiplier=cm))
        pool(lambda e: e.memset(self.CR[:, :, :], 0.0))
        pool(lambda e: e.memset(self.OSH[:, :, :], 0.0))
        pool(lambda e: e.memset(self.OCV[:, :, :, :], 0.0))
        self.P.op("pool", lambda e: e.memset(self.Bp[:, :, :], 0.0), (), [self.bBp])
        self.P.op("pool", lambda e: e.memset(self.Sp[:, :, :], 0.0), (), [self.bSp])
        for j in range(4):
            self.ts(self.c2[:, j:j + 1], self.col("ka%d" % j), -1.0, ALU.mult, [self.bcols], [self.bc2], s2=1.0, op1=ALU.add)
        self.act(self.c2[0:4, 4:5], self.col("alog", 4), AF.Exp, [self.bcols], [self.bc2])
        self.ts(self.c2[0:4, 5:6], self.c2[0:4, 4:5], -1.0, ALU.mult, [self.bc2], [self.bc2])

    def run_pass(self, p):
        ps_ = PASSES[p]
        self.cur = ps_
        self.pidx = p
        T = ps_["T"]
        col0 = ps_["col0"]
        d = self.d
        self.P.op("pool", lambda e: e.memset(self.MR[:, :], 1.0), (), [self.bMR])
        if ps_["nsamp"]:
            self.P.op("pool", lambda e: e.memset(self.MR[:, 0:17], 0.0), (), [self.bMR])
            self.P.op("pool", lambda e: e.memset(self.MR[:, 32:544:64], 0.0), (), [self.bMR])
        else:
            self.P.op("pool", lambda e: e.memset(self.MR[:, 0:512:64], 0.0), (), [self.bMR])
        src = d["xT"].rearrange("k p t -> p k t")[:, :, col0:col0 + T]
        self.dma(self.h[:, :, 0:T], src, [], list(self.bh))
        self.norm("gf1")
        if self.stop <= 1:
            return self.dbg_dump()
        self.ffn("wgu1", "wd1")
        if self.stop <= 2:
            return self.dbg_dump()
        self.norm("gmx")
        self.project_rwkv()
        if self.stop <= 3:
            return self.dbg_dump()
        if self.do_mix:
            self.rwkv_prep()
            if self.stop <= 4:
                return self.dbg_dump()
            self.rwkv_chunks()
            if self.stop <= 5:
                return self.dbg_dump()
        self.project_gdn()
        if self.stop <= 6:
            return self.dbg_dump()
        if self.do_mix:
            self.gdn_prep()
            if self.stop <= 7:
                return self.dbg_dump()
            self.gdn_chunks()
            if self.stop <= 8:
                return self.dbg_dump()
        if self.do_tail:
            self.down(lambda c: (self.mixT[:, c, :], self.bmx[c]), 8, "wout", 1.0)
            self.norm("gf2")
            self.ffn("wgu2", "wd2")
            self.final_norm_store()

    def dbg_dump(self):
        T = self.cur["T"]
        col0 = self.cur["col0"]
        for kc in range(8):
            t = self.dma(self.d["yT"][kc, :, col0:col0 + T], self.h[:, kc, 0:T], [self.bh[kc]], [])
            self.out_tokens.append(t)

    def rstd_tile(self):
        T = self.cur["T"]
        subs = self.cur["subs"]
        for si, (s0, n) in enumerate(subs):
            bank, bb = self.ps[6 + si], self.bps[6 + si]
            for kc in range(8):
                ti = 18 + (kc % 2)
                tmp, tb = self.PJb[:, ti * 2 * TM:ti * 2 * TM + TM], self.bpj[ti]
                self.act(tmp[:, 0:n], self.h[:, kc, s0:s0 + n], AF.Square, [self.bh[kc]], [tb])
                self.mm(bank[:, 0:n], self.onesb[:, :], tmp[:, 0:n], [tb, self.bconst], [bb], start=(kc == 0), stop=(kc == 7))
            r1, b1 = self.PJ[:, 17, :], self.bpj[17]
            self.act(r1[:, s0:s0 + n], bank[:, 0:n], AF.Ln, [bb, self.bc3], [b1], bias=self.epsn[:, 0:1], scale=1.0 / 1024.0)
        r2, b2 = self.PJ[:, 16, :], self.bpj[16]
        self.act(r2[:, 0:T], r1[:, 0:T], AF.Exp, [b1], [b2], scale=-0.5)
        return r2, b2

    def norm(self, gname):
        T = self.cur["T"]
        if not hasattr(self, "epsn"):
            self.epsn = self.P.sbuf("epsn", [128, 4])
            self.bc3 = self.P.buf("epsn")
            self.memset(self.epsn[:, 0:1], 1e-6, [self.bc3])
            self.memset(self.epsn[:, 1:2], 1e-12, [self.bc3])
            self.memset(self.epsn[:, 2:3], 64e-5, [self.bc3])
            self.memset(self.epsn[:, 3:4], 1.0, [self.bc3])
        r2, b2 = self.rstd_tile()
        for kc in range(8):
            self.stt(self.xn[:, kc, 0:T], self.h[:, kc, 0:T], self.col("%s%d" % (gname, kc)), r2[:, 0:T],
                     ALU.mult, ALU.mult, [self.bh[kc], b2, self.bcols], [self.bxn[kc]])

    def final_norm_store(self):
        T = self.cur["T"]
        col0 = self.cur["col0"]
        r2, b2 = self.rstd_tile()
        for kc in range(8):
            o, ob = self.PJ[:, kc, :], self.bpj[kc]
            self.stt(o[:, 0:T], self.h[:, kc, 0:T], self.col("gfn%d" % kc), r2[:, 0:T],
                     ALU.mult, ALU.mult, [self.bh[kc], b2, self.bcols], [ob])
            t = self.dma(self.d["yT"][kc, :, col0:col0 + T], o[:, 0:T], [ob], [])
            self.out_tokens.append(t)

    def ffn(self, wgu, wd):
        T = self.cur["T"]
        subs = self.cur["subs"]
        cnt = 0
        for c in range(22):
            wt, wb = self.get_w(wgu)
            v = wt[:, :].rearrange("p (k c) -> p k c", k=8)
            a_ap = self.PJb[:, c * TM:c * TM + T]
            a_buf = self.bpj[c // 2]
            for si, (s0, n) in enumerate(subs):
                st_ = cnt % 2
                cnt += 1
                gb, gbb = self.ps[2 * st_], self.bps[2 * st_]
                ub, ubb = self.ps[2 * st_ + 1], self.bps[2 * st_ + 1]
                for kc in range(8):
                    self.mm(gb[:, 0:n], v[:, kc, 0:128], self.xn[:, kc, s0:s0 + n],
                            [wb, self.bxn[kc]], [gbb], start=(kc == 0), stop=(kc == 7))
                for kc in range(8):
                    self.mm(ub[:, 0:n], v[:, kc, 128:256], self.xn[:, kc, s0:s0 + n],
                            [wb, self.bxn[kc]], [ubb], start=(kc == 0), stop=(kc == 7))
                sg, sgb = self.PJ[:, 16 + st_, :], self.bpj[16 + st_]
                self.act(sg[:, 0:n], gb[:, 0:n], AF.Silu, [gbb], [sgb])
                self.tt(a_ap[:, s0:s0 + n], sg[:, 0:n], ub[:, 0:n], ALU.mult, [sgb, ubb], [a_buf])
        self.down(lambda c: (self.PJb[:, c * TM:c * TM + TM], self.bpj[c // 2]), 22, wd, 0.5)

    def down(self, src, n_c, wname, scale):
        subs = self.cur["subs"]
        ns = len(subs)
        for half in range(2):
            for g in range(n_c // 2):
                wt, wb = self.get_w(wname)
                v = wt[:, 0:1024].rearrange("p (c n) -> p c n", c=2)
                for cc in range(2):
                    c = 2 * g + cc
                    s_ap, s_buf = src(c)
                    for jj in range(4):
                        for si, (s0, n) in enumerate(subs):
                            b = jj * ns + si + (4 * half if ns == 1 else 0)
                            self.mm(self.ps[b][:, 0:n], v[:, cc, jj * 128:(jj + 1) * 128], s_ap[:, s0:s0 + n],
                                    [wb, s_buf], [self.bps[b]], start=(c == 0), stop=(c == n_c - 1))
            for jj in range(4):
                j = half * 4 + jj
                for si, (s0, n) in enumerate(subs):
                    b = jj * ns + si + (4 * half if ns == 1 else 0)
                    self.stt(self.h[:, j, s0:s0 + n], self.ps[b][:, 0:n], scale, self.h[:, j, s0:s0 + n],
                             ALU.mult, ALU.add, [self.bps[b], self.bh[j]], [self.bh[j]])

    def project(self, g0, g1, post):
        subs = self.cur["subs"]
        cnt = 0
        for g in range(g0, g1):
            wt, wb = self.get_w("win")
            v = wt[:, :].rearrange("p (k c) -> p k c", k=8)
            for cc in range(2):
                oc = 2 * g + cc
                if OC[oc] is None:
                    continue
                M = OC[oc][1]
                banks = []
                for si, (s0, n) in enumerate(subs):
                    b = 4 + (cnt % 4)
                    cnt += 1
                    for kc in range(8):
                        self.mm(self.ps[b][0:M, 0:n], v[:, kc, cc * 128:cc * 128 + M], self.xn[:, kc, s0:s0 + n],
                                [wb, self.bxn[kc]], [self.bps[b]], start=(kc == 0), stop=(kc == 7))
                    banks.append(b)
                post(oc, M, banks)

    def evac_raw(self, oc, M, banks):
        subs = self.cur["subs"]
        ns_ = self.cur["nsamp"]
        k = oc % 2
        raw, rb = self.RAW[k], self.bRAW[k]
        rs, rsb = self.RS[k], self.bRS[k]
        for (s0, n), b in zip(subs, banks):
            a = max(s0, ns_)
            if a < s0 + n:
                self.cp(raw[0:M, 3 + a - ns_:3 + s0 + n - ns_], self.ps[b][0:M, a - s0:n], [self.bps[b]], [rb], eng="act")
            if s0 < ns_:
                self.cp(rs[0:M, s0:ns_], self.ps[b][0:M, 0:ns_ - s0], [self.bps[b]], [rsb], eng="act")
        return raw, rb, rs, rsb

    def project_rwkv(self):
        T = self.cur["T"]
        ns_ = self.cur["nsamp"]
        Tp = T - ns_

        def post(oc, M, banks):
            raw, rb, rs, rsb = self.evac_raw(oc, M, banks)
            if oc < 12:
                dst, db = self.PJ[:, oc, :], self.bpj[oc]
            else:
                dst, db = self.SM[:, oc - 12, :], self.bsm[oc - 12]
            mu = self.col("mu%d" % oc, M)
            self.cp(raw[0:M, 0:3], self.CR[0:M, oc, :], [self.bCR], [rb], eng="pool")
            tmp, tb = self.PJ[:, 12 + (oc % 2), :], self.bpj[12 + (oc % 2)]
            self.tt(tmp[0:M, 0:Tp], raw[0:M, 2:2 + Tp], raw[0:M, 3:3 + Tp], ALU.subtract, [rb], [tb])
            self.stt(dst[0:M, ns_:T], tmp[0:M, 0:Tp], mu, raw[0:M, 3:3 + Tp], ALU.mult, ALU.add, [tb, rb, self.bcols], [db])
            if ns_:
                t2, t2b = self.PJ[:, 14 + (oc % 2), :], self.bpj[14 + (oc % 2)]
                self.tt(t2[0:M, 0:16], self.SSH[0:M, oc, :], rs[0:M, :], ALU.subtract, [self.bsst, rsb], [t2b], eng="pool")
                self.stt(dst[0:M, 0:16], t2[0:M, 0:16], mu, rs[0:M, :], ALU.mult, ALU.add, [t2b, rsb, self.bcols], [db])
                self.cp(self.OSH[0:M, oc, 0:16], rs[0:M, :], [rsb], [self.bOSH], eng="pool")
            self.cp(self.CR[0:M, oc, :], raw[0:M, Tp:Tp + 3], [rb], [self.bCR], eng="pool")

        self.project(0, 8, post)

    def project_gdn(self):
        T = self.cur["T"]
        ns_ = self.cur["nsamp"]
        Tp = T - ns_
        subs = self.cur["subs"]

        def post(oc, M, banks):
            if oc >= 32:
                dst, db = self.SM[:, oc - 32, :], self.bsm[oc - 32]
                for (s0, n), b in zip(subs, banks):
                    self.cp(dst[0:4, s0:s0 + n], self.ps[b][0:4, 0:n], [self.bps[b]], [db], eng="act")
                return
            if oc >= 28:
                dst, db = self.PJ[:, 12 + (oc - 28), :], self.bpj[12 + (oc - 28)]
                for (s0, n), b in zip(subs, banks):
                    self.act(dst[:, s0:s0 + n], self.ps[b][:, 0:n], AF.Silu, [self.bps[b]], [db])
                return
            i = oc - 16
            raw, rb, rs, rsb = self.evac_raw(oc, M, banks)
            dst, db = self.PJ[:, i, :], self.bpj[i]
            self.cp(raw[:, 0:3], self.CR[:, 15 + i, :], [self.bCR], [rb], eng="pool")
            acc, ab = self.PJ[:, 16 + (i % 2), :], self.bpj[16 + (i % 2)]
            self.ts(acc[:, 0:Tp], raw[:, 0:Tp], self.col("cw%d_0" % i), ALU.mult, [rb, self.bcols], [ab])
            for t in range(1, 4):
                self.stt(acc[:, 0:Tp], raw[:, t:t + Tp], self.col("cw%d_%d" % (i, t)), acc[:, 0:Tp], ALU.mult, ALU.add,
                         [rb, ab, self.bcols], [ab])
            self.act(dst[:, ns_:T], acc[:, 0:Tp], AF.Silu, [ab], [db])
            if ns_:
                a2, a2b = self.PJ[:, 18 + (i % 2), :], self.bpj[18 + (i % 2)]
                self.ts(a2[:, 0:16], self.SCV[:, i, 0, :], self.col("cw%d_0" % i), ALU.mult, [self.bsst, self.bcols], [a2b])
                for t in range(1, 3):
                    self.stt(a2[:, 0:16], self.SCV[:, i, t, :], self.col("cw%d_%d" % (i, t)), a2[:, 0:16], ALU.mult, ALU.add,
                             [self.bsst, a2b, self.bcols], [a2b])
                self.stt(a2[:, 0:16], rs[:, :], self.col("cw%d_3" % i), a2[:, 0:16], ALU.mult, ALU.add,
                         [rsb, a2b, self.bcols], [a2b])
                self.act(dst[:, 0:16], a2[:, 0:16], AF.Silu, [a2b], [db])
                self.cp(self.OCV[:, i, 0:2, 0:16], self.SCV[:, i, 1:3, :], [self.bsst], [self.bOCV], eng="pool")
                self.cp(self.OCV[:, i, 2, 0:16], rs[:, :], [rsb], [self.bOCV], eng="pool")
            self.cp(self.CR[:, 15 + i, :], raw[:, Tp:Tp + 3], [rb], [self.bCR], eng="pool")

        self.project(8, 17, post)

    def rwkv_prep(self):
        T = self.cur["T"]
        subs = self.cur["subs"]
        chunks = self.cur["chunks"]
        XWD, bwd = self.SM[:, 0, :], self.bsm[0]
        XAD, bad = self.SM[:, 1, :], self.bsm[1]
        XGD, bgd = self.SM[:, 2, :], self.bsm[2]
        self.act(XWD[0:32, 0:T], XWD[0:32, 0:T], AF.Tanh, [bwd], [bwd])
        self.act(XGD[0:96, 0:T], XGD[0:96, 0:T], AF.Sigmoid, [bgd], [bgd])
        for j in range(4):
            XR, bxr = self.PJ[:, j, :], self.bpj[j]
            XK, bxk = self.PJ[:, 4 + j, :], self.bpj[4 + j]
            KAP, bkap = self.PJ[:, 12 + j, :], self.bpj[12 + j]
            AH, bah = self.PJ[:, 16 + j, :], self.bpj[16 + j]
            sig, bsig = self.sc(0)
            lam, blam = self.sc(1)
            eP, beP = self.sc(2)
            eN, beN = self.sc(3)
            ePx, bePx = self.sc(4)
            A, bA = self.sc(5)
            kr, bkr = self.sc(6)
            t8, bt8 = self.sc(7)
            jc = slice(j * 128, (j + 1) * 128)
            for si, (s0, n) in enumerate(subs):
                b = 4 + si
                self.mm(self.ps[b][:, 0:n], self.wdu[0:32, jc], XWD[0:32, s0:s0 + n], [self.bwsm, bwd], [self.bps[b]])
                self.act(sig[:, s0:s0 + n], self.ps[b][:, 0:n], AF.Sigmoid, [self.bps[b], self.bcols], [bsig], bias=self.col("w0%d" % j))
                b2 = 6 + si
                self.mm(self.ps[b2][:, 0:n], self.wau[0:32, jc], XAD[0:32, s0:s0 + n], [self.bwsm, bad], [self.bps[b2]])
                self.act(A[:, s0:s0 + n], self.ps[b2][:, 0:n], AF.Sigmoid, [self.bps[b2], self.bcols], [bA], bias=self.col("a0%d" % j))
            self.scan(lam[:, 0:T], self.MR[:, 0:T], sig[:, 0:T], [self.bMR, bsig], [blam])
            self.act(eP[:, 0:T], lam[:, 0:T], AF.Exp, [blam], [beP], scale=-EXPM05)
            self.act(eN[:, 0:T], lam[:, 0:T], AF.Exp, [blam], [beN], scale=EXPM05)
            self.tt(ePx[:, 0:T], lam[:, 0:T], sig[:, 0:T], ALU.subtract, [blam, bsig], [bePx])
            self.act(ePx[:, 0:T], ePx[:, 0:T], AF.Exp, [bePx], [bePx], scale=-EXPM05)
            if self.cur["nsamp"]:
                self.cp(self.PCs[:, j, 0:16], eP[:, 0:16], [beP], [self.bPCs])
                self.cp(self.PCs[:, j, 16:17], eP[:, 31:32], [beP], [self.bPCs])
                self.cp(self.PCs[:, j, 17:25], eP[:, 95:544:64], [beP], [self.bPCs])
            else:
                self.cp(self.PCs[:, j, 0:8], eP[:, 63:512:64], [beP], [self.bPCs])
            self.ts(kr[:, 0:T], XK[:, 0:T], self.col("kk%d" % j), ALU.mult, [bxk, self.bcols], [bkr])
            self.act(t8[:, 0:T], kr[:, 0:T], AF.Square, [bkr], [bt8])
            for si, (s0, n) in enumerate(subs):
                b = 4 + si
                self.mm(self.ps[b][:, 0:n], self.blk[:, :], t8[:, s0:s0 + n], [self.bconst, bt8], [self.bps[b]])
                self.act(sig[:, s0:s0 + n], self.ps[b][:, 0:n], AF.Ln, [self.bps[b], self.bc3], [bsig], bias=self.epsn[:, 1:2])
            self.act(t8[:, 0:T], sig[:, 0:T], AF.Exp, [bsig], [bt8], scale=-0.5)
            self.tt(kr[:, 0:T], kr[:, 0:T], t8[:, 0:T], ALU.mult, [bkr, bt8], [bkr])
            self.tt(AH[:, 0:T], kr[:, 0:T], A[:, 0:T], ALU.mult, [bkr, bA], [bah])
            self.tt(AH[:, 0:T], AH[:, 0:T], eN[:, 0:T], ALU.mult, [bah, beN], [bah])
            self.tt(KAP[:, 0:T], kr[:, 0:T], ePx[:, 0:T], ALU.mult, [bkr, bePx], [bkap])
            self.ts(t8[:, 0:T], A[:, 0:T], self.col("ka%d" % j), ALU.mult, [bA, self.bcols, self.bc2], [bt8],
                    s2=self.c2[:, j:j + 1], op1=ALU.add)
            self.tt(XK[:, 0:T], XK[:, 0:T], t8[:, 0:T], ALU.mult, [bxk, bt8], [bxk])
            self.tt(XK[:, 0:T], XK[:, 0:T], eN[:, 0:T], ALU.mult, [bxk, beN], [bxk])
            self.tt(XR[:, 0:T], XR[:, 0:T], eP[:, 0:T], ALU.mult, [bxr, beP], [bxr])

    def inverse(self, U, bU, L, bL, C, H, scr, banks):
        (Ub, bUb), (Lb, bLb), (IL, bIL), (Xa, bXa), (Xb, bXb) = scr
        pL, pU, pX = banks
        idb = self.ident[0:C, 0:C].unsqueeze(1).broadcast_to([C, H, C])
        self.stt(R(Xa[0:C, :, 0:C]), U[0:C, :, 0:C], -1.0, idb, ALU.mult, ALU.add, [self.bconst, bU], [bXa])
        nlev = {64: 5, 32: 4, 16: 3, 8: 2, 4: 1, 2: 0}[C]
        Uc, bUc, Lc, bLc = U, bU, L, bL
        Un, bUn, Ln, bLn = Ub, bUb, Lb, bLb
        X, bX, Xn, bXn = Xa, bXa, Xb, bXb
        for lev in range(nlev):
            last = lev == nlev - 1
            PL = self.ps[pL][0:C, 0:H * 64].rearrange("p (h c) -> p h c", h=H)
            PU = self.ps[pU][0:C, 0:H * 64].rearrange("p (h c) -> p h c", h=H)
            PX = self.ps[pX][0:C, 0:H * 64].rearrange("p (h c) -> p h c", h=H)
            for hh in range(H):
                self.mm(PL[:, hh, 0:C], Uc[0:C, hh, 0:C], Lc[0:C, hh, 0:C], [bUc, bLc], [self.bps[pL]], r=True)
            if not last:
                for hh in range(H):
                    self.mm(PU[:, hh, 0:C], Lc[0:C, hh, 0:C], Uc[0:C, hh, 0:C], [bUc, bLc], [self.bps[pU]], r=True)
            self.tt(R(IL[0:C, :, 0:C]), PL[:, :, 0:C], idb, ALU.add, [self.bps[pL], self.bconst], [bIL])
            if not last:
                self.cp(R(Ln[0:C, :, 0:C]), PL[:, :, 0:C], [self.bps[pL]], [bLn], eng="act")
                self.cp(R(Un[0:C, :, 0:C]), PU[:, :, 0:C], [self.bps[pU]], [bUn], eng="act")
            for hh in range(H):
                self.mm(PX[:, hh, 0:C], IL[0:C, hh, 0:C], X[0:C, hh, 0:C], [bIL, bX], [self.bps[pX]], r=True)
            self.cp(R(Xn[0:C, :, 0:C]), PX[:, :, 0:C], [self.bps[pX]], [bXn], eng="dve")
            X, bX, Xn, bXn = Xn, bXn, X, bX
            if not last:
                Uc, bUc, Un, bUn = Un, bUn, Uc, bUc
                Lc, bLc, Ln, bLn = Ln, bLn, Lc, bLc
        return X, bX

    def sc3(self, i, H, w=64):
        ap, b = self.sc(i)
        return ap[:, 0:H * w].rearrange("p (h c) -> p h c", h=H), b

    def rwkv_chunks(self):
        d = self.d
        H = 8
        chunks = self.cur["chunks"]
        bR = [self.bpj[j] for j in range(4)]
        bK = [self.bpj[4 + j] for j in range(4)]
        bV = [self.bpj[8 + j] for j in range(4)]
        bKA = [self.bpj[12 + j] for j in range(4)]
        bAH = [self.bpj[16 + j] for j in range(4)]
        SG, bsg = self.SM[:, 2, :], self.bsm[2]
        lnw = self.bcst[:, 0:512]
        lnb = self.bcst[:, 512:1024]
        ctx = [dict(), dict()]

        def genA(ci):
            c0, C, sidx = chunks[ci]
            par = ci % 2
            cx = ctx[par]
            cx.clear()
            rr = C > 1

            def P3(b_, C_):
                return self.ps[b_][0:C_, 0:512].rearrange("p (h c) -> p h c", h=8)

            hm = self.blk2[:, :].unsqueeze(1).unsqueeze(3).broadcast_to([128, 4, 2, C])
            bdv = []
            for t, si_, bsrc in ((0, 15 if par == 0 else 20, bR), (1, 16, bK), (3, 17 if par == 0 else 21, bKA), (4, 18, bAH)):
                ap, bb = self.sr(si_)
                v = ap[:, 0:8 * C].rearrange("p (j e c) -> p j e c", j=4, e=2)
                srcv = self.PJ[:, t * 4:t * 4 + 4, c0:c0 + C].unsqueeze(2).broadcast_to([128, 4, 2, C])
                self.tt(R(v), srcv, hm, ALU.mult, list(bsrc) + [self.bconst], [bb])
                bdv.append((v, bb))
            (Rbd, bRbd), (Kbd, bKbd), (KAbd, bKAbd), (Abd, bAbd) = bdv
            yield
            U3, bU = self.sr3(5, 8)
            L3, bL = self.sr3(6, 8)
            Kk3, bKk = self.sr3(7 if par == 0 else 22, 8)
            Ma3, bMa = self.sr3(8 if par == 0 else 23, 8)
            Mk3, bMk = self.sr3(9 if par == 0 else 24, 8)
            for hh in range(H):
                j, e_ = hh // 2, hh % 2
                if C > 1:
                    self.mm(P3(0, C)[:, hh, 0:C], Abd[:, j, e_, :], KAbd[:, j, e_, :], [bAbd, bKAbd], [self.bps[0]], r=rr)
                    self.mm(P3(1, C)[:, hh, 0:C], KAbd[:, j, e_, :], Abd[:, j, e_, :], [bKAbd, bAbd], [self.bps[1]], r=rr)
                    self.mm(P3(2, C)[:, hh, 0:C], Kbd[:, j, e_, :], KAbd[:, j, e_, :], [bKbd, bKAbd], [self.bps[2]], r=rr)
                self.mm(P3(3, C)[:, hh, 0:C], Abd[:, j, e_, :], Rbd[:, j, e_, :], [bAbd, bRbd], [self.bps[3]], r=rr)
                self.mm(P3(4, C)[:, hh, 0:C], Kbd[:, j, e_, :], Rbd[:, j, e_, :], [bKbd, bRbd], [self.bps[4]], r=rr)
            yield

            def mask(m):
                return m[0:C, 0:C].unsqueeze(1).broadcast_to([C, 8, C])

            if C > 1:
                self.tt(R(U3[0:C, :, 0:C]), P3(0, C)[:, :, 0:C], mask(self.msu), ALU.mult, [self.bps[0], self.bconst], [bU])
                self.tt(R(L3[0:C, :, 0:C]), P3(1, C)[:, :, 0:C], mask(self.msl), ALU.mult, [self.bps[1], self.bconst], [bL])
                self.tt(R(Kk3[0:C, :, 0:C]), P3(2, C)[:, :, 0:C], mask(self.msu), ALU.mult, [self.bps[2], self.bconst], [bKk])
            self.tt(R(Ma3[0:C, :, 0:C]), P3(3, C)[:, :, 0:C], mask(self.miu), ALU.mult, [self.bps[3], self.bconst], [bMa])
            self.tt(R(Mk3[0:C, :, 0:C]), P3(4, C)[:, :, 0:C], mask(self.miu), ALU.mult, [self.bps[4], self.bconst], [bMk])
            yield
            if C > 1:
                xfin = self.sr3(14 if par == 0 else 19, 8)
                xtmp = self.sr3(13, 8)
                nlev = {64: 5, 32: 4, 16: 3, 8: 2, 4: 1, 2: 0}[C]
                xa, xb = (xtmp, xfin) if nlev % 2 == 1 else (xfin, xtmp)
                scr = [self.sr3(10, 8), self.sr3(11, 8), self.sr3(12, 8), xa, xb]
                res = {}
                yield from self.inverse_gen(U3, bU, L3, bL, C, 8, scr, (0, 1, 2), res)
                cx["X"] = res["X"]
            cx.update(Rbd=(Rbd, bRbd), KAbd=(KAbd, bKAbd), Kk=(Kk3, bKk), Ma=(Ma3, bMa), Mk=(Mk3, bMk))
            yield

        def genB(ci, slot=0):
            c0, C, sidx = chunks[ci]
            b0, b1, b2 = (5, 6, 7) if slot == 0 else (0, 1, 2)
            tA, tK, tV, tN = (0, 1, 2, 4) if slot == 0 else (5, 6, 10, 11)
            so = 0 if slot == 0 else 3
            par = ci % 2
            cx = ctx[par]
            rr = C > 1
            samp = sidx < 16
            Rbd, bRbd = cx["Rbd"]
            KAbd, bKAbd = cx["KAbd"]
            Kk3, bKk = cx["Kk"]
            Ma3, bMa = cx["Ma"]
            Mk3, bMk = cx["Mk"]
            if samp:
                Bt, bB = self.Bs[sidx % 2], self.bBs[sidx % 2]
                self.dma(Bt[:, :, :].rearrange("p j v -> p (j v)"), d["srw"][sidx], [], [bB])
            else:
                Bt, bB = self.Bp, self.bBp
            At, bAt = self.sr(tA)
            Kt, bKt = self.sr(tK)
            Vt, bVt = self.sr(tV)
            if rr:
                Br_t, bBr = self.sr(3)
                Br = Br_t[:, 0:256].rearrange("p (j v) -> p j v", j=4)
                self.cp(R(Br), Bt[:, :, :], [bB], [bBr], eng="dve")
            else:
                Br, bBr = Bt, bB
            trs = ((4, At, bAt, b0, bAH), (1, Kt, bKt, b1, bK), (2, Vt, bVt, b2, bV))
            for (t, dst, db, bank, bsrc) in trs:
                for j in range(4):
                    self.tr(self.ps[bank][0:C, j * 128:(j + 1) * 128], self.PJ[:, t * 4 + j, c0:c0 + C], self.ident[:, :],
                            [bsrc[j], self.bconst], [self.bps[bank]])
            yield
            for (t, dst, db, bank, bsrc) in trs:
                self.cp(R(dst[0:C, 0:512]), self.ps[bank][0:C, 0:512], [self.bps[bank]], [db], eng="act")
            yield
            for hh in range(H):
                j = hh // 2
                hc = slice(hh * 64, (hh + 1) * 64)
                self.mm(self.ps[b0][0:C, hc], KAbd[:, j, hh % 2, :], Br[:, j, :], [bKAbd, bBr], [self.bps[b0]],
                        start=(hh == 0), stop=True, r=rr, g=True)
            if C > 1:
                for hh in range(H):
                    hc = slice(hh * 64, (hh + 1) * 64)
                    self.mm(self.ps[b0][0:C, hc], Kk3[0:C, hh, 0:C], Vt[0:C, hc], [bKk, bVt], [self.bps[b0]],
                            start=False, stop=True, r=rr, g=True)
            yield
            nY, bnY = self.sr(tN)
            if C > 1:
                X3, bX = cx["X"]
                Rs, bRs = self.sr(tN)
                self.cp(R(Rs[0:C, 0:512]), self.ps[b0][0:C, 0:512], [self.bps[b0]], [bRs], eng="act")
                yield
                for hh in range(H):
                    hc = slice(hh * 64, (hh + 1) * 64)
                    self.mm(self.ps[b1][0:C, hc], X3[0:C, hh, 0:C], Rs[0:C, hc], [bX, bRs], [self.bps[b1]], r=rr)
                yield
                self.act(R(nY[0:C, 0:512]), self.ps[b1][0:C, 0:512], AF.Copy, [self.bps[b1]], [bnY], scale=-1.0)
            else:
                self.act(R(nY[0:C, 0:512]), self.ps[b0][0:C, 0:512], AF.Copy, [self.bps[b0]], [bnY], scale=-1.0)
            yield
            def sout(hh):
                j, p0 = hh // 2, (hh % 2) * 64
                return self.ps[b0][p0:p0 + 64, j * 64:(j + 1) * 64]

            for hh in range(H):
                j, p0 = hh // 2, (hh % 2) * 64
                self.mm(sout(hh), self.ident[:, p0:p0 + 64], Bt[:, j, :], [self.bconst, bB], [self.bps[b0]],
                        start=(hh < 2), stop=True, g=True)
            for hh in range(H):
                hc = slice(hh * 64, (hh + 1) * 64)
                self.mm(sout(hh), At[0:C, hc], nY[0:C, hc], [bAt, bnY], [self.bps[b0]], start=False, stop=True, g=True)
            for hh in range(H):
                hc = slice(hh * 64, (hh + 1) * 64)
                self.mm(sout(hh), Kt[0:C, hc], Vt[0:C, hc], [bKt, bVt], [self.bps[b0]], start=False, stop=True, g=True)
            for hh in range(H):
                j = hh // 2
                hc = slice(hh * 64, (hh + 1) * 64)
                self.mm(self.ps[b2][0:C, hc], Rbd[:, j, hh % 2, :], Br[:, j, :], [bRbd, bBr], [self.bps[b2]],
                        start=(hh == 0), stop=True, r=rr, g=True)
            for hh in range(H):
                hc = slice(hh * 64, (hh + 1) * 64)
                self.mm(self.ps[b2][0:C, hc], Ma3[0:C, hh, 0:C], nY[0:C, hc], [bMa, bnY], [self.bps[b2]], start=False, stop=True, r=rr, g=True)
            for hh in range(H):
                hc = slice(hh * 64, (hh + 1) * 64)
                self.mm(self.ps[b2][0:C, hc], Mk3[0:C, hh, 0:C], Vt[0:C, hc], [bMk, bVt], [self.bps[b2]], start=False, stop=True, r=rr, g=True)
            yield
            pcs = self.PCs[:, :, ci:ci + 1].broadcast_to([128, 4, 64])
            self.tt(Bt[:, :, :], self.ps[b0][:, 0:256].rearrange("p (j v) -> p j v", j=4), pcs, ALU.mult,
                    [self.bps[b0], self.bPCs], [bB])
            if samp:
                t = self.dma(d["orw"][sidx], Bt[:, :, :].rearrange("p j v -> p (j v)"), [bB], [])
                self.out_tokens.append(t)
            RKc, bRK = self.FM[:, slot, :], self.bfm[slot]
            for j in range(4):
                self.stt(RKc[:, j * 64:j * 64 + C], self.PJ[:, j, c0:c0 + C], self.col("rk%d" % j), self.PJ[:, 4 + j, c0:c0 + C],
                         ALU.mult, ALU.mult, [bR[j], bK[j], self.bcols], [bRK])
            yield
            for j in range(4):
                self.mm(self.ps[b1][0:C, 2 * j:2 * j + 2], RKc[:, j * 64:j * 64 + C], self.blk2[:, :], [bRK, self.bconst], [self.bps[b1]])
            yield
            rks, brks = self.sm8[:, so + 0, :], self.bsm8[so + 0]
            self.cp(rks[0:C, :], self.ps[b1][0:C, 0:8], [self.bps[b1]], [brks], eng="act")
            O3 = self.ps[b2][0:C, 0:512].rearrange("p (h v) -> p h v", h=8)
            s1, bs1 = self.sm8[:, so + 1, :], self.bsm8[so + 1]
            self.red(s1[0:C, :], O3, [self.bps[b2]], [bs1])
            self.ts(s1[0:C, :], s1[0:C, :], -1.0 / 64.0, ALU.mult, [bs1], [bs1])
            cen, bcen = self.sc(so + 0)
            cen3 = cen[0:C, 0:512].rearrange("p (h v) -> p h v", h=8)
            self.tt(cen3, O3, s1[0:C, :].unsqueeze(2).broadcast_to([C, 8, 64]), ALU.add, [self.bps[b2], bs1], [bcen])
            yield
            self.mm(self.ps[b1][0:C, 0:512], SG[0:96, c0:c0 + C], self.wgu[0:96, :], [bsg, self.bwsm], [self.bps[b1]])
            sq, bsq = self.sc(so + 1)
            self.act(sq[0:C, 0:512], cen[0:C, 0:512], AF.Square, [bcen], [bsq])
            yield
            s2, bs2 = self.sm8[:, so + 2, :], self.bsm8[so + 2]
            self.red(s2[0:C, :], sq[0:C, 0:512].rearrange("p (h v) -> p h v", h=8), [bsq], [bs2])
            self.act(s2[0:C, :], s2[0:C, :], AF.Sqrt, [bs2, self.bc3], [bs2], bias=self.epsn[0:C, 2:3], scale=1.0 / 64.0)
            self.recip(s2[0:C, :], s2[0:C, :], [bs2], [bs2])
            yield
            self.tt(cen3, cen3, s2[0:C, :].unsqueeze(2).broadcast_to([C, 8, 64]), ALU.mult, [bcen, bs2], [bcen])
            self.tt(cen[0:C, 0:512], cen[0:C, 0:512], lnw[0:C, :], ALU.mult, [bcen, self.bbc], [bcen], eng="pool")
            self.tt(cen[0:C, 0:512], cen[0:C, 0:512], lnb[0:C, :], ALU.add, [bcen, self.bbc], [bcen], eng="pool")
            bon, bbon = self.sc(so + 2)
            self.tt(bon[0:C, 0:512].rearrange("p (h v) -> p h v", h=8), Vt[0:C, 0:512].rearrange("p (h v) -> p h v", h=8),
                    rks[0:C, :].unsqueeze(2).broadcast_to([C, 8, 64]), ALU.mult, [bVt, brks], [bbon])
            yield
            self.tt(cen[0:C, 0:512], cen[0:C, 0:512], bon[0:C, 0:512], ALU.add, [bcen, bbon], [bcen], eng="pool")
            self.tt(sq[0:C, 0:512], cen[0:C, 0:512], self.ps[b1][0:C, 0:512], ALU.mult, [bcen, self.bps[b1]], [bsq])
            yield
            for j in range(4):
                self.tr(self.ps[b0][:, 256 + j * 64:256 + j * 64 + C], sq[0:C, j * 128:(j + 1) * 128], self.ident[0:C, 0:C],
                        [bsq, self.bconst], [self.bps[b0]])
            yield
            for j in range(4):
                self.cp(self.mixT[:, j, c0:c0 + C], self.ps[b0][:, 256 + j * 64:256 + j * 64 + C], [self.bps[b0]], [self.bmx[j]], eng="act")
            yield

        self.run_chunks(genA, genB, chunks)

    def gdn_prep(self):
        T = self.cur["T"]
        subs = self.cur["subs"]
        AR, bar = self.SM[:, 0, :], self.bsm[0]
        BR, bbr = self.SM[:, 1, :], self.bsm[1]
        G, bG = self.SM[:, 2, :], self.bsm[2]
        for i in range(8):
            X, bx = self.PJ[:, i, :], self.bpj[i]
            sq, bsq = self.sc(0 + (i % 2))
            nr, bnr = self.sc(2 + (i % 2))
            self.act(sq[:, 0:T], X[:, 0:T], AF.Square, [bx], [bsq])
            for si, (s0, n) in enumerate(subs):
                b = 4 + (2 * i + si) % 4
                self.mm(self.ps[b][:, 0:n], self.ones[:, :], sq[:, s0:s0 + n], [self.bconst, bsq], [self.bps[b]])
                self.act(nr[:, s0:s0 + n], self.ps[b][:, 0:n], AF.Ln, [self.bps[b], self.bc3], [bnr], bias=self.epsn[:, 1:2])
            self.act(sq[:, 0:T], nr[:, 0:T], AF.Exp, [bnr], [bsq], scale=-0.5)
            if i < 4:
                self.stt(X[:, 0:T], X[:, 0:T], 128.0 ** -0.5, sq[:, 0:T], ALU.mult, ALU.mult, [bx, bsq], [bx])
            else:
                self.tt(X[:, 0:T], X[:, 0:T], sq[:, 0:T], ALU.mult, [bx, bsq], [bx])
        self.act(BR[0:4, 0:T], BR[0:4, 0:T], AF.Sigmoid, [bbr], [bbr])
        self.act(AR[0:4, 0:T], AR[0:4, 0:T], AF.Exp, [bar, self.bcols], [bar], bias=self.col("dtb", 4))
        self.act(AR[0:4, 0:T], AR[0:4, 0:T], AF.Ln, [bar, self.bc3], [bar], bias=self.epsn[0:4, 3:4])
        self.ts(AR[0:4, 0:T], AR[0:4, 0:T], self.c2[0:4, 5:6], ALU.mult, [bar, self.bc2], [bar])
        self.scan(G[0:4, 0:T], self.MR[0:4, 0:T], AR[0:4, 0:T], [self.bMR, bar], [bG])
        for (c0, C, sidx) in self.gdn_chunk_list():
            if C == 128:
                self.ts(G[0:4, c0 + 64:c0 + 128], G[0:4, c0 + 64:c0 + 128], G[0:4, c0 + 63:c0 + 64], ALU.add, [bG], [bG])

    def gdn_chunk_list(self):
        ch = self.cur["chunks"]
        small = [c for c in ch if c[1] < 64]
        big = [c for c in ch if c[1] == 64]
        merged = [(big[k][0], 128, 16) for k in range(0, len(big), 2)]
        return small + merged

    def interleave(self, gens):
        alive = [g for g in gens if g is not None]
        while alive:
            for g in list(alive):
                try:
                    next(g)
                except StopIteration:
                    alive.remove(g)

    def run_chunks(self, genA, genB, chunks):
        idx = list(range(len(chunks)))
        if self.cf != "all":
            idx = [i for i in idx if {1: "samp", 16: "meta", 64: "big", 128: "big"}[chunks[i][1]] in self.cf]
        samp = [i for i in idx if chunks[i][2] < 16]
        rest = [i for i in idx if chunks[i][2] >= 16]
        for k in range(0, len(samp), 2):
            pair = samp[k:k + 2]
            for i in pair:
                self.interleave([genA(i)])
            self.interleave([genB(i, slot) for slot, i in enumerate(pair)])
        for k in range(len(rest) + 1):
            ga = genA(rest[k]) if k < len(rest) else None
            gb = genB(rest[k - 1], 0) if k >= 1 else None
            self.interleave([gb, ga])

    def pipeline(self, genA, genB, n):
        for i in range(n + 1):
            ga = genA(i) if i < n else None
            gb = genB(i - 1) if i >= 1 else None
            self.interleave([gb, ga])

    def inverse_gen(self, U, bU, L, bL, C, H, scr, banks, out):
        (Ub, bUb), (Lb, bLb), (IL, bIL), (Xa, bXa), (Xb, bXb) = scr
        pL, pU, pX = banks
        idb = self.ident[0:C, 0:C].unsqueeze(1).broadcast_to([C, H, C])
        self.stt(R(Xa[0:C, :, 0:C]), U[0:C, :, 0:C], -1.0, idb, ALU.mult, ALU.add, [self.bconst, bU], [bXa])
        nlev = {128: 6, 64: 5, 32: 4, 16: 3, 8: 2, 4: 1, 2: 0}[C]
        W = max(C, 64)
        Uc, bUc, Lc, bLc = U, bU, L, bL
        Un, bUn, Ln, bLn = Ub, bUb, Lb, bLb
        X, bX, Xn, bXn = Xa, bXa, Xb, bXb
        for lev in range(nlev):
            last = lev == nlev - 1
            PL = self.ps[pL][0:C, 0:H * W].rearrange("p (h c) -> p h c", h=H)
            PU = self.ps[pU][0:C, 0:H * W].rearrange("p (h c) -> p h c", h=H)
            PX = self.ps[pX][0:C, 0:H * W].rearrange("p (h c) -> p h c", h=H)
            for hh in range(H):
                self.mm(PL[:, hh, 0:C], Uc[0:C, hh, 0:C], Lc[0:C, hh, 0:C], [bUc, bLc], [self.bps[pL]], r=True)
            if not last:
                for hh in range(H):
                    self.mm(PU[:, hh, 0:C], Lc[0:C, hh, 0:C], Uc[0:C, hh, 0:C], [bUc, bLc], [self.bps[pU]], r=True)
            yield
            self.tt(R(IL[0:C, :, 0:C]), PL[:, :, 0:C], idb, ALU.add, [self.bps[pL], self.bconst], [bIL])
            if not last:
                self.cp(R(Ln[0:C, :, 0:C]), PL[:, :, 0:C], [self.bps[pL]], [bLn], eng="act")
                self.cp(R(Un[0:C, :, 0:C]), PU[:, :, 0:C], [self.bps[pU]], [bUn], eng="act")
            yield
            for hh in range(H):
                self.mm(PX[:, hh, 0:C], IL[0:C, hh, 0:C], X[0:C, hh, 0:C], [bIL, bX], [self.bps[pX]], r=True)
            yield
            self.cp(R(Xn[0:C, :, 0:C]), PX[:, :, 0:C], [self.bps[pX]], [bXn], eng="dve")
            yield
            X, bX, Xn, bXn = Xn, bXn, X, bX
            if not last:
                Uc, bUc, Un, bUn = Un, bUn, Uc, bUc
                Lc, bLc, Ln, bLn = Ln, bLn, Lc, bLc
        out["X"] = (X, bX)

    def gdn_chunks(self):
        d = self.d
        H = 4
        chunks = self.gdn_chunk_list()
        bQ = [self.bpj[h] for h in range(4)]
        bK = [self.bpj[4 + h] for h in range(4)]
        bV = [self.bpj[8 + h] for h in range(4)]
        bZ = [self.bpj[12 + h] for h in range(4)]
        BR, bbr = self.SM[:, 1, :], self.bsm[1]
        G, bG = self.SM[:, 2, :], self.bsm[2]
        nw = self.bcst[:, 1024:1536]
        ctx = [dict(), dict()]
        FMf = self.FM[:, :, :].rearrange("p a c -> p (a c)")

        def genA(ci):
            c0, C, sidx = chunks[ci]
            w = 128 if C == 128 else 64
            par = ci % 2
            cx = ctx[par]
            cx.clear()
            rr = C > 1

            def v4(ap):
                return ap[:, 0:4 * w].rearrange("p (h c) -> p h c", h=4)

            def P3(b_):
                return v4(self.ps[b_])[0:C]

            PG = v4(self.ps[3])
            PB = v4(self.ps[0])
            for hh in range(H):
                self.mm(PG[:, hh, 0:C], self.sel[0:4, hh, :], G[0:4, c0:c0 + C], [self.bconst, bG], [self.bps[3]])
            for hh in range(H):
                self.mm(PB[:, hh, 0:C], self.sel[0:4, hh, :], BR[0:4, c0:c0 + C], [self.bconst, bbr], [self.bps[0]])
            self.mm(self.ps[1][0:C, 0:4], G[0:4, c0:c0 + C], self.ident[0:4, 0:4], [bG, self.bconst], [self.bps[1]])
            self.mm(self.ps[1][0:C, 4:8], BR[0:4, c0:c0 + C], self.ident[0:4, 0:4], [bbr, self.bconst], [self.bps[1]])
            yield
            Gbc, bGbc = v4(FMf[:, 0:512]), self.bfm[0]
            gam, bgam = v4(FMf[:, 512:1024]), self.bfm[1]
            gamC, bgamC = self.sm8[:, par, :], self.bsm8[par]
            self.cp(Gbc[:, :, 0:C], PG[:, :, 0:C], [self.bps[3]], [bGbc], eng="act")
            self.act(gam[:, :, 0:C], PG[:, :, 0:C], AF.Exp, [self.bps[3]], [bgam])
            self.act(gamC[:, 0:4], PG[:, :, C - 1], AF.Exp, [self.bps[3]], [bgamC])
            cl, bcl = self.sm8[:, 3, :], self.bsm8[3]
            self.cp(cl[0:C, :], self.ps[1][0:C, 0:8], [self.bps[1]], [bcl], eng="dve")
            dc, bdc = self.sm8[:, 4, :], self.bsm8[4]
            self.tt(dc[0:C, 0:4], Gbc[0:C, :, C - 1], cl[0:C, 0:4], ALU.subtract, [bGbc, bcl], [bdc])
            self.act(dc[0:C, 0:4], dc[0:C, 0:4], AF.Exp, [bdc], [bdc])
            yield
            Kv = self.PJ[:, 4:8, c0:c0 + C]
            Qv = self.PJ[:, 0:4, c0:c0 + C]
            kb, bkb = self.sr3(3, 4, w)
            Kc, bKc = self.sr3(4, 4, w)
            Qc, bQc = self.sr3(15, 4, w)
            kbg, bkbg = self.sr3(16 if par == 0 else 17, 4, w)
            qg, bqg = self.sr3(21 if par == 0 else 22, 4, w)
            self.tt(R(kb[:, :, 0:C]), Kv, PB[:, :, 0:C], ALU.mult, bK + [self.bps[0]], [bkb])
            self.tt(R(kbg[:, :, 0:C]), kb[:, :, 0:C], gam[:, :, 0:C], ALU.mult, [bkb, bgam], [bkbg])
            self.tt(R(qg[:, :, 0:C]), Qv, gam[:, :, 0:C], ALU.mult, bQ + [bgam], [bqg])
            self.cp(R(Kc[:, :, 0:C]), Kv, bK, [bKc], eng="dve")
            self.cp(R(Qc[:, :, 0:C]), Qv, bQ, [bQc], eng="dve")
            QK3, bQK = self.sr3(7 if par == 0 else 18, 4, w)
            if C > 1:
                D1, bD1 = self.sc3(5, 4, w)
                D2, bD2 = self.sc3(6, 4, w)
                D3, bD3 = self.sc3(7, 4, w)
                gcol_b = cl[0:C, 0:4].unsqueeze(2).broadcast_to([C, 4, C])
                self.tt(D1[0:C, :, 0:C], Gbc[0:C, :, 0:C], gcol_b, ALU.subtract, [bGbc, bcl], [bD1])
                self.stt(D3[0:C, :, 0:C], Gbc[0:C, :, 0:C], -1.0, gcol_b, ALU.mult, ALU.add, [bGbc, bcl], [bD3])

                def nmask(m):
                    return m[0:C, 0:C].unsqueeze(1).broadcast_to([C, 4, C])

                self.tt(D2[0:C, :, 0:C], D1[0:C, :, 0:C], nmask(self.niu), ALU.add, [bD1, self.bconst], [bD2], eng="pool")
                self.tt(D1[0:C, :, 0:C], D1[0:C, :, 0:C], nmask(self.nsu), ALU.add, [bD1, self.bconst], [bD1], eng="pool")
                self.tt(D3[0:C, :, 0:C], D3[0:C, :, 0:C], nmask(self.nsl), ALU.add, [bD3, self.bconst], [bD3], eng="pool")
                self.act(D1[0:C, :, 0:C], D1[0:C, :, 0:C], AF.Exp, [bD1], [bD1])
                self.act(D2[0:C, :, 0:C], D2[0:C, :, 0:C], AF.Exp, [bD2], [bD2])
                self.act(D3[0:C, :, 0:C], D3[0:C, :, 0:C], AF.Exp, [bD3], [bD3])
            yield
            U3, bU = self.sr3(5, 4, w)
            L3, bL = self.sr3(6, 4, w)
            for hh in range(H):
                Kh = Kc[:, hh, 0:C]
                Qh = Qc[:, hh, 0:C]
                if C > 1:
                    self.mm(P3(0)[:, hh, 0:C], Kh, kb[:, hh, 0:C], [bKc, bkb], [self.bps[0]], r=rr)
                    self.mm(P3(1)[:, hh, 0:C], kb[:, hh, 0:C], Kh, [bKc, bkb], [self.bps[1]], r=rr)
                self.mm(P3(2)[:, hh, 0:C], Kh, Qh, [bKc, bQc], [self.bps[2]], r=rr)
            yield
            if C > 1:
                self.tt(R(U3[0:C, :, 0:C]), P3(0)[:, :, 0:C], D1[0:C, :, 0:C], ALU.mult, [self.bps[0], bD1], [bU])
                self.tt(R(L3[0:C, :, 0:C]), P3(1)[:, :, 0:C], D3[0:C, :, 0:C], ALU.mult, [self.bps[1], bD3], [bL])
                self.tt(R(QK3[0:C, :, 0:C]), P3(2)[:, :, 0:C], D2[0:C, :, 0:C], ALU.mult, [self.bps[2], bD2], [bQK])
                yield
                xfin = self.sr3(14 if par == 0 else 9, 4, w)
                xtmp = self.sr3(13, 4, w)
                nlev = {128: 6, 64: 5, 32: 4, 16: 3, 8: 2, 4: 1, 2: 0}[C]
                xa, xb = (xtmp, xfin) if nlev % 2 == 1 else (xfin, xtmp)
                scr = [self.sr3(10, 4, w), self.sr3(11, 4, w), self.sr3(12, 4, w), xa, xb]
                res = {}
                yield from self.inverse_gen(U3, bU, L3, bL, C, 4, scr, (0, 1, 2), res)
                cx["X"] = res["X"]
            else:
                self.cp(R(QK3[0:C, :, 0:C]), P3(2)[:, :, 0:C], [self.bps[2]], [bQK], eng="dve")
                yield
            bV_, bbV = self.sc(1 if par == 0 else 0)
            Kd, bKd = self.sr(0 if par == 0 else 19)
            Zt, bZt = self.sc(2 if par == 0 else 3)
            for hh in range(H):
                self.tr(self.ps[3][0:C, hh * 128:(hh + 1) * 128], self.PJ[:, 8 + hh, c0:c0 + C], self.ident[:, :], [bV[hh], self.bconst], [self.bps[3]])
            for hh in range(H):
                self.tr(self.ps[0][0:C, hh * 128:(hh + 1) * 128], self.PJ[:, 4 + hh, c0:c0 + C], self.ident[:, :], [bK[hh], self.bconst], [self.bps[0]])
            for hh in range(H):
                self.tr(self.ps[1][0:C, hh * 128:(hh + 1) * 128], self.PJ[:, 12 + hh, c0:c0 + C], self.ident[:, :], [bZ[hh], self.bconst], [self.bps[1]])
            yield

            def T3(ap):
                return ap[0:C, 0:512].rearrange("p (h v) -> p h v", h=4)

            self.tt(T3(bV_), T3(self.ps[3]), cl[0:C, 4:8].unsqueeze(2).broadcast_to([C, 4, 128]), ALU.mult, [self.bps[3], bcl], [bbV])
            self.tt(R(T3(Kd)), T3(self.ps[0]), dc[0:C, 0:4].unsqueeze(2).broadcast_to([C, 4, 128]), ALU.mult, [self.bps[0], bdc], [bKd])
            self.cp(Zt[0:C, 0:512], self.ps[1][0:C, 0:512], [self.bps[1]], [bZt], eng="act")
            cx.update(kbg=(kbg, bkbg), qg=(qg, bqg), QK=(QK3, bQK), bV=(bV_, bbV), Kd=(Kd, bKd), Zt=(Zt, bZt), gam=(gamC, bgamC))
            yield

        def genB(ci, slot=0):
            c0, C, sidx = chunks[ci]
            b4, b5, b6, b7 = (4, 5, 6, 7) if slot == 0 else (0, 1, 2, 3)
            par = ci % 2
            cx = ctx[par]
            rr = C > 1
            samp = sidx < 16
            kbg, bkbg = cx["kbg"]
            qg, bqg = cx["qg"]
            QK3, bQK = cx["QK"]
            bV_, bbV = cx["bV"]
            Kd, bKd = cx["Kd"]
            Zt, bZt = cx["Zt"]
            gam, bgam = cx["gam"]

            def T3(ap):
                return ap[0:C, 0:512].rearrange("p (h v) -> p h v", h=4)

            if samp:
                St, bS = self.Ss[sidx % 2], self.bSs[sidx % 2]
                self.dma(St[:, :, :].rearrange("p h v -> p (h v)"), d["sgd"][sidx], [], [bS])
            else:
                St, bS = self.Sp, self.bSp
            if rr:
                Sr_t, bSr = self.sr(8)
                Sr = Sr_t[:, 0:512].rearrange("p (h v) -> p h v", h=4)
                self.cp(R(Sr), St[:, :, :], [bS], [bSr], eng="dve")
            else:
                Sr, bSr = St, bS
            for hh in range(H):
                self.mm(self.ps[b4][0:C, hh * 128:(hh + 1) * 128], kbg[:, hh, 0:C], Sr[:, hh, :], [bkbg, bSr], [self.bps[b4]], r=rr)
            yield
            Rs, bRs = self.sr(1 if slot == 0 else 20)
            self.tt(R(Rs[0:C, 0:512]), bV_[0:C, 0:512], self.ps[b4][0:C, 0:512], ALU.subtract, [bbV, self.bps[b4]], [bRs])
            yield
            if C > 1:
                X3, bX = cx["X"]
                VN, bVN = self.sr(2)
                for hh in range(H):
                    hc = slice(hh * 128, (hh + 1) * 128)
                    self.mm(self.ps[b5][0:C, hc], X3[0:C, hh, 0:C], Rs[0:C, hc], [bX, bRs], [self.bps[b5]], r=rr)
                yield
                self.cp(R(VN[0:C, 0:512]), self.ps[b5][0:C, 0:512], [self.bps[b5]], [bVN], eng="act")
                yield
            else:
                VN, bVN = Rs, bRs
            for hh in range(H):
                hc = slice(hh * 128, (hh + 1) * 128)
                self.mm(self.ps[b7][:, hc], Kd[0:C, hc], VN[0:C, hc], [bKd, bVN], [self.bps[b7]], r=rr)
            for hh in range(H):
                hc = slice(hh * 128, (hh + 1) * 128)
                self.mm(self.ps[b6][0:C, hc], qg[:, hh, 0:C], Sr[:, hh, :], [bqg, bSr], [self.bps[b6]], start=(hh == 0), stop=True, r=rr, g=True)
            for hh in range(H):
                hc = slice(hh * 128, (hh + 1) * 128)
                self.mm(self.ps[b6][0:C, hc], QK3[0:C, hh, 0:C], VN[0:C, hc], [bQK, bVN], [self.bps[b6]], start=False, stop=True, r=rr, g=True)
            yield
            for hh in range(H):
                hc = slice(hh * 128, (hh + 1) * 128)
                self.stt(St[:, hh, :], St[:, hh, :], gam[:, hh:hh + 1], self.ps[b7][:, hc], ALU.mult, ALU.add,
                         [bS, bgam, self.bps[b7]], [bS])
            if samp:
                t = self.dma(d["ogd"][sidx], St[:, :, :].rearrange("p h v -> p (h v)"), [bS], [])
                self.out_tokens.append(t)
            yield
            sq, bsq = self.sc(4 if slot == 0 else 5)
            self.act(sq[0:C, 0:512], self.ps[b6][0:C, 0:512], AF.Square, [self.bps[b6]], [bsq])
            s2, bs2 = self.sm8[:, 5 + slot, :], self.bsm8[5 + slot]
            self.red(s2[0:C, 0:4], T3(sq), [bsq], [bs2])
            self.act(s2[0:C, 0:4], s2[0:C, 0:4], AF.Ln, [bs2, self.bc3], [bs2], bias=self.epsn[0:C, 0:1], scale=1.0 / 128.0)
            self.act(s2[0:C, 0:4], s2[0:C, 0:4], AF.Exp, [bs2], [bs2], scale=-0.5)
            yield
            o2, bo2 = sq, bsq
            self.tt(T3(o2), T3(self.ps[b6]), s2[0:C, 0:4].unsqueeze(2).broadcast_to([C, 4, 128]), ALU.mult, [self.bps[b6], bs2], [bo2])
            self.tt(o2[0:C, 0:512], o2[0:C, 0:512], nw[0:C, :], ALU.mult, [bo2, self.bbc], [bo2], eng="pool")
            self.tt(o2[0:C, 0:512], o2[0:C, 0:512], Zt[0:C, 0:512], ALU.mult, [bo2, bZt], [bo2], eng="pool")
            yield
            wo = 128 if C == 128 else 64
            for hh in range(H):
                self.tr(self.ps[b4][:, hh * wo:hh * wo + C], o2[0:C, hh * 128:(hh + 1) * 128], self.ident[0:C, 0:C],
                        [bo2, self.bconst], [self.bps[b4]])
            yield
            for hh in range(H):
                self.cp(self.mixT[:, 4 + hh, c0:c0 + C], self.ps[b4][:, hh * wo:hh * wo + C], [self.bps[b4]], [self.bmx[4 + hh]], eng="act")
            yield

        self.run_chunks(genA, genB, chunks)

    def finish_outputs(self):
        d = self.d
        self.cp(self.OSH[:, :, 16], self.CR[:, 0:15, 2], [self.bCR], [self.bOSH])
        self.cp(self.OCV[:, :, :, 16], self.CR[:, 15:27, :], [self.bCR], [self.bOCV])
        self.out_tokens.append(self.dma(d["osh"], self.OSH[:, :, :].rearrange("p a s -> p (a s)"), [self.bOSH], []))
        self.out_tokens.append(self.dma(d["ocv"], self.OCV[:, :, :, :].rearrange("p a t s -> p (a t s)"), [self.bOCV], []))
        self.out_tokens.append(self.dma(d["orw"][16], self.Bp[:, :, :].rearrange("p j v -> p (j v)"), [self.bBp], []))
        self.out_tokens.append(self.dma(d["ogd"][16], self.Sp[:, :, :].rearrange("p h v -> p (h v)"), [self.bSp], []))


def _prep_shared(inp):
    f = np.float32
    sh = {}

    def gate(w):
        return np.ascontiguousarray(w.reshape(8, 128, 11, 256).transpose(2, 1, 0, 3)).reshape(11, 128, 2048)

    def down(w, ng):
        return np.ascontiguousarray(w.reshape(ng, 2, 128, 2, 512).transpose(3, 0, 2, 1, 4)).reshape(2 * ng, 128, 1024)

    def gateup(wg, wu):
        g = wg.reshape(8, 128, 22, 128).transpose(2, 1, 0, 3)
        u = wu.reshape(8, 128, 22, 128).transpose(2, 1, 0, 3)
        return np.ascontiguousarray(np.concatenate([g, u], axis=3)).reshape(22, 128, 2048)

    sh["wgu1"] = gateup(inp["w_gate1"][0], inp["w_up1"][0])
    sh["wgu2"] = gateup(inp["w_gate2"][0], inp["w_up2"][0])
    sh["wd1"] = down(inp["w_down1"][0], 11)
    sh["wd2"] = down(inp["w_down2"][0], 11)
    sh["wout"] = down(inp["w_out"][0], 4)
    W = inp["w_in"][0]
    win = np.zeros((17, 128, 8, 256), f)
    for oc in range(NOC):
        if OC[oc] is None:
            continue
        s, M = OC[oc]
        blk = W[:, s:s + M].reshape(8, 128, M).transpose(1, 0, 2)
        win[oc // 2, :, :, (oc % 2) * 128:(oc % 2) * 128 + M] = blk
    sh["win"] = win.reshape(17, 128, 2048)
    cols = np.zeros((128, NCOLS), f)

    def put(name, vec):
        cols[:len(vec), COLS[name]] = vec

    for nm, key in (("gf1", "g_ffn1"), ("gmx", "g_mix"), ("gf2", "g_ffn2")):
        for k in range(8):
            put("%s%d" % (nm, k), inp[key][0][k * 128:(k + 1) * 128])
    for k in range(8):
        put("gfn%d" % k, inp["g_final"][k * 128:(k + 1) * 128])
    for oc in range(15):
        s, M = OC[oc]
        put("mu%d" % oc, inp["mu_shift"][0][s:s + M])
    for nm, key in (("w0", "w0"), ("a0", "a0"), ("kk", "k_k"), ("ka", "k_a")):
        for j in range(4):
            put("%s%d" % (nm, j), inp[key][0][j * 128:(j + 1) * 128])
    rk = inp["r_k"][0].reshape(512)
    for j in range(4):
        put("rk%d" % j, rk[j * 128:(j + 1) * 128])
    for i in range(12):
        for t in range(4):
            put("cw%d_%d" % (i, t), inp["conv_w"][0][t, i * 128:(i + 1) * 128])
    put("dtb", inp["dt_bias"][0])
    put("alog", inp["a_log"][0])
    sh["cols"] = cols
    bc = np.concatenate([inp["lnx_w"][0], inp["lnx_b"][0], np.tile(inp["gdn_norm_w"][0], 4)]).astype(f)
    sh["bc"] = np.ascontiguousarray(np.tile(bc[None, :], (128, 1)))
    sh["wdu"] = np.ascontiguousarray(inp["w_decay_up"][0])
    sh["wau"] = np.ascontiguousarray(inp["w_a_up"][0])
    sh["wgu"] = np.ascontiguousarray(inp["w_g_up"][0])
    return sh


def _prep_core(inp, c):
    f = np.float32
    m = {}
    s0 = 16 * c
    rows = np.concatenate([inp["x_sample"][s0:s0 + 16, 0, :], inp["meta_tokens"], inp["x_prompt"][c]], axis=0)
    m["xT"] = np.ascontiguousarray(rows.T).reshape(8, 128, TTOT)
    st = inp["state_rwkv"][0, s0:s0 + 16]
    m["srw"] = np.ascontiguousarray(st.reshape(16, 4, 2, 64, 64).transpose(0, 2, 4, 1, 3)).reshape(16, 128, 256)
    sg = inp["state_gdn"][0, s0:s0 + 16]
    m["sgd"] = np.ascontiguousarray(sg.transpose(0, 2, 1, 3)).reshape(16, 128, 512)
    ss = inp["state_shift"][0, s0:s0 + 16]
    ssh = np.zeros((128, 15, 16), f)
    for oc in range(15):
        s, M = OC[oc]
        ssh[:M, oc, :] = ss[:, s:s + M].T
    m["ssh"] = ssh.reshape(128, 240)
    cv = inp["state_conv"][0, s0:s0 + 16]
    m["scv"] = np.ascontiguousarray(cv.reshape(16, 3, 12, 128).transpose(3, 2, 1, 0)).reshape(128, 576)
    return m


_NC_CACHE = {}
_BUILD_KW = {}


def _get_nc(**kw):
    key = tuple(sorted(kw.items()))
    if key not in _NC_CACHE:
        _NC_CACHE[key] = Builder(**kw).build()
    return _NC_CACHE[key]


def kernel(**inputs):
    inp = {k: np.asarray(v, dtype=np.float32) for k, v in inputs.items()}
    nc = _get_nc(**_BUILD_KW)
    sh = _prep_shared(inp)
    in_maps = []
    for c in range(NCORE):
        m = dict(sh)
        m.update(_prep_core(inp, c))
        in_maps.append(m)
    res = run_bass_kernel_spmd(nc, in_maps, core_ids=list(range(NCORE)))
    f = np.float32
    y_prompt = np.zeros((8, 2048, 1024), f)
    y_sample = np.zeros((128, 1, 1024), f)
    rw_p = np.zeros((1, 8, 8, 64, 64), f)
    sh_p = np.zeros((1, 8, 1696), f)
    gd_p = np.zeros((1, 8, 4, 128, 128), f)
    cv_p = np.zeros((1, 8, 3, 1536), f)
    rw_s = np.zeros((1, 128, 8, 64, 64), f)
    sh_s = np.zeros((1, 128, 1696), f)
    gd_s = np.zeros((1, 128, 4, 128, 128), f)
    cv_s = np.zeros((1, 128, 3, 1536), f)
    for c in range(NCORE):
        r = res.results[c]
        y = np.asarray(r["yT"]).reshape(1024, TTOT).T
        y_sample[16 * c:16 * c + 16, 0, :] = y[0:16]
        y_prompt[c] = y[32:]
        orw = np.asarray(r["orw"]).reshape(17, 2, 64, 4, 64).transpose(0, 3, 1, 4, 2).reshape(17, 8, 64, 64)
        rw_s[0, 16 * c:16 * c + 16] = orw[0:16]
        rw_p[0, c] = orw[16]
        ogd = np.asarray(r["ogd"]).reshape(17, 128, 4, 128).transpose(0, 2, 1, 3)
        gd_s[0, 16 * c:16 * c + 16] = ogd[0:16]
        gd_p[0, c] = ogd[16]
        osh = np.asarray(r["osh"]).reshape(128, 15, 17)
        full = np.zeros((17, 1696), f)
        for oc in range(15):
            s, M = OC[oc]
            full[:, s:s + M] = osh[:M, oc, :].T
        sh_s[0, 16 * c:16 * c + 16] = full[0:16]
        sh_p[0, c] = full[16]
        ocv = np.asarray(r["ocv"]).reshape(128, 12, 3, 17).transpose(3, 2, 1, 0).reshape(17, 3, 1536)
        cv_s[0, 16 * c:16 * c + 16] = ocv[0:16]
        cv_p[0, c] = ocv[16]
    return (y_prompt, y_sample, rw_p, sh_p, gd_p, cv_p, rw_s, sh_s, gd_s, cv_s)
```
